# Optimizing a Trainium2 kernel written in Bass

```python
import jax
import jax.numpy as jnp
from jax import lax
import numpy as np

D_MODEL = 1024
BATCH = 8
SEQ = 8192
DEPTH = 2

GLA_HEADS = 4
GLA_DK = 64
GLA_DV = 128
GLA_GATE_RANK = 16
GLA_GATE_TEMP = 16.0
GLA_CHUNK = 64
NSA_HEADS = 8
NSA_KV_GROUPS = 2
NSA_HPG = NSA_HEADS // NSA_KV_GROUPS
NSA_DH = 64
NSA_CMP_BLOCK = 32
NSA_CMP_STRIDE = 16
NSA_CMP_HIDDEN = 128
NSA_SEL_BLOCK = 64
NSA_TOPN = 16
NSA_WINDOW = 512
NSA_QBLOCK = 128
NSA_BRANCHES = 3
SGU_WIDTH = 2 * D_MODEL
SGU_GROUPS = 8
SGU_CHUNK = 128
FFN_HIDDEN = 4 * D_MODEL
NORM_EPS = 1e-6
NEG_BIG = -1e30
POS_BIG = 1e30

EVEN_PROJ_SIZES = (GLA_HEADS * GLA_DK, GLA_HEADS * GLA_DK, GLA_HEADS * GLA_DV, GLA_GATE_RANK, GLA_HEADS * GLA_DV, NSA_HEADS * NSA_DH, NSA_KV_GROUPS * NSA_DH, NSA_KV_GROUPS * NSA_DH, NSA_KV_GROUPS * NSA_DH, NSA_KV_GROUPS * NSA_DH, NSA_KV_GROUPS * NSA_DH, NSA_KV_GROUPS * NSA_DH, NSA_HEADS * NSA_BRANCHES)
EVEN_PROJ_WIDTH = sum(EVEN_PROJ_SIZES)
EVEN_MIX_WIDTH = GLA_HEADS * GLA_DV + NSA_HEADS * NSA_DH

kernel_name = 'hybrid_gla_nsa_sgu_trunk'


def _rms_norm(x, g):
    xf = x.astype(jnp.float32)
    y = xf * lax.rsqrt(jnp.mean(xf * xf, axis=-1, keepdims=True) + NORM_EPS)
    return (y * g).astype(x.dtype)


def _masked_softmax(s, mask):
    m = mask.astype(jnp.float32)
    s = jnp.where(mask, s, NEG_BIG)
    p = jnp.exp(s - jnp.max(s, axis=-1, keepdims=True)) * m
    return p / jnp.maximum(jnp.sum(p, axis=-1, keepdims=True), 1e-30)


def _squared_relu_mlp(x, w1, w2):
    h = jax.nn.relu(x @ w1)
    return (h * h) @ w2


def _gla(q, k, v, g_lr, r, w_gate, b_gate, norm_g):
    f32 = jnp.float32
    bsz, seq, _ = q.shape
    c = GLA_CHUNK
    nc = seq // c
    glog = jax.nn.log_sigmoid((g_lr @ w_gate + b_gate).astype(f32)) / GLA_GATE_TEMP

    def chunks(t, d):
        return t.astype(f32).reshape(bsz, nc, c, GLA_HEADS, d).transpose(0, 3, 1, 2, 4)

    qc = chunks(q, GLA_DK) * (GLA_DK ** -0.5)
    kc = chunks(k, GLA_DK)
    vc = chunks(v, GLA_DV)
    b = jnp.cumsum(chunks(glog, GLA_DK), axis=3)
    b_last = b[:, :, :, -1:, :]
    q_t = qc * jnp.exp(b)
    k_t = kc * jnp.exp(-b)
    k_end = kc * jnp.exp(b_last - b)
    causal = jnp.tril(jnp.ones((c, c), dtype=bool))
    a = jnp.where(causal, jnp.einsum('bhnid,bhnjd->bhnij', q_t, k_t), 0.0)
    o_intra = jnp.einsum('bhnij,bhnjv->bhniv', a, vc)
    ds = jnp.einsum('bhnjd,bhnjv->bhndv', k_end, vc)
    decay = jnp.exp(b_last[:, :, :, 0, :])

    def step(state, inp):
        dec, d = inp
        return dec[..., None] * state + d, state

    s0 = jnp.zeros((bsz, GLA_HEADS, GLA_DK, GLA_DV), f32)
    _, s_prev = lax.scan(step, s0, (jnp.moveaxis(decay, 2, 0), jnp.moveaxis(ds, 2, 0)))
    s_prev = jnp.moveaxis(s_prev, 0, 2)
    o = o_intra + jnp.einsum('bhnid,bhndv->bhniv', q_t, s_prev)
    o = o.transpose(0, 2, 3, 1, 4).reshape(bsz, seq, GLA_HEADS, GLA_DV)
    o = o * lax.rsqrt(jnp.mean(o * o, axis=-1, keepdims=True) + NORM_EPS) * norm_g
    o = o.reshape(bsz, seq, GLA_HEADS * GLA_DV) * jax.nn.silu(r.astype(f32))
    return o.astype(q.dtype)


def _nsa(q, kc, vc, ks, vs, kw, vw, gate_logits, cmp_pos, cmp_w1, cmp_w2):
    f32 = jnp.float32
    bsz, seq, _ = q.shape
    g_n, hpg, dh, qb_len, w = NSA_KV_GROUPS, NSA_HPG, NSA_DH, NSA_QBLOCK, NSA_WINDOW
    n_qb = seq // qb_len
    n_cmp = (seq - NSA_CMP_BLOCK) // NSA_CMP_STRIDE + 1
    n_sel = seq // NSA_SEL_BLOCK
    topn = min(NSA_TOPN, n_sel)
    scale = dh ** -0.5

    def kv_heads(t):
        return t.reshape(bsz, seq, g_n, dh)

    cmp_idx = jnp.arange(n_cmp)[:, None] * NSA_CMP_STRIDE + jnp.arange(NSA_CMP_BLOCK)[None, :]

    def compress(t, i):
        blocks = kv_heads(t)[:, cmp_idx] + cmp_pos[i][None, None, :, None, :]
        blocks = blocks.transpose(0, 3, 1, 2, 4).reshape(bsz, g_n, n_cmp, NSA_CMP_BLOCK * dh)
        return jax.nn.gelu(blocks @ cmp_w1[i]) @ cmp_w2[i]

    k_cmp = compress(kc, 0)
    v_cmp = compress(vc, 1)
    cmp_start = jnp.arange(n_cmp) * NSA_CMP_STRIDE
    cmp_end = cmp_start + NSA_CMP_BLOCK - 1
    sel_start = jnp.arange(n_sel) * NSA_SEL_BLOCK
    overlap = ((cmp_start[:, None] < sel_start[None, :] + NSA_SEL_BLOCK)
               & (cmp_start[:, None] + NSA_CMP_BLOCK > sel_start[None, :])).astype(f32)

    k_sel = kv_heads(ks).reshape(bsz, n_sel, NSA_SEL_BLOCK, g_n, dh).transpose(0, 3, 1, 2, 4)
    v_sel = kv_heads(vs).reshape(bsz, n_sel, NSA_SEL_BLOCK, g_n, dh).transpose(0, 3, 1, 2, 4)
    pad = ((0, 0), (0, 0), (w, 0), (0, 0))
    k_win = jnp.pad(kv_heads(kw).transpose(0, 2, 1, 3), pad)
    v_win = jnp.pad(kv_heads(vw).transpose(0, 2, 1, 3), pad)

    qb = (q * scale).reshape(bsz, n_qb, qb_len, g_n, hpg, dh).transpose(1, 0, 3, 4, 2, 5)
    gb = jax.nn.sigmoid(gate_logits.astype(f32)).reshape(bsz, n_qb, qb_len, g_n, hpg, NSA_BRANCHES)
    gb = gb.transpose(1, 0, 3, 4, 2, 5)
    b_ix = jnp.arange(bsz)[:, None, None, None]
    g_ix = jnp.arange(g_n)[None, :, None, None]
    within = jnp.arange(NSA_SEL_BLOCK)
    win_off = jnp.arange(qb_len + w)
    sel_j = jnp.arange(n_sel)

    def block(args):
        n, qn, gn = args
        t = n * qb_len + jnp.arange(qb_len)
        s_c = jnp.einsum('bghqd,bgcd->bghqc', qn, k_cmp).astype(f32)
        p_c = _masked_softmax(s_c, cmp_end[None, :] <= t[:, None])
        o_c = jnp.einsum('bghqc,bgcd->bghqd', p_c.astype(v_cmp.dtype), v_cmp)
        imp = jnp.einsum('bgqc,cj->bgqj', jnp.sum(p_c, axis=2), overlap)
        cur = t // NSA_SEL_BLOCK
        forced = (sel_j[None, :] == 0) | (sel_j[None, :] == cur[:, None]) | (sel_j[None, :] == cur[:, None] - 1)
        allowed = sel_start[None, :] <= t[:, None]
        score = jnp.where(forced, POS_BIG, jnp.where(allowed, imp, NEG_BIG))
        top_val, top_idx = lax.top_k(score, topn)
        blk_ok = top_val > 0.5 * NEG_BIG
        k_g = k_sel[b_ix, g_ix, top_idx]
        v_g = v_sel[b_ix, g_ix, top_idx]
        kpos = top_idx[..., None] * NSA_SEL_BLOCK + within
        ok = blk_ok[..., None] & (kpos <= t[None, None, :, None, None])
        s_s = jnp.einsum('bghqd,bgqnkd->bghqnk', qn, k_g).astype(f32)
        s_s = s_s.reshape(bsz, g_n, hpg, qb_len, topn * NSA_SEL_BLOCK)
        p_s = _masked_softmax(s_s, ok.reshape(bsz, g_n, 1, qb_len, topn * NSA_SEL_BLOCK))
        p_s = p_s.reshape(bsz, g_n, hpg, qb_len, topn, NSA_SEL_BLOCK).astype(v_g.dtype)
        o_s = jnp.einsum('bghqnk,bgqnkd->bghqd', p_s, v_g)
        k_w = lax.dynamic_slice_in_dim(k_win, n * qb_len, qb_len + w, axis=2)
        v_w = lax.dynamic_slice_in_dim(v_win, n * qb_len, qb_len + w, axis=2)
        kpos_w = n * qb_len - w + win_off
        ok_w = (kpos_w[None, :] >= 0) & (kpos_w[None, :] <= t[:, None]) & (kpos_w[None, :] > t[:, None] - w)
        s_w = jnp.einsum('bghqd,bgkd->bghqk', qn, k_w).astype(f32)
        p_w = _masked_softmax(s_w, ok_w)
        o_w = jnp.einsum('bghqk,bgkd->bghqd', p_w.astype(v_w.dtype), v_w)
        out = gn[..., 0:1] * o_c + gn[..., 1:2] * o_s + gn[..., 2:3] * o_w
        return out.astype(q.dtype)

    o = lax.map(block, (jnp.arange(n_qb), qb, gb))
    return o.transpose(1, 0, 4, 2, 3, 5).reshape(bsz, seq, NSA_HEADS * dh)


def _even_mixer(x, w_in, w_out, gla_w_gate, gla_b_gate, gla_norm, nsa_gate_b, nsa_cmp_pos, nsa_cmp_w1, nsa_cmp_w2):
    offsets = [int(o) for o in np.cumsum(EVEN_PROJ_SIZES)[:-1]]
    gq, gk, gv, glr, gr, nq, kc, vc, ks, vs, kw, vw, ng = jnp.split(x @ w_in, offsets, axis=-1)
    o_a = _gla(gq, gk, gv, glr, gr, gla_w_gate, gla_b_gate, gla_norm)
    o_b = _nsa(nq, kc, vc, ks, vs, kw, vw, ng + nsa_gate_b, nsa_cmp_pos, nsa_cmp_w1, nsa_cmp_w2)
    return jnp.concatenate([o_a, o_b], axis=-1) @ w_out


def _sgu_mixer(x, w_in, ln_g, ln_b, w_s, b_s, w_out):
    bsz, seq, _ = x.shape
    h = jax.nn.gelu(x @ w_in)
    u, v = jnp.split(h, 2, axis=-1)
    vf = v.astype(jnp.float32)
    mu = jnp.mean(vf, axis=-1, keepdims=True)
    var = jnp.mean((vf - mu) ** 2, axis=-1, keepdims=True)
    v = ((vf - mu) * lax.rsqrt(var + NORM_EPS) * ln_g + ln_b).astype(x.dtype)
    n_ch = seq // SGU_CHUNK
    v = v.reshape(bsz, n_ch, SGU_CHUNK, SGU_GROUPS, SGU_WIDTH // SGU_GROUPS)
    w_causal = jnp.where(jnp.tril(jnp.ones((SGU_CHUNK, SGU_CHUNK), dtype=bool)), w_s, 0.0)
    mixed = jnp.einsum('gts,bnsgc->bntgc', w_causal, v) + b_s.T[:, :, None]
    y = u * mixed.reshape(bsz, seq, SGU_WIDTH)
    return y @ w_out


def setup_inputs(seed: int = 0) -> dict:
    key = jax.random.key(seed)
    k = jax.random.split(key, 20)
    ne = (DEPTH + 1) // 2
    no = DEPTH // 2
    f32 = jnp.float32

    def normal(kk, shape):
        return jax.random.normal(kk, shape, f32)

    def dense(kk, shape, fan_in):
        return normal(kk, shape) * (fan_in ** -0.5)

    return {
        'x': normal(k[0], (BATCH, SEQ, D_MODEL)),
        'norm_g': 1.0 + 0.05 * normal(k[1], (DEPTH, 4, D_MODEL)),
        'ffn_w1': dense(k[2], (DEPTH, D_MODEL, FFN_HIDDEN), D_MODEL),
        'ffn_w2': dense(k[3], (DEPTH, FFN_HIDDEN, D_MODEL), FFN_HIDDEN),
        'e_w_in': dense(k[4], (ne, D_MODEL, EVEN_PROJ_WIDTH), D_MODEL),
        'e_w_out': dense(k[5], (ne, EVEN_MIX_WIDTH, D_MODEL), EVEN_MIX_WIDTH),
        'gla_w_gate': dense(k[6], (ne, GLA_GATE_RANK, GLA_HEADS * GLA_DK), GLA_GATE_RANK),
        'gla_b_gate': 0.1 * normal(k[7], (ne, GLA_HEADS * GLA_DK)),
        'gla_norm': 1.0 + 0.05 * normal(k[8], (ne, GLA_HEADS, GLA_DV)),
        'nsa_gate_b': 0.1 * normal(k[9], (ne, NSA_HEADS * NSA_BRANCHES)),
        'nsa_cmp_pos': 0.1 * normal(k[10], (ne, 2, NSA_CMP_BLOCK, NSA_DH)),
        'nsa_cmp_w1': dense(k[11], (ne, 2, NSA_CMP_BLOCK * NSA_DH, NSA_CMP_HIDDEN), NSA_CMP_BLOCK * NSA_DH),
        'nsa_cmp_w2': dense(k[12], (ne, 2, NSA_CMP_HIDDEN, NSA_DH), NSA_CMP_HIDDEN),
        'o_w_in': dense(k[13], (no, D_MODEL, 2 * SGU_WIDTH), D_MODEL),
        'o_ln_g': 1.0 + 0.05 * normal(k[14], (no, SGU_WIDTH)),
        'o_ln_b': 0.02 * normal(k[15], (no, SGU_WIDTH)),
        'o_w_s': dense(k[16], (no, SGU_GROUPS, SGU_CHUNK, SGU_CHUNK), SGU_CHUNK),
        'o_b_s': 1.0 + 0.1 * normal(k[17], (no, SGU_GROUPS, SGU_CHUNK)),
        'o_w_out': dense(k[18], (no, SGU_WIDTH, D_MODEL), SGU_WIDTH),
    }


def reference(x, norm_g, ffn_w1, ffn_w2, e_w_in, e_w_out, gla_w_gate, gla_b_gate, gla_norm, nsa_gate_b, nsa_cmp_pos, nsa_cmp_w1, nsa_cmp_w2, o_w_in, o_ln_g, o_ln_b, o_w_s, o_b_s, o_w_out):
    h = x
    for layer in range(DEPTH):
        i = layer // 2
        xn = _rms_norm(h, norm_g[layer, 0])
        if layer % 2 == 0:
            m = _even_mixer(xn, e_w_in[i], e_w_out[i], gla_w_gate[i], gla_b_gate[i], gla_norm[i],
                            nsa_gate_b[i], nsa_cmp_pos[i], nsa_cmp_w1[i], nsa_cmp_w2[i])
        else:
            m = _sgu_mixer(xn, o_w_in[i], o_ln_g[i], o_ln_b[i], o_w_s[i], o_b_s[i], o_w_out[i])
        h = h + _rms_norm(m, norm_g[layer, 1])
        f = _squared_relu_mlp(_rms_norm(h, norm_g[layer, 2]), ffn_w1[layer], ffn_w2[layer])
        h = h + _rms_norm(f, norm_g[layer, 3])
    return h
```

```python
import math
from contextlib import ExitStack

import numpy as np
import concourse.bass as bass
import concourse.mybir as mybir
from concourse.bass_utils import run_bass_kernel_spmd

F32 = mybir.dt.float32
BF16 = mybir.dt.bfloat16
AF = mybir.ActivationFunctionType
ALU = mybir.AluOpType

PE, ACT, DVE, POOL, SP = "tensor", "scalar", "vector", "gpsimd", "sync"
COMPUTE = (PE, ACT, DVE, POOL)
ALLENG = (PE, ACT, DVE, POOL, SP)

D = 1024
NEG = -30000.0
EPS = 1e-6


class Buf:
    __slots__ = ("name", "last_w", "reads")

    def __init__(self, name):
        self.name = name
        self.last_w = None
        self.reads = {}


class Tl:
    __slots__ = ("t", "b")

    def __init__(self, t, name):
        self.t = t
        self.b = Buf(name)

    def __getitem__(self, k):
        return self.t[k]


class _Rec:
    def __init__(self):
        self.call = None

    def __getattr__(self, name):
        def f(*a, **k):
            self.call = (name, a, k)
        return f


def _record(fn):
    r = _Rec()
    fn(r)
    assert r.call is not None
    return r.call


class Prog:
    def __init__(self, nc, n_dma_sems=8):
        self.nc = nc
        self.ops = {e: [] for e in ALLENG}
        self.count = {e: 0 for e in COMPUTE}
        self.waited = {e: {} for e in ALLENG}
        self.n_dma_sems = n_dma_sems
        self.dma_issued = {}
        self.dma_rr = {e: 0 for e in (SP, POOL, ACT)}
        self.sems = {}
        self.sem_stack = ExitStack()
        self.ninstr = 0

    def _need(self, eng, ev, waits, strict=False):
        if ev is None:
            return
        key, val = ev
        if not strict:
            if key == eng and eng == PE:
                return
            if key == eng and val <= self.count[eng] - 3:
                return
        if self.waited[eng].get(key, 0) >= val:
            return
        self.waited[eng][key] = val
        waits.append((key, val))

    def _deps(self, eng, reads, writes, waits, skip_same_war):
        strict = not skip_same_war
        for b in reads:
            self._need(eng, b.last_w, waits, strict)
        for b in writes:
            self._need(eng, b.last_w, waits, strict)
            for k, v in b.reads.items():
                if skip_same_war and k == eng:
                    continue
                self._need(eng, (k, v), waits, strict)

    def _mark(self, ev, reads, writes):
        for b in reads:
            b.reads[ev[0]] = ev[1]
        for b in writes:
            b.last_w = ev
            b.reads = {}

    def op(self, eng, fn, reads=(), writes=()):
        reads = [r.b if isinstance(r, Tl) else r for r in reads]
        writes = [w.b if isinstance(w, Tl) else w for w in writes]
        waits = []
        self._deps(eng, reads, writes, waits, True)
        self.count[eng] += 1
        ev = (eng, self.count[eng])
        self._mark(ev, reads, writes)
        self.ops[eng].append((waits, _record(fn), (eng, 1)))
        self.ninstr += 1
        return ev

    def dma(self, eng, fn, reads=(), writes=()):
        reads = [r.b if isinstance(r, Tl) else r for r in reads]
        writes = [w.b if isinstance(w, Tl) else w for w in writes]
        slot = self.dma_rr[eng] % (48 if eng == POOL else self.n_dma_sems)
        self.dma_rr[eng] += 1
        key = ("dma", eng, slot)
        issued = self.dma_issued.get(key, 0)
        waits = []
        if issued:
            self._need(eng, (key, 16 * issued), waits, True)
        self._deps(eng, reads, writes, waits, False)
        self.dma_issued[key] = issued + 1
        ev = (key, 16 * (issued + 1))
        self._mark(ev, reads, writes)
        self.ops[eng].append((waits, _record(fn), (key, 16)))
        self.ninstr += 1
        return ev

    def barrier(self):
        evs = [(e, self.count[e]) for e in COMPUTE if self.count[e]]
        evs += [(k, 16 * n) for k, n in self.dma_issued.items()]
        for eng in ALLENG:
            waits = []
            for key, val in evs:
                if key == eng:
                    continue
                if self.waited[eng].get(key, 0) >= val:
                    continue
                self.waited[eng][key] = val
                waits.append((key, val))
            if waits:
                self.ops[eng].append((waits, None, None))

    def emit(self):
        nc = self.nc
        keys = set()
        for e in self.ops:
            for waits, fn, inc in self.ops[e]:
                for k, _ in waits:
                    keys.add(k)
                if inc is not None:
                    keys.add(inc[0])
        for k in sorted(keys, key=str):
            if k in self.sems:
                continue
            nm = "s_" + "_".join(str(x) for x in (k if isinstance(k, tuple) else (k,)))
            self.sems[k] = self.sem_stack.enter_context(nc.semaphore(nm))
        sems = self.sems
        with nc.Block() as block:
            def run(engname):
                def body(eng):
                    for waits, fn, inc in self.ops[engname]:
                        for k, v in waits:
                            eng.wait_ge(sems[k], v)
                        if fn is not None:
                            getattr(eng, fn[0])(*fn[1], **fn[2]).then_inc(sems[inc[0]], inc[1])
                return body

            if self.ops[SP]:
                block.sync(run(SP))
            if self.ops[PE]:
                block.tensor(run(PE))
            if self.ops[ACT]:
                block.scalar(run(ACT))
            if self.ops[DVE]:
                block.vector(run(DVE))
            if self.ops[POOL]:
                block.gpsimd(run(POOL))
        for e in self.ops:
            self.ops[e] = []


class Rot:
    def __init__(self, tiles):
        self.tiles = tiles
        self.i = 0

    def get(self):
        t = self.tiles[self.i % len(self.tiles)]
        self.i += 1
        return t


def make_consts(S):
    NSEL = S // 64
    NCMP = S // 16 - 1
    NCC = (NCMP + 127) // 128
    p = np.arange(128)
    c = {}
    c["c_ident"] = np.eye(128, dtype=np.float32)
    U = (p[:, None] <= p[None, :]).astype(np.float32)
    c["c_U"] = U
    c["c_Ls"] = (p[:, None] > p[None, :]).astype(np.float32)
    c["c_ones"] = np.ones((128, 128), np.float32)
    c["c_U4"] = np.tile(U, (1, 4))
    c["c_CB4"] = np.tile(np.where(p[:, None] <= p[None, :], 0.0, NEG).astype(np.float32), (1, 4))
    c["c_WB4"] = np.tile(np.where(p[:, None] > p[None, :], 0.0, NEG).astype(np.float32), (1, 4))
    u = np.arange(S)
    c["c_G"] = np.where(16 * p[:, None] + 31 <= u[None, :], 0.0, NEG).astype(np.float32)
    jb = np.arange(NSEL)
    r64 = np.arange(64)
    c["c_Zaug"] = ((u[None, :] // 64) % 64 == r64[:, None]).astype(np.float32)
    ov = np.zeros((128, NCC, NSEL), np.float32)
    for cc in range(NCC):
        cidx = cc * 128 + p
        m = (cidx[:, None] >= 4 * jb[None, :] - 1) & (cidx[:, None] <= 4 * jb[None, :] + 3) & (cidx[:, None] < NCMP)
        ov[:, cc, :] = m
    c["c_OV"] = ov
    w = np.arange(2 * NSEL)
    jr = w[None, :] - NSEL
    h = (p[:, None] >= 64).astype(np.int64)
    TA = (jr <= h - 2).astype(np.float32)
    TB = np.where(jr == h, 1000.0, np.where(jr == h - 1, 1001.0, np.where(jr <= h - 2, 0.0, -1000.0 - w[None, :])))
    c["c_TA"] = TA
    c["c_TB"] = TB.astype(np.float32)
    return c


OFF = {}
_o = 0
for _n, _sz in (("gq", 256), ("gk", 256), ("gv", 512), ("glr", 16), ("gr", 512), ("nq", 512), ("kc", 128), ("vc", 128),
                ("ks", 128), ("vs", 128), ("kw", 128), ("vw", 128), ("ng", 24)):
    OFF[_n] = _o
    _o += _sz
WIN = _o


def build(S, debug=False, phases="ABCDEFG"):
    NT = S // 128
    NST = S // 512
    NSEL = S // 64
    NCMP = S // 16 - 1
    NCC = (NCMP + 127) // 128
    nc = bass.Bass("TRN2", target_bir_lowering=False)
    P = Prog(nc)

    def din(name, shape):
        return nc.dram_tensor(name, list(shape), F32, kind="ExternalInput").ap()

    x_d = din("x", [S, D])
    norm_g = din("norm_g", [2, 4, D])
    ffn_w1 = din("ffn_w1", [2, D, 4096])
    ffn_w2 = din("ffn_w2", [2, 4096, D])
    e_w_in = din("e_w_in", [D, WIN])
    e_w_out = din("e_w_out", [D, D])
    gla_w_gate = din("gla_w_gate", [16, 256])
    gla_b_gate = din("gla_b_gate", [256])
    gla_norm = din("gla_norm", [512])
    nsa_gate_b = din("nsa_gate_b", [24])
    nsa_cmp_pos = din("nsa_cmp_pos", [2, 32, 64])
    nsa_cmp_w1 = din("nsa_cmp_w1", [2, 2048, 128])
    nsa_cmp_w2 = din("nsa_cmp_w2", [2, 128, 64])
    o_w_in = din("o_w_in", [D, 4096])
    o_ln_g = din("o_ln_g", [2048])
    o_ln_b = din("o_ln_b", [2048])
    o_w_s = din("o_w_s", [8, 128, 128])
    o_b_s = din("o_b_s", [8, 128])
    o_w_out = din("o_w_out", [2048, D])
    cshapes = {k: v.shape for k, v in make_consts(S).items()}
    cd = {k: din(k, shp) for k, shp in cshapes.items()}

    y_d = nc.dram_tensor("y", [S, D], F32, kind="ExternalOutput").ap()

    def scratch(name, shape, dt):
        kind = "ExternalOutput" if debug else "Internal"
        t = Tl(None, name)
        t.t = nc.dram_tensor(name, list(shape), dt, kind=kind).ap()
        return t

    QT = scratch("QT", [8, 64, S], BF16)
    KT = scratch("KT", [8, 64, S], BF16)
    VT = scratch("VT", [S, 256], BF16)
    GT = scratch("GT", [S, 24], F32)
    OA = scratch("OA", [S, 512], BF16)
    OB = scratch("OB", [S, 512], BF16)
    H0 = scratch("H0", [S, D], F32)
    H1 = scratch("H1", [S, D], F32)
    H2 = scratch("H2", [S, D], F32)
    Bx = Buf("x")
    By = Buf("y")

    top = ExitStack()

    def sbt(st, name, shape, dt):
        return Tl(st.enter_context(nc.sbuf_tensor(name, list(shape), dt)), name)

    def pst(st, name, shape, dt=F32):
        return Tl(st.enter_context(nc.psum_tensor(name, list(shape), dt)), name)

    ident = sbt(top, "ident", [128, 128], BF16)
    P.dma(POOL, lambda e: e.dma_start(out=ident[:], in_=cd["c_ident"]), writes=[ident])

    def load_gain(st, l, j):
        t = sbt(st, f"gain{l}{j}", [128, D], F32)
        P.dma(SP, lambda e: e.dma_start(out=t[:], in_=norm_g[l, j].partition_broadcast(128)), writes=[t])
        return t

    def rms_rstd(st_tile, col, src_ap, src_tl, junk, n, lnexp=False):
        if lnexp:
            P.op(DVE, lambda e: e.memset(st_tile[:], 0.0), writes=[st_tile])
            P.op(ACT, lambda e: e.activation(out=junk[:, 0:n], in_=src_ap, func=AF.Square, accum_out=st_tile[:, col:col + 1]),
                 reads=[src_tl], writes=[junk, st_tile])
            P.op(DVE, lambda e: e.tensor_scalar(out=st_tile[:, col + 1:col + 2], in0=st_tile[:, col:col + 1], scalar1=1.0 / n, scalar2=EPS,
                                                op0=ALU.mult, op1=ALU.add), reads=[st_tile], writes=[st_tile])
            P.op(ACT, lambda e: e.activation(out=st_tile[:, col + 1:col + 2], in_=st_tile[:, col + 1:col + 2], func=AF.Ln), reads=[st_tile], writes=[st_tile])
            P.op(ACT, lambda e: e.activation(out=st_tile[:, col + 2:col + 3], in_=st_tile[:, col + 1:col + 2], func=AF.Exp, scale=-0.5),
                 reads=[st_tile], writes=[st_tile])
            return st_tile[:, col + 2:col + 3]
        P.op(DVE, lambda e: e.memset(st_tile[:], 0.0), writes=[st_tile])
        P.op(ACT, lambda e: e.activation(out=junk[:, 0:n], in_=src_ap, func=AF.Square, accum_out=st_tile[:, col:col + 1]),
             reads=[src_tl], writes=[junk, st_tile])
        P.op(ACT, lambda e: e.activation(out=st_tile[:, col + 1:col + 2], in_=st_tile[:, col:col + 1], func=AF.Sqrt,
                                         scale=1.0 / n, bias=EPS), reads=[st_tile], writes=[st_tile])
        P.op(DVE, lambda e: e.reciprocal(out=st_tile[:, col + 2:col + 3], in_=st_tile[:, col + 1:col + 2]),
             reads=[st_tile], writes=[st_tile])
        return st_tile[:, col + 2:col + 3]

    def transpose_to(tp, src, nchunk, dst_ap_fn, dst_tl):
        for c in range(nchunk):
            P.op(PE, (lambda c: lambda e: e.transpose(out=tp[:, c * 128:(c + 1) * 128], in_=src[:, c * 128:(c + 1) * 128],
                                                      identity=ident[:]))(c), reads=[src, ident], writes=[tp])
        P.op(ACT, lambda e: e.copy(out=dst_ap_fn(), in_=tp[:, 0:nchunk * 128].rearrange("p (c t) -> p c t", c=nchunk)),
             reads=[tp], writes=[dst_tl])

    def load_w_bf16(dst, src_ap, nk, name):
        v = src_ap.rearrange("(k p) n -> p k n", p=128)
        for k in range(0, nk, 8):
            k1 = min(nk, k + 8)
            P.dma(POOL, (lambda k, k1: lambda e: e.dma_start(out=dst[:, k:k1, :], in_=v[:, k:k1, :]))(k, k1), writes=[dst])

    def norm_residual_store(ph, m_ps2, gain, hin_ap, hin_buf, hout_ap, hout_tl, stt, junk, hres, outt):
        P.op(DVE, lambda e: e.memset(stt[:], 0.0), writes=[stt])
        for hf in range(2):
            P.op(ACT, (lambda hf: lambda e: e.activation(out=junk[:, 0:512], in_=m_ps2[hf][:], func=AF.Square,
                                                         accum_out=stt[:, hf:hf + 1]))(hf),
                 reads=[m_ps2[hf]], writes=[junk, stt])
        P.op(DVE, lambda e: e.tensor_tensor(out=stt[:, 2:3], in0=stt[:, 0:1], in1=stt[:, 1:2], op=ALU.add), reads=[stt], writes=[stt])
        P.op(ACT, lambda e: e.activation(out=stt[:, 3:4], in_=stt[:, 2:3], func=AF.Sqrt, scale=1.0 / D, bias=EPS),
             reads=[stt], writes=[stt])
        P.op(DVE, lambda e: e.reciprocal(out=stt[:, 4:5], in_=stt[:, 3:4]), reads=[stt], writes=[stt])
        P.dma(SP, lambda e: e.dma_start(out=hres[:], in_=hin_ap), reads=[hin_buf], writes=[hres])
        for hf in range(2):
            P.op(DVE, (lambda hf: lambda e: e.scalar_tensor_tensor(out=outt[:, hf * 512:(hf + 1) * 512], in0=m_ps2[hf][:],
                                                                   scalar=stt[:, 4:5], in1=gain[:, hf * 512:(hf + 1) * 512],
                                                                   op0=ALU.mult, op1=ALU.mult))(hf),
                 reads=[m_ps2[hf], stt, gain], writes=[outt])
        P.op(POOL, lambda e: e.tensor_tensor(out=outt[:], in0=outt[:], in1=hres[:], op=ALU.add), reads=[outt, hres], writes=[outt])
        P.dma(POOL, lambda e: e.dma_start(out=hout_ap, in_=outt[:]), reads=[outt], writes=[hout_tl])

    def phase_A():
        with ExitStack() as ph:
            g00 = load_gain(ph, 0, 0)
            w = sbt(ph, "A_w", [128, 8, WIN], BF16)
            load_w_bf16(w, e_w_in, 8, "w_in")
            wg = sbt(ph, "A_wg", [17, 256], BF16)
            P.dma(POOL, lambda e: e.dma_start(out=wg[0:16, :], in_=gla_w_gate), writes=[wg])
            P.dma(POOL, lambda e: e.dma_start(out=wg[16:17, :], in_=gla_b_gate.unsqueeze(0)), writes=[wg])
            U32 = sbt(ph, "A_U32", [128, 128], F32)
            L32 = sbt(ph, "A_L32", [128, 128], F32)
            U4 = sbt(ph, "A_U4", [128, 512], F32)
            P.dma(SP, lambda e: e.dma_start(out=U32[:], in_=cd["c_U"]), writes=[U32])
            P.dma(SP, lambda e: e.dma_start(out=L32[:], in_=cd["c_Ls"]), writes=[L32])
            P.dma(SP, lambda e: e.dma_start(out=U4[:], in_=cd["c_U4"]), writes=[U4])
            gnb = sbt(ph, "A_gnb", [128, 512], F32)
            P.dma(SP, lambda e: e.dma_start(out=gnb[:], in_=gla_norm.partition_broadcast(128)), writes=[gnb])
            ngb = sbt(ph, "A_ngb", [128, 24], F32)
            P.dma(SP, lambda e: e.dma_start(out=ngb[:], in_=nsa_gate_b.partition_broadcast(128)), writes=[ngb])

            xt_r = Rot([sbt(ph, f"A_xt{i}", [128, D], F32) for i in range(2)])
            stA_r = Rot([sbt(ph, f"A_stA{i}", [128, 4], F32) for i in range(4)])
            stG_r = Rot([sbt(ph, f"A_stG{i}", [128, 16], F32) for i in range(2)])
            xn_r = Rot([sbt(ph, f"A_xn{i}", [128, D], BF16) for i in range(2)])
            xnT_r = Rot([sbt(ph, f"A_xnT{i}", [128, 8, 512], BF16) for i in range(2)])
            qT = sbt(ph, "A_qT", [64, 4, 512], F32)
            kT = sbt(ph, "A_kT", [64, 4, 512], F32)
            glrT = sbt(ph, "A_glrT", [32, 512], BF16)
            P.op(DVE, lambda e: e.memset(glrT[:], 1.0), writes=[glrT])
            nq_st = Rot([sbt(ph, f"A_nq{i}", [128, 4, 512], BF16) for i in range(2)])
            kt_st = Rot([sbt(ph, f"A_kt{i}", [128, 4, 512], BF16) for i in range(2)])
            ktok_r = Rot([sbt(ph, f"A_ktok{i}", [128, 256], F32) for i in range(2)])
            v_r = Rot([sbt(ph, f"A_v{i}", [128, 512], BF16) for i in range(3)])
            sr_r = Rot([sbt(ph, f"A_sr{i}", [128, 512], F32) for i in range(3)])
            vt_r = Rot([sbt(ph, f"A_vt{i}", [128, 256], BF16) for i in range(2)])
            gt_r = Rot([sbt(ph, f"A_gt{i}", [128, 24], F32) for i in range(2)])
            ex_r = Rot([sbt(ph, f"A_ex{i}", [128, 256], F32) for i in range(2)])
            sp_r = Rot([sbt(ph, f"A_sp{i}", [128, 256], F32) for i in range(2)])
            eb_r = Rot([sbt(ph, f"A_eb{i}", [128, 256], F32) for i in range(2)])
            kend_r = Rot([sbt(ph, f"A_kend{i}", [128, 256], BF16) for i in range(2)])
            eq_r = Rot([sbt(ph, f"A_eq{i}", [64, 4, 128], F32) for i in range(2)])
            ek_r = Rot([sbt(ph, f"A_ek{i}", [64, 4, 128], F32) for i in range(2)])
            qtT_r = Rot([sbt(ph, f"A_qtT{i}", [128, 4, 128], BF16) for i in range(2)])
            ktT_r = Rot([sbt(ph, f"A_ktT{i}", [128, 4, 128], BF16) for i in range(2)])
            for t in qtT_r.tiles + ktT_r.tiles:
                P.op(DVE, (lambda t: lambda e: e.memset(t[:], 0.0))(t), writes=[t])
            AT_r = Rot([sbt(ph, f"A_AT{i}", [128, 512], BF16) for i in range(2)])
            S32 = sbt(ph, "A_S32", [64, 4, 128], F32)
            Sbf = sbt(ph, "A_Sbf", [128, 4, 128], BF16)
            P.op(DVE, lambda e: e.memset(S32[:], 0.0), writes=[S32])
            P.op(DVE, lambda e: e.memset(Sbf[:], 0.0), writes=[Sbf])
            on = sbt(ph, "A_on", [128, 512], F32)
            oa_r = Rot([sbt(ph, f"A_oa{i}", [128, 512], BF16) for i in range(2)])
            tp = pst(ph, "A_tp", [128, 1024], BF16)
            pp = Rot([pst(ph, f"A_pp{i}", [128, 512]) for i in range(7)])

            class Cx:
                pass

            def S1(st):
                c = Cx()
                c.st = st
                c.xnT = xnT_r.get()
                for tl in range(4):
                    n = st * 4 + tl
                    xt = xt_r.get()
                    stt = stA_r.get()
                    xn = xn_r.get()
                    P.dma(SP, lambda e: e.dma_start(out=xt[:], in_=x_d[n * 128:(n + 1) * 128, :]), reads=[Bx], writes=[xt])
                    r = rms_rstd(stt, 0, xt[:], xt, xn, D, lnexp=True)
                    P.op(DVE, lambda e: e.scalar_tensor_tensor(out=xn[:], in0=xt[:], scalar=r, in1=g00[:], op0=ALU.mult, op1=ALU.mult),
                         reads=[xt, stt, g00], writes=[xn])
                    transpose_to(tp, xn, 8, lambda: c.xnT[:, :, tl * 128:(tl + 1) * 128], c.xnT)
                return c

            def S2(c):
                st, xnT = c.st, c.xnT

                def fm_proj(col0, m):
                    ps = pp.get()
                    for kc in range(8):
                        P.op(PE, lambda e: e.matmul(ps[0:m, :], lhsT=w[:, kc, col0:col0 + m], rhs=xnT[:, kc, :], start=(kc == 0), stop=(kc == 7)),
                             reads=[w, xnT], writes=[ps])
                    return ps

                for h in range(4):
                    ps = fm_proj(OFF["gq"] + 64 * h, 64)
                    P.op(ACT, lambda e: e.mul(out=qT[:, h, :], in_=ps[0:64, :], mul=0.125), reads=[ps], writes=[qT])
                    ps = fm_proj(OFF["gk"] + 64 * h, 64)
                    P.op(DVE, lambda e: e.tensor_copy(out=kT[:, h, :], in_=ps[0:64, :]), reads=[ps], writes=[kT])
                ps = fm_proj(OFF["glr"], 16)
                P.op(DVE, lambda e: e.tensor_copy(out=glrT[0:16, :], in_=ps[0:16, :]), reads=[ps], writes=[glrT])
                nqs = nq_st.get()
                for hp in range(4):
                    ps = fm_proj(OFF["nq"] + 128 * hp, 128)
                    P.op(ACT, lambda e: e.mul(out=nqs[:, hp, :], in_=ps[:, :], mul=0.125), reads=[ps], writes=[nqs])
                P.dma(POOL, lambda e: e.dma_start(out=QT[:, :, st * 512:(st + 1) * 512].rearrange("(hp hh) d s -> (hh d) hp s", hh=2), in_=nqs[:]),
                      reads=[nqs], writes=[QT])
                kts = kt_st.get()
                for i, nm in enumerate(("kc", "vc", "ks", "kw")):
                    ps = fm_proj(OFF[nm], 128)
                    P.op(DVE, lambda e: e.tensor_copy(out=kts[:, i, :], in_=ps[:, :]), reads=[ps], writes=[kts])
                P.dma(POOL, lambda e: e.dma_start(out=KT[:, :, st * 512:(st + 1) * 512].rearrange("(i g) d s -> (g d) i s", g=2), in_=kts[:]),
                      reads=[kts], writes=[KT])

            def S3a(c, tl):
                t = Cx()
                t.n = c.st * 4 + tl
                t.tsl = slice(tl * 128, (tl + 1) * 128)
                t.xnT = c.xnT
                xnT, tsl = c.xnT, t.tsl

                def tm_proj(ps, pc0, col0, ncol):
                    for kc in range(8):
                        P.op(PE, lambda e: e.matmul(ps[:, pc0:pc0 + ncol], lhsT=xnT[:, kc, tsl], rhs=w[:, kc, col0:col0 + ncol], start=(kc == 0), stop=(kc == 7)),
                             reads=[w, xnT], writes=[ps])

                t.tm_proj = tm_proj
                t.ktok = ktok_r.get()
                ps = pp.get()
                tm_proj(ps, 0, OFF["gk"], 256)
                P.op(ACT, lambda e: e.copy(out=t.ktok[:], in_=ps[:, 0:256]), reads=[ps], writes=[t.ktok])
                t.v = v_r.get()
                ps = pp.get()
                tm_proj(ps, 0, OFF["gv"], 512)
                P.op(DVE, lambda e: e.tensor_copy(out=t.v[:], in_=ps[:]), reads=[ps], writes=[t.v])
                return t

            def S3b(t):
                n, tm_proj = t.n, t.tm_proj
                t.sr = sr_r.get()
                sr = t.sr
                ps = pp.get()
                tm_proj(ps, 0, OFF["gr"], 512)
                P.op(ACT, lambda e: e.activation(out=sr[:], in_=ps[:], func=AF.Exp, scale=-1.0), reads=[ps], writes=[sr])
                P.op(DVE, lambda e: e.tensor_scalar(out=sr[:], in0=sr[:], scalar1=1.0, scalar2=None, op0=ALU.add), reads=[sr], writes=[sr])
                P.op(DVE, lambda e: e.reciprocal(out=sr[:], in_=sr[:]), reads=[sr], writes=[sr])
                P.op(DVE, lambda e: e.tensor_tensor(out=sr[:], in0=ps[:], in1=sr[:], op=ALU.mult), reads=[ps, sr], writes=[sr])
                ps = pp.get()
                tm_proj(ps, 0, OFF["vs"], 128)
                tm_proj(ps, 128, OFF["vw"], 128)
                tm_proj(ps, 256, OFF["ng"], 24)
                vt = vt_r.get()
                gt = gt_r.get()
                P.op(DVE, lambda e: e.tensor_copy(out=vt[:], in_=ps[:, 0:256]), reads=[ps], writes=[vt])
                P.op(DVE, lambda e: e.tensor_tensor(out=gt[:], in0=ps[:, 256:280], in1=ngb[:], op=ALU.add), reads=[ps, ngb], writes=[gt])
                P.op(ACT, lambda e: e.activation(out=gt[:], in_=gt[:], func=AF.Exp, scale=-1.0), reads=[gt], writes=[gt])
                P.op(DVE, lambda e: e.tensor_scalar(out=gt[:], in0=gt[:], scalar1=1.0, scalar2=None, op0=ALU.add), reads=[gt], writes=[gt])
                P.op(DVE, lambda e: e.reciprocal(out=gt[:], in_=gt[:]), reads=[gt], writes=[gt])
                P.dma(POOL, lambda e: e.dma_start(out=VT[n * 128:(n + 1) * 128, :], in_=vt[:]), reads=[vt], writes=[VT])
                P.dma(POOL, lambda e: e.dma_start(out=GT[n * 128:(n + 1) * 128, :], in_=gt[:]), reads=[gt], writes=[GT])

            def G1a(t):
                tsl = t.tsl
                t.ex, t.sp, t.eb = ex_r.get(), sp_r.get(), eb_r.get()
                t.kend, t.eq, t.ek = kend_r.get(), eq_r.get(), ek_r.get()
                t.qtT, t.ktT, t.AT = qtT_r.get(), ktT_r.get(), AT_r.get()
                ex, sp = t.ex, t.sp
                ps = pp.get()
                P.op(PE, lambda e: e.matmul(ps[:, 0:256], lhsT=glrT[0:17, tsl], rhs=wg[0:17, :], start=True, stop=True), reads=[glrT, wg], writes=[ps])
                P.op(ACT, lambda e: e.activation(out=ex[:], in_=ps[:, 0:256], func=AF.Exp, scale=-1.0), reads=[ps], writes=[ex])
                P.op(ACT, lambda e: e.activation(out=sp[:], in_=ex[:], func=AF.Ln, bias=1.0), reads=[ex], writes=[sp])

            def G1b(t):
                tsl, sp, eb, ek = t.tsl, t.sp, t.eb, t.ek
                ps = pp.get()
                P.op(PE, lambda e: e.matmul(ps[:, 0:256], lhsT=L32[:], rhs=sp[:], start=True, stop=True), reads=[L32, sp], writes=[ps])
                P.op(ACT, lambda e: e.activation(out=eb[:], in_=ps[:, 0:256], func=AF.Exp, scale=-1.0 / 16), reads=[ps], writes=[eb])
                P.op(DVE, lambda e: e.tensor_tensor(out=t.kend[:], in0=t.ktok[:], in1=eb[:], op=ALU.mult), reads=[t.ktok, eb], writes=[t.kend])
                ps = pp.get()
                for h in range(4):
                    P.op(PE, lambda e: e.matmul(ps[0:64, h * 128:(h + 1) * 128], lhsT=sp[:, 64 * h:64 * h + 64], rhs=U32[:], start=True, stop=True),
                         reads=[sp, U32], writes=[ps])
                P.op(ACT, lambda e: e.activation(out=t.eq[:].rearrange("p h t -> p (h t)"), in_=ps[0:64, :], func=AF.Exp, scale=-1.0 / 16), reads=[ps], writes=[t.eq])
                P.op(ACT, lambda e: e.activation(out=ek[:].rearrange("p h t -> p (h t)"), in_=ps[0:64, :], func=AF.Exp, scale=1.0 / 16), reads=[ps], writes=[ek])
                P.op(DVE, lambda e: e.tensor_tensor(out=t.qtT[0:64], in0=qT[:, :, tsl], in1=t.eq[:], op=ALU.mult), reads=[qT, t.eq], writes=[t.qtT])
                P.op(DVE, lambda e: e.tensor_tensor(out=t.ktT[0:64], in0=kT[:, :, tsl], in1=ek[:], op=ALU.mult), reads=[kT, ek], writes=[t.ktT])

            def G1c(t):
                ktT = t.ktT
                ps = pp.get()
                for h in range(4):
                    P.op(PE, lambda e: e.matmul(ps[:, h * 128:(h + 1) * 128], lhsT=ktT[:, h, :], rhs=t.qtT[:, h, :], start=True, stop=True),
                         reads=[ktT, t.qtT], writes=[ps])
                P.op(DVE, lambda e: e.tensor_tensor(out=t.AT[:], in0=ps[:], in1=U4[:], op=ALU.mult), reads=[ps, U4], writes=[t.AT])

            def G2(t):
                n, v, AT, qtT, kend, eq, sr = t.n, t.v, t.AT, t.qtT, t.kend, t.eq, t.sr
                po = pp.get()
                for h in range(4):
                    hs = slice(h * 128, (h + 1) * 128)
                    P.op(PE, lambda e: e.matmul(po[:, hs], lhsT=AT[:, hs], rhs=v[:, hs], start=True, stop=False), reads=[AT, v], writes=[po])
                    P.op(PE, lambda e: e.matmul(po[:, hs], lhsT=qtT[:, h, :], rhs=Sbf[:, h, :], start=False, stop=True), reads=[qtT, Sbf], writes=[po])
                pd = pp.get()
                for h in range(4):
                    hs = slice(h * 128, (h + 1) * 128)
                    P.op(PE, lambda e: e.matmul(pd[0:64, hs], lhsT=kend[:, 64 * h:64 * h + 64], rhs=v[:, hs], start=True, stop=True), reads=[kend, v], writes=[pd])
                for h in range(4):
                    hs = slice(h * 128, (h + 1) * 128)
                    P.op(DVE, lambda e: e.scalar_tensor_tensor(out=S32[:, h, :], in0=S32[:, h, :], scalar=eq[:, h, 127:128], in1=pd[0:64, hs],
                                                               op0=ALU.mult, op1=ALU.add), reads=[S32, eq, pd], writes=[S32])
                P.op(ACT, lambda e: e.copy(out=Sbf[0:64], in_=S32[:]), reads=[S32], writes=[Sbf])
                stt = stG_r.get()
                oa = oa_r.get()
                P.op(DVE, lambda e: e.memset(stt[:], 0.0), writes=[stt])
                for h in range(4):
                    hs = slice(h * 128, (h + 1) * 128)
                    P.op(ACT, lambda e: e.activation(out=oa[:, hs], in_=po[:, hs], func=AF.Square, accum_out=stt[:, h:h + 1]), reads=[po], writes=[oa, stt])
                P.op(DVE, lambda e: e.tensor_scalar(out=stt[:, 4:8], in0=stt[:, 0:4], scalar1=1.0 / 128, scalar2=EPS, op0=ALU.mult, op1=ALU.add),
                     reads=[stt], writes=[stt])
                P.op(ACT, lambda e: e.activation(out=stt[:, 4:8], in_=stt[:, 4:8], func=AF.Ln), reads=[stt], writes=[stt])
                P.op(ACT, lambda e: e.activation(out=stt[:, 8:12], in_=stt[:, 4:8], func=AF.Exp, scale=-0.5), reads=[stt], writes=[stt])
                for h in range(4):
                    hs = slice(h * 128, (h + 1) * 128)
                    P.op(DVE, lambda e: e.scalar_tensor_tensor(out=on[:, hs], in0=po[:, hs], scalar=stt[:, 8 + h:9 + h], in1=gnb[:, hs],
                                                               op0=ALU.mult, op1=ALU.mult), reads=[po, stt, gnb], writes=[on])
                P.op(DVE, lambda e: e.tensor_tensor(out=oa[:], in0=on[:], in1=sr[:], op=ALU.mult), reads=[on, sr], writes=[oa])
                P.dma(POOL, lambda e: e.dma_start(out=OA[n * 128:(n + 1) * 128, :], in_=oa[:]), reads=[oa], writes=[OA])

            cur = S1(0)
            pend = None
            for st in range(NST):
                S2(cur)
                nxt = S1(st + 1) if st + 1 < NST else None
                for tl in range(4):
                    t = S3a(cur, tl)
                    G1a(t)
                    S3b(t)
                    G1b(t)
                    if pend is not None:
                        G2(pend)
                    G1c(t)
                    pend = t
                cur = nxt
            G2(pend)
            P.barrier()
            P.emit()

    def phase_BC():
        with ExitStack() as ph:
            ones = sbt(ph, "C_ones", [128, 128], BF16)
            P.dma(POOL, lambda e: e.dma_start(out=ones[:], in_=cd["c_ones"]), writes=[ones])
            KCT = [sbt(ph, f"C_KCT{g}", [128, NCC * 128], BF16) for g in range(2)]
            VC = [sbt(ph, f"C_VC{g}", [128, NCC, 65], BF16) for g in range(2)]
            for g in range(2):
                P.op(DVE, (lambda g: lambda e: e.memset(KCT[g][:], 0.0))(g), writes=[KCT[g]])
                P.op(DVE, (lambda g: lambda e: e.memset(VC[g][:], 1.0))(g), writes=[VC[g]])
            with ExitStack() as pb:
                w1 = sbt(pb, "B_w1", [128, 2, 16, 128], BF16)
                w2 = sbt(pb, "B_w2", [128, 2, 64], BF16)
                posT = sbt(pb, "B_posT", [128, 2, 16], BF16)
                for i in range(2):
                    P.dma(POOL, (lambda i: lambda e: e.dma_start(out=w1[:, i, :, :], in_=nsa_cmp_w1[i].rearrange("(l q) h -> q l h", q=128)))(i),
                          writes=[w1])
                    P.dma(POOL, (lambda i: lambda e: e.dma_start(out=w2[:, i, :], in_=nsa_cmp_w2[i]))(i), writes=[w2])
                    P.dma(POOL, (lambda i: lambda e: e.dma_start(out=posT[:, i, :], in_=nsa_cmp_pos[i].rearrange("(l two) d -> (two d) l", two=2),
                                                                 allow_slow_non_contiguous=True))(i), writes=[posT])
                xT_r = Rot([sbt(pb, f"B_xT{i}", [128, S], BF16) for i in range(2)])
                for t in xT_r.tiles:
                    P.op(POOL, (lambda t: lambda e: e.memset(t[:], 0.0))(t), writes=[t])
                bias = sbt(pb, "B_bias", [128, 2], F32)
                gh_r = Rot([sbt(pb, f"B_gh{i}", [128, NCC * 128], BF16) for i in range(2)])
                for t in gh_r.tiles:
                    P.op(DVE, (lambda t: lambda e: e.memset(t[:], 0.0))(t), writes=[t])
                pp = Rot([pst(pb, f"B_pp{i}", [128, 512]) for i in range(4)])
                for i in range(2):
                    ps = pp.get()
                    for l in range(16):
                        P.op(PE, (lambda i, l, ps: lambda e: e.matmul(ps[:, 0:1], lhsT=w1[:, i, l, :], rhs=posT[:, i, l:l + 1],
                                                                      start=(l == 0), stop=(l == 15)))(i, l, ps), reads=[w1, posT], writes=[ps])
                    P.op(DVE, (lambda i, ps: lambda e: e.tensor_copy(out=bias[:, i:i + 1], in_=ps[:, 0:1]))(i, ps), reads=[ps], writes=[bias])
                for i in range(2):
                    for g in range(2):
                        xT = xT_r.get()
                        P.dma(SP, (lambda i, g, xT: lambda e: e.dma_start(out=xT[0:64, :], in_=KT[i * 2 + g, :, :]))(i, g, xT), reads=[KT], writes=[xT])
                        P.dma(SP, (lambda i, g, xT: lambda e: e.dma_start(out=xT[64:128, 0:S - 1], in_=KT[i * 2 + g, :, 1:S]))(i, g, xT), reads=[KT], writes=[xT])
                        gh = gh_r.get()
                        for c0 in range(0, NCMP, 512):
                            ncol = min(512, NCMP - c0)
                            ps = pp.get()
                            for l in range(16):
                                a0 = c0 * 16 + 2 * l
                                P.op(PE, (lambda i, l, ps, a0, ncol, xT: lambda e: e.matmul(
                                    ps[:, 0:ncol], lhsT=w1[:, i, l, :], rhs=xT[:, a0:a0 + 16 * (ncol - 1) + 1:16],
                                    start=(l == 0), stop=(l == 15)))(i, l, ps, a0, ncol, xT), reads=[w1, xT], writes=[ps])
                            P.op(ACT, (lambda i, ps, c0, ncol, gh: lambda e: e.activation(out=gh[:, c0:c0 + ncol], in_=ps[:, 0:ncol],
                                                                                          func=AF.Gelu_apprx_tanh, bias=bias[:, i:i + 1]))(i, ps, c0, ncol, gh),
                                 reads=[ps, bias], writes=[gh])
                        if i == 0:
                            for c0 in range(0, NCMP, 512):
                                ncol = min(512, NCMP - c0)
                                ps = pp.get()
                                P.op(PE, (lambda ps, c0, ncol, gh: lambda e: e.matmul(ps[0:64, 0:ncol], lhsT=w2[:, 0, :], rhs=gh[:, c0:c0 + ncol],
                                                                                      start=True, stop=True))(ps, c0, ncol, gh), reads=[w2, gh], writes=[ps])
                                P.op(DVE, (lambda g, ps, c0, ncol: lambda e: e.tensor_copy(out=KCT[g][0:64, c0:c0 + ncol], in_=ps[0:64, 0:ncol]))(g, ps, c0, ncol),
                                     reads=[ps], writes=[KCT[g]])
                        else:
                            ps = pp.get()
                            for cc in range(NCC):
                                P.op(PE, (lambda ps, cc, gh: lambda e: e.matmul(ps[:, cc * 64:(cc + 1) * 64], lhsT=gh[:, cc * 128:(cc + 1) * 128], rhs=w2[:, 1, :],
                                                                                start=True, stop=True))(ps, cc, gh), reads=[w2, gh], writes=[ps])
                            P.op(DVE, (lambda g, ps: lambda e: e.tensor_copy(out=VC[g][:, :, 0:64],
                                                                             in_=ps[:, 0:NCC * 64].rearrange("p (c d) -> p c d", c=NCC)))(g, ps),
                                 reads=[ps], writes=[VC[g]])
                P.barrier()
                P.emit()
            Gb = sbt(ph, "C_G", [128, S], BF16)
            OV = sbt(ph, "C_OV", [128, NCC, NSEL], BF16)
            CB4 = sbt(ph, "C_CB4", [128, 512], BF16)
            WB4 = sbt(ph, "C_WB4", [128, 512], BF16)
            TA = sbt(ph, "C_TA", [128, 2 * NSEL], F32)
            TB = sbt(ph, "C_TB", [128, 2 * NSEL], F32)
            P.dma(POOL, lambda e: e.dma_start(out=Gb[:], in_=cd["c_G"]), writes=[Gb])
            P.dma(POOL, lambda e: e.dma_start(out=OV[:], in_=cd["c_OV"]), writes=[OV])
            P.dma(POOL, lambda e: e.dma_start(out=CB4[:], in_=cd["c_CB4"]), writes=[CB4])
            P.dma(POOL, lambda e: e.dma_start(out=WB4[:], in_=cd["c_WB4"]), writes=[WB4])
            P.dma(SP, lambda e: e.dma_start(out=TA[:], in_=cd["c_TA"]), writes=[TA])
            P.dma(SP, lambda e: e.dma_start(out=TB[:], in_=cd["c_TB"]), writes=[TB])
            KS = sbt(ph, "C_KS", [128, S], BF16)
            KW = sbt(ph, "C_KW", [128, S], BF16)
            P.op(POOL, lambda e: e.memset(KW[:], 0.0), writes=[KW])
            P.dma(POOL, lambda e: e.dma_start(out=KS[64:128, :], in_=cd["c_Zaug"]), writes=[KS])
            VS = sbt(ph, "C_VS", [128, NT, 65], BF16)
            VW = sbt(ph, "C_VW", [128, NT, 65], BF16)
            P.op(POOL, lambda e: e.memset(VS[:], 1.0), writes=[VS])
            P.op(POOL, lambda e: e.memset(VW[:], 1.0), writes=[VW])
            NQ = 2 if NSEL > 64 else 1
            qa_r = Rot([sbt(ph, f"C_qa{i}", [128, NQ, 4, 128], BF16) for i in range(3)])
            for t in qa_r.tiles:
                P.op(POOL, (lambda t: lambda e: e.memset(t[:], 0.0))(t), writes=[t])
            gt_r = Rot([sbt(ph, f"C_gt{i}", [128, 24], F32) for i in range(3)])
            pc_r = Rot([sbt(ph, f"C_pc{i}", [128, 512], BF16) for i in range(2 * NCC + 1)])
            pe_r = Rot([sbt(ph, f"C_pe{i}", [128, 512], BF16) for i in range(5)])
            sc = sbt(ph, "C_sc", [128, NSEL], F32)
            wk = sbt(ph, "C_wk", [128, NSEL], F32)
            t8 = sbt(ph, "C_t8", [128, 16], F32)
            selm = sbt(ph, "C_selm", [128, NSEL], F32)
            okm = sbt(ph, "C_okm", [128, NSEL], F32)
            sb16 = sbt(ph, "C_sb16", [128, 256], BF16)
            P.op(DVE, lambda e: e.memset(sb16[:], 0.0), writes=[sb16])
            cf = sbt(ph, "C_cf", [128, 32], F32)
            acc_r = Rot([sbt(ph, f"C_acc{i}", [128, 4, 64], F32) for i in range(2)])
            ob_r = Rot([sbt(ph, f"C_ob{i}", [128, 4, 64], BF16) for i in range(2)])
            p_oc = pst(ph, "C_poc", [128, 512])
            p_os = pst(ph, "C_pos", [128, 512])
            p_ow = pst(ph, "C_pow", [128, 512])
            p_imp = pst(ph, "C_pimp", [128, 512])
            p_tp = pst(ph, "C_ptp", [128, 1024], BF16)
            p_s = Rot([pst(ph, f"C_ps{i}", [128, 512]) for i in range(3)])

            class Ctx:
                pass

            def pv(po, pt, vtile, vidx, first, last):
                for h in range(4):
                    P.op(PE, (lambda h: lambda e: e.matmul(po[:, h * 65:(h + 1) * 65], lhsT=pt[:, h * 128:(h + 1) * 128], rhs=vtile[:, vidx, :],
                                                           start=(first and h == 0), stop=last, skip_group_check=True))(h),
                         reads=[pt, vtile], writes=[po])

            def combine(c, po, x, firstb, lastb):
                g = c.g
                den = po[:, 0:260].rearrange("p (h c) -> p h c", h=4)[:, :, 64]
                rs_ = slice(8 * x, 8 * x + 4)
                cs = slice(8 * x + 4, 8 * x + 8)
                P.op(DVE, lambda e: e.tensor_scalar(out=cf[:, rs_], in0=den, scalar1=1e-30, scalar2=None, op0=ALU.max), reads=[po], writes=[cf])
                P.op(DVE, lambda e: e.reciprocal(out=cf[:, rs_], in_=cf[:, rs_]), reads=[cf], writes=[cf])
                gv = c.gt[:, 12 * g:12 * g + 12].rearrange("p (h x) -> p h x", x=3)[:, :, x]
                P.op(DVE, lambda e: e.tensor_tensor(out=cf[:, cs], in0=cf[:, rs_], in1=gv, op=ALU.mult), reads=[cf, c.gt], writes=[cf])
                for h in range(4):
                    src = po[:, h * 65:h * 65 + 64]
                    dst = c.ob[:, h, :] if lastb else c.acc[:, h, :]
                    col = 8 * x + 4 + h
                    if firstb:
                        P.op(DVE, lambda e: e.tensor_scalar(out=dst, in0=src, scalar1=cf[:, col:col + 1], scalar2=None, op0=ALU.mult),
                             reads=[po, cf], writes=[c.acc, c.ob])
                    else:
                        P.op(DVE, lambda e: e.scalar_tensor_tensor(out=dst, in0=src, scalar=cf[:, col:col + 1], in1=c.acc[:, h, :],
                                                                   op0=ALU.mult, op1=ALU.add), reads=[po, cf, c.acc], writes=[c.acc, c.ob])

            def comp_part(g, n):
                c = Ctx()
                c.g, c.n = g, n
                c.qa = qa_r.get()
                c.gt = gt_r.get()
                c.acc = acc_r.get()
                c.ob = ob_r.get()
                for qi in range(NQ if n >= 32 else 1):
                    P.dma(SP, (lambda qi: lambda e: e.dma_start(out=c.qa[0:64, qi], in_=QT[4 * g:4 * g + 4, :, n * 128:(n + 1) * 128].rearrange("h d s -> d h s")))(qi),
                          reads=[QT], writes=[c.qa])
                P.dma(SP, lambda e: e.dma_start(out=c.gt[:], in_=GT[n * 128:(n + 1) * 128, :]), reads=[GT], writes=[c.gt])
                c.q2 = [c.qa[:, qi].rearrange("p h t -> p (h t)") for qi in range(NQ)]
                c.ncc = min((128 * n + 127) // 2048 + 1, NCC)
                c.pcs = [None] * c.ncc
                return c

            def select_part(c):
                n, ncc = c.n, c.ncc
                for h in range(4):
                    for cc in range(ncc):
                        P.op(PE, lambda e: e.matmul(p_imp[:, h * NSEL:(h + 1) * NSEL], lhsT=c.pcs[cc][:, h * 128:(h + 1) * 128], rhs=OV[:, cc, :],
                                                    start=(cc == 0), stop=(cc == ncc - 1)), reads=[c.pcs[cc], OV], writes=[p_imp])
                combine(c, p_oc, 0, True, False)
                for h in range(4):
                    src = p_imp[:, h * NSEL:(h + 1) * NSEL]
                    if h == 0:
                        P.op(DVE, lambda e: e.tensor_scalar(out=sc[:], in0=src, scalar1=cf[:, 0:1], scalar2=None, op0=ALU.mult), reads=[p_imp, cf], writes=[sc])
                    else:
                        P.op(DVE, lambda e: e.scalar_tensor_tensor(out=sc[:], in0=src, scalar=cf[:, h:h + 1], in1=sc[:], op0=ALU.mult, op1=ALU.add),
                             reads=[p_imp, cf, sc], writes=[sc])
                w0 = NSEL - 2 * n
                P.op(DVE, lambda e: e.tensor_tensor(out=sc[:], in0=sc[:], in1=TA[:, w0:w0 + NSEL], op=ALU.mult), reads=[sc, TA], writes=[sc])
                P.op(DVE, lambda e: e.tensor_tensor(out=sc[:], in0=sc[:], in1=TB[:, w0:w0 + NSEL], op=ALU.add), reads=[sc, TB], writes=[sc])
                P.op(DVE, lambda e: e.memset(sc[:, 0:1], 1002.0), reads=[sc], writes=[sc])
                P.op(DVE, lambda e: e.max(out=t8[:, 0:8], in_=sc[:]), reads=[sc], writes=[t8])
                P.op(DVE, lambda e: e.match_replace(out=wk[:], in_to_replace=t8[:, 0:8], in_values=sc[:], imm_value=-1e9), reads=[sc, t8], writes=[wk])
                P.op(DVE, lambda e: e.max(out=t8[:, 8:16], in_=wk[:]), reads=[wk], writes=[t8])
                P.op(DVE, lambda e: e.tensor_scalar(out=selm[:], in0=sc[:], scalar1=t8[:, 15:16], scalar2=None, op0=ALU.is_ge), reads=[sc, t8], writes=[selm])
                P.op(DVE, lambda e: e.tensor_single_scalar(out=okm[:], in_=sc[:], scalar=-500.0, op=ALU.is_gt), reads=[sc], writes=[okm])
                P.op(DVE, lambda e: e.tensor_tensor(out=selm[:], in0=selm[:], in1=okm[:], op=ALU.mult), reads=[selm, okm], writes=[selm])
                lo = min(NSEL, 64)
                P.op(DVE, lambda e: e.tensor_scalar(out=sb16[:, 128 + 64:128 + 64 + lo], in0=selm[:, 0:lo], scalar1=-1.0, scalar2=-NEG, op0=ALU.add, op1=ALU.mult),
                     reads=[selm], writes=[sb16])
                P.op(PE, lambda e: e.transpose(out=p_tp[:, 0:128], in_=sb16[:, 128:256], identity=ident[:]), reads=[sb16, ident], writes=[p_tp])
                if NSEL > 64:
                    P.op(DVE, lambda e: e.tensor_scalar(out=sb16[:, 64:NSEL], in0=selm[:, 64:NSEL], scalar1=-1.0, scalar2=-NEG, op0=ALU.add, op1=ALU.mult),
                         reads=[selm], writes=[sb16])
                    if n >= 32:
                        P.op(PE, lambda e: e.transpose(out=p_tp[:, 128:256], in_=sb16[:, 0:128], identity=ident[:]), reads=[sb16, ident], writes=[p_tp])
                P.op(DVE, lambda e: e.tensor_copy(out=c.qa[64:128, 0], in_=p_tp[64:128, 0:128].unsqueeze(1).to_broadcast([64, 4, 128])),
                     reads=[p_tp], writes=[c.qa])
                if NSEL > 64 and n >= 32:
                    P.op(DVE, lambda e: e.tensor_copy(out=c.qa[64:128, 1], in_=p_tp[64:128, 128:256].unsqueeze(1).to_broadcast([64, 4, 128])),
                         reads=[p_tp], writes=[c.qa])

            def scores(item):
                kind, k, c = item
                n = c.n
                g = c.g
                ps = p_s.get()
                ksl = slice(k * 128, (k + 1) * 128)
                if kind == "c":
                    cc = k
                    off = 128 * n - 2048 * cc
                    need_mask = off < 2176
                    P.op(PE, lambda e: e.matmul(ps[:], lhsT=KCT[g][:, cc * 128:(cc + 1) * 128], rhs=c.q2[0], start=True, stop=(not need_mask)),
                         reads=[KCT[g], c.qa], writes=[ps])
                    if need_mask:
                        P.op(PE, lambda e: e.matmul(ps[:].rearrange("p (h t) -> p h t", h=4), lhsT=ident[:],
                                                    rhs=Gb[:, off:off + 128].unsqueeze(1).to_broadcast([128, 4, 128]), start=False, stop=True),
                             reads=[ident, Gb], writes=[ps])
                    pc = pc_r.get()
                    c.pcs[cc] = pc
                    P.op(ACT, lambda e: e.activation(out=pc[:], in_=ps[:], func=AF.Exp), reads=[ps], writes=[pc])
                    return pc
                if kind == "s":
                    q = c.q2[1] if k >= 32 else c.q2[0]
                    P.op(PE, lambda e: e.matmul(ps[:], lhsT=KS[:, ksl], rhs=q, start=True, stop=(k != n)), reads=[KS, c.qa], writes=[ps])
                    if k == n:
                        P.op(PE, lambda e: e.matmul(ps[:], lhsT=ident[:], rhs=CB4[:], start=False, stop=True), reads=[ident, CB4], writes=[ps])
                else:
                    edge = (k == n) or (k == n - 4)
                    P.op(PE, lambda e: e.matmul(ps[:], lhsT=KW[:, ksl], rhs=c.q2[0], start=True, stop=(not edge)), reads=[KW, c.qa], writes=[ps])
                    if k == n:
                        P.op(PE, lambda e: e.matmul(ps[:], lhsT=ident[:], rhs=CB4[:], start=False, stop=True), reads=[ident, CB4], writes=[ps])
                    elif k == n - 4:
                        P.op(PE, lambda e: e.matmul(ps[:], lhsT=ident[:], rhs=WB4[:], start=False, stop=True), reads=[ident, WB4], writes=[ps])
                pe = pe_r.get()
                P.op(ACT, lambda e: e.activation(out=pe[:], in_=ps[:], func=AF.Exp), reads=[ps], writes=[pe])
                return pe

            def finish(item, pe):
                kind, k, c = item
                n = c.n
                if kind == "c":
                    pv(p_oc, pe, VC[c.g], k, k == 0, k == c.ncc - 1)
                elif kind == "s":
                    pv(p_os, pe, VS, k, k == 0, k == n)
                    if k == n:
                        combine(c, p_os, 1, False, True)
                        P.dma(POOL, lambda e: e.dma_start(out=OB[n * 128:(n + 1) * 128, c.g * 256:(c.g + 1) * 256],
                                                          in_=c.ob[:].rearrange("p h d -> p (h d)")), reads=[c.ob], writes=[OB])
                else:
                    pv(p_ow, pe, VW, k, k == max(0, n - 4), k == n)
                    if k == n:
                        combine(c, p_ow, 2, False, False)

            LOOK = 2
            queue = []

            def push(item):
                queue.append((item, scores(item)))
                while len(queue) > LOOK:
                    it, pe = queue.pop(0)
                    finish(it, pe)

            def drain():
                while queue:
                    it, pe = queue.pop(0)
                    finish(it, pe)

            for g in range(2):
                P.dma(SP, (lambda g: lambda e: e.dma_start(out=KS[0:64, :], in_=KT[4 + g, :, :]))(g), reads=[KT], writes=[KS])
                P.dma(SP, (lambda g: lambda e: e.dma_start(out=KW[0:64, :], in_=KT[6 + g, :, :]))(g), reads=[KT], writes=[KW])
                P.dma(SP, (lambda g: lambda e: e.dma_start(out=VS[:, :, 0:64], in_=VT[:, g * 64:(g + 1) * 64].rearrange("(n p) d -> p n d", p=128)))(g),
                      reads=[VT], writes=[VS])
                P.dma(SP, (lambda g: lambda e: e.dma_start(out=VW[:, :, 0:64], in_=VT[:, 128 + g * 64:128 + (g + 1) * 64].rearrange("(n p) d -> p n d", p=128)))(g),
                      reads=[VT], writes=[VW])
                prev = None
                for n in range(NT + 1):
                    cur = comp_part(g, n) if n < NT else None
                    if cur is not None:
                        for cc in range(cur.ncc):
                            push(("c", cc, cur))
                    sel_items = [("s", k, prev) for k in range(prev.n + 1)] if prev is not None else []
                    half = len(sel_items) // 2
                    for it in sel_items[:half]:
                        push(it)
                    if cur is not None:
                        if any(it[0] == "c" and it[2] is cur for it, _ in queue):
                            drain()
                        select_part(cur)
                    for it in sel_items[half:]:
                        push(it)
                    if cur is not None:
                        for k in range(max(0, n - 4), n + 1):
                            push(("w", k, cur))
                    prev = cur
                drain()
            P.barrier()
            P.emit()

    def phase_D():
        with ExitStack() as ph:
            g01 = load_gain(ph, 0, 1)
            wo = sbt(ph, "D_wo", [128, 8, D], BF16)
            load_w_bf16(wo, e_w_out, 8, "wo")
            m_r = Rot([sbt(ph, f"D_m{i}", [128, D], BF16) for i in range(2)])
            mT_r = Rot([sbt(ph, f"D_mT{i}", [128, 8, 128], BF16) for i in range(2)])
            st_r = Rot([sbt(ph, f"D_st{i}", [128, 8], F32) for i in range(2)])
            hres_r = Rot([sbt(ph, f"D_hr{i}", [128, D], F32) for i in range(2)])
            out_r = Rot([sbt(ph, f"D_out{i}", [128, D], F32) for i in range(2)])
            tp = pst(ph, "D_tp", [128, 1024], BF16)
            pp = Rot([pst(ph, f"D_pp{i}", [128, 512]) for i in range(6)])
            def Da(n):
                m = m_r.get()
                rs = slice(n * 128, (n + 1) * 128)
                P.dma(SP, lambda e: e.dma_start(out=m[:, 0:512], in_=OA[rs, :]), reads=[OA], writes=[m])
                P.dma(SP, lambda e: e.dma_start(out=m[:, 512:1024], in_=OB[rs, :]), reads=[OB], writes=[m])
                mT = mT_r.get()
                transpose_to(tp, m, 8, lambda: mT[:], mT)
                return (n, mT)

            def Db(c):
                n, mT = c
                rs = slice(n * 128, (n + 1) * 128)
                ps2 = [pp.get(), pp.get()]
                for hf in range(2):
                    for kc in range(8):
                        P.op(PE, lambda e: e.matmul(ps2[hf][:], lhsT=mT[:, kc, :], rhs=wo[:, kc, hf * 512:(hf + 1) * 512], start=(kc == 0), stop=(kc == 7)),
                             reads=[mT, wo], writes=[ps2[hf]])
                outt = out_r.get()
                norm_residual_store(ph, ps2, g01, x_d[rs, :], Bx, H0[rs, :], H0, st_r.get(), outt, hres_r.get(), outt)

            cur = Da(0)
            for n in range(NT):
                nxt = Da(n + 1) if n + 1 < NT else None
                Db(cur)
                cur = nxt
            P.barrier()
            P.emit()

    def phase_FFN(layer, Hin, Hout_ap_fn, Hout_tl, tag):
        TS = 256
        NG = S // TS
        TPG = TS // 128
        with ExitStack() as ph:
            gpre = load_gain(ph, layer, 2)
            gpost = load_gain(ph, layer, 3)
            w1 = sbt(ph, tag + "_w1", [128, 8, 4096], BF16)
            w2 = sbt(ph, tag + "_w2", [128, 32, D], BF16)
            load_w_bf16(w1, ffn_w1[layer], 8, "w1")
            load_w_bf16(w2, ffn_w2[layer], 32, "w2")
            xt_r = Rot([sbt(ph, tag + f"_xt{i}", [128, D], F32) for i in range(2)])
            hr_r = Rot([sbt(ph, tag + f"_hr{i}", [128, D], F32) for i in range(1)])
            stA_r = Rot([sbt(ph, tag + f"_stA{i}", [128, 4], F32) for i in range(4)])
            stC_r = Rot([sbt(ph, tag + f"_stC{i}", [128, 8], F32) for i in range(2)])
            xn_r = Rot([sbt(ph, tag + f"_xn{i}", [128, D], BF16) for i in range(2)])
            xnT_r = Rot([sbt(ph, tag + f"_xnT{i}", [128, 8, TS], BF16) for i in range(2)])
            hT_r = Rot([sbt(ph, tag + f"_hT{i}", [128, 32, TS], BF16) for i in range(2)])
            rl_r = Rot([sbt(ph, tag + f"_rl{i}", [128, TS], F32) for i in range(3)])
            out_r = Rot([sbt(ph, tag + f"_out{i}", [128, D], F32) for i in range(2)])
            tp = pst(ph, tag + "_tp", [128, 1024], BF16)
            pp = Rot([pst(ph, tag + f"_pp{i}", [128, 512]) for i in range(7)])

            class Cx:
                pass

            def S1a(sg):
                c = Cx()
                c.sg = sg
                c.xnT = xnT_r.get()
                c.xns = []
                for tl in range(TPG):
                    n = sg * TPG + tl
                    rs = slice(n * 128, (n + 1) * 128)
                    xt = xt_r.get()
                    stt = stA_r.get()
                    xn = xn_r.get()
                    c.xns.append(xn)
                    P.dma(SP, lambda e: e.dma_start(out=xt[:], in_=Hin[rs, :]), reads=[Hin], writes=[xt])
                    r = rms_rstd(stt, 0, xt[:], xt, xn, D)
                    P.op(DVE, lambda e: e.scalar_tensor_tensor(out=xn[:], in0=xt[:], scalar=r, in1=gpre[:], op0=ALU.mult, op1=ALU.mult),
                         reads=[xt, stt, gpre], writes=[xn])
                return c

            def S1b(c):
                for tl in range(TPG):
                    transpose_to(tp, c.xns[tl], 8, lambda: c.xnT[:, :, tl * 128:(tl + 1) * 128], c.xnT)

            def S2(c):
                c.hT = hT_r.get()
                for fc in range(32):
                    ps = pp.get()
                    for kc in range(8):
                        P.op(PE, lambda e: e.matmul(ps[:, 0:TS], lhsT=w1[:, kc, fc * 128:(fc + 1) * 128], rhs=c.xnT[:, kc, :], start=(kc == 0), stop=(kc == 7)),
                             reads=[w1, c.xnT], writes=[ps])
                    rl = rl_r.get()
                    P.op(ACT, lambda e: e.activation(out=rl[:], in_=ps[:, 0:TS], func=AF.Relu), reads=[ps], writes=[rl])
                    P.op(DVE, lambda e: e.tensor_tensor(out=c.hT[:, fc, :], in0=rl[:], in1=rl[:], op=ALU.mult), reads=[rl], writes=[c.hT])

            def S3(c):
                for tl in range(TPG):
                    n = c.sg * TPG + tl
                    rs = slice(n * 128, (n + 1) * 128)
                    ps2 = [pp.get(), pp.get()]
                    for hf in range(2):
                        for fc in range(32):
                            P.op(PE, lambda e: e.matmul(ps2[hf][:], lhsT=c.hT[:, fc, tl * 128:(tl + 1) * 128], rhs=w2[:, fc, hf * 512:(hf + 1) * 512],
                                                        start=(fc == 0), stop=(fc == 31)), reads=[c.hT, w2], writes=[ps2[hf]])
                    outt = out_r.get()
                    norm_residual_store(ph, ps2, gpost, Hin[rs, :], Hin.b, Hout_ap_fn(rs), Hout_tl, stC_r.get(), outt, hr_r.get(), outt)

            cx = {0: S1a(0)}
            S1b(cx[0])
            for i in range(NG + 1):
                if i + 1 < NG:
                    cx[i + 1] = S1a(i + 1)
                if i < NG:
                    S2(cx[i])
                if i + 1 < NG:
                    S1b(cx[i + 1])
                if 0 <= i - 1 < NG:
                    S3(cx.pop(i - 1))
            P.barrier()
            P.emit()

    def phase_E():
        with ExitStack() as ph:
            g10 = load_gain(ph, 1, 0)
            g11 = load_gain(ph, 1, 1)
            wi = sbt(ph, "E_wi", [128, 8, 4096], BF16)
            wo = sbt(ph, "E_wo", [128, 16, D], BF16)
            load_w_bf16(wi, o_w_in, 8, "owi")
            load_w_bf16(wo, o_w_out, 16, "owo")
            U16 = sbt(ph, "E_U16", [128, 128], BF16)
            P.dma(POOL, lambda e: e.dma_start(out=U16[:], in_=cd["c_U"]), writes=[U16])
            wcT = sbt(ph, "E_wcT", [128, 8, 128], BF16)
            bs = sbt(ph, "E_bs", [128, 8], F32)
            P.dma(SP, lambda e: e.dma_start(out=bs[:], in_=o_b_s.rearrange("g t -> t g"), allow_slow_non_contiguous=True), writes=[bs])
            lng = sbt(ph, "E_lng", [128, 2048], F32)
            lnb = sbt(ph, "E_lnb", [128, 2048], F32)
            P.dma(SP, lambda e: e.dma_start(out=lng[:], in_=o_ln_g.partition_broadcast(128)), writes=[lng])
            P.dma(SP, lambda e: e.dma_start(out=lnb[:], in_=o_ln_b.partition_broadcast(128)), writes=[lnb])
            xt_r = Rot([sbt(ph, f"E_xt{i}", [128, D], F32) for i in range(2)])
            hr_r = Rot([sbt(ph, f"E_hr{i}", [128, D], F32) for i in range(1)])
            out_r = Rot([sbt(ph, f"E_out{i}", [128, D], F32) for i in range(1)])
            stA_r = Rot([sbt(ph, f"E_stA{i}", [128, 4], F32) for i in range(2)])
            stB_r = Rot([sbt(ph, f"E_stB{i}", [128, 16], F32) for i in range(3)])
            stC_r = Rot([sbt(ph, f"E_stC{i}", [128, 8], F32) for i in range(2)])
            xn_r = Rot([sbt(ph, f"E_xn{i}", [128, D], BF16) for i in range(2)])
            xnT_r = Rot([sbt(ph, f"E_xnT{i}", [128, 8, 128], BF16) for i in range(2)])
            u_r = Rot([sbt(ph, f"E_u{i}", [128, 2048], F32) for i in range(3)])
            v_r = Rot([sbt(ph, f"E_v{i}", [128, 2048], F32) for i in range(2)])
            vn_r = Rot([sbt(ph, f"E_vn{i}", [128, 2048], BF16) for i in range(2)])
            yb_r = Rot([sbt(ph, f"E_y{i}", [128, 2048], BF16) for i in range(1)])
            yT_r = Rot([sbt(ph, f"E_yT{i}", [128, 16, 128], BF16) for i in range(2)])
            tp = pst(ph, "E_tp", [128, 1024], BF16)
            pp = Rot([pst(ph, f"E_pp{i}", [128, 512]) for i in range(7)])
            ws = yb_r.tiles[0]
            P.dma(POOL, lambda e: e.dma_start(out=ws[:, 0:1024].rearrange("p (g s) -> p g s", g=8), in_=o_w_s.rearrange("g t s -> t g s")), writes=[ws])
            for g in range(8):
                P.op(PE, (lambda g: lambda e: e.transpose(out=tp[:, g * 128:(g + 1) * 128], in_=ws[:, g * 128:(g + 1) * 128], identity=ident[:]))(g),
                     reads=[ws, ident], writes=[tp])
            P.op(DVE, lambda e: e.tensor_tensor(out=wcT[:], in0=tp[:, 0:1024].rearrange("p (g t) -> p g t", g=8),
                                                in1=U16[:].unsqueeze(1).to_broadcast([128, 8, 128]), op=ALU.mult), reads=[tp, U16], writes=[wcT])

            class Cx:
                pass

            def F1a(n):
                c = Cx()
                c.n = n
                c.rs = slice(n * 128, (n + 1) * 128)
                xt = xt_r.get()
                stt = stA_r.get()
                c.xn = xn_r.get()
                c.xnT = xnT_r.get()
                P.dma(SP, lambda e: e.dma_start(out=xt[:], in_=H1[c.rs, :]), reads=[H1], writes=[xt])
                r = rms_rstd(stt, 0, xt[:], xt, c.xn, D)
                P.op(DVE, lambda e: e.scalar_tensor_tensor(out=c.xn[:], in0=xt[:], scalar=r, in1=g10[:], op0=ALU.mult, op1=ALU.mult),
                     reads=[xt, stt, g10], writes=[c.xn])
                return c

            def F1b(c):
                transpose_to(tp, c.xn, 8, lambda: c.xnT[:], c.xnT)

            def F2(c, part):
                if part == 0:
                    c.u = u_r.get()
                    c.v = v_r.get()
                    c.st = stB_r.get()
                    P.op(DVE, lambda e: e.memset(c.st[:], 0.0), writes=[c.st])
                for cg in range(4 * part, 4 * part + 4):
                    ps = pp.get()
                    for kc in range(8):
                        P.op(PE, lambda e: e.matmul(ps[:], lhsT=c.xnT[:, kc, :], rhs=wi[:, kc, cg * 512:(cg + 1) * 512], start=(kc == 0), stop=(kc == 7)),
                             reads=[wi, c.xnT], writes=[ps])
                    if cg < 4:
                        P.op(ACT, lambda e: e.activation(out=c.u[:, cg * 512:(cg + 1) * 512], in_=ps[:], func=AF.Gelu_apprx_tanh), reads=[ps], writes=[c.u])
                    else:
                        c2 = cg - 4
                        P.op(ACT, lambda e: e.activation(out=c.v[:, c2 * 512:(c2 + 1) * 512], in_=ps[:], func=AF.Gelu_apprx_tanh,
                                                         accum_out=c.st[:, 4 + c2:5 + c2]), reads=[ps], writes=[c.v, c.st])

            def L(c):
                stt, v32 = c.st, c.v
                c.vn = vn_r.get()
                vn = c.vn
                P.op(ACT, lambda e: e.activation(out=vn[:], in_=v32[:], func=AF.Square, accum_out=stt[:, 8:9]), reads=[v32], writes=[vn, stt])
                P.op(DVE, lambda e: e.tensor_tensor(out=stt[:, 9:11], in0=stt[:, 4:6], in1=stt[:, 6:8], op=ALU.add), reads=[stt], writes=[stt])
                P.op(DVE, lambda e: e.tensor_tensor(out=stt[:, 11:12], in0=stt[:, 9:10], in1=stt[:, 10:11], op=ALU.add), reads=[stt], writes=[stt])
                P.op(DVE, lambda e: e.tensor_scalar(out=stt[:, 12:13], in0=stt[:, 11:12], scalar1=1.0 / 2048, scalar2=None, op0=ALU.mult), reads=[stt], writes=[stt])
                P.op(DVE, lambda e: e.tensor_tensor(out=stt[:, 13:14], in0=stt[:, 12:13], in1=stt[:, 12:13], op=ALU.mult), reads=[stt], writes=[stt])
                P.op(DVE, lambda e: e.scalar_tensor_tensor(out=stt[:, 14:15], in0=stt[:, 8:9], scalar=1.0 / 2048, in1=stt[:, 13:14],
                                                           op0=ALU.mult, op1=ALU.subtract), reads=[stt], writes=[stt])
                P.op(ACT, lambda e: e.activation(out=stt[:, 15:16], in_=stt[:, 14:15], func=AF.Sqrt, bias=EPS), reads=[stt], writes=[stt])
                P.op(DVE, lambda e: e.reciprocal(out=stt[:, 15:16], in_=stt[:, 15:16]), reads=[stt], writes=[stt])
                P.op(DVE, lambda e: e.scalar_tensor_tensor(out=v32[:], in0=v32[:], scalar=stt[:, 12:13], in1=lng[:], op0=ALU.subtract, op1=ALU.mult),
                     reads=[v32, stt, lng], writes=[v32])
                P.op(DVE, lambda e: e.scalar_tensor_tensor(out=vn[:], in0=v32[:], scalar=stt[:, 15:16], in1=lnb[:], op0=ALU.mult, op1=ALU.add),
                     reads=[v32, stt, lnb], writes=[vn])

            def M1a(c):
                c.yb = yb_r.get()
                c.yT = yT_r.get()
                yb = c.yb
                for g2 in range(4):
                    ps = pp.get()
                    for gg in range(2):
                        g = g2 * 2 + gg
                        P.op(PE, lambda e: e.matmul(ps[:, gg * 256:(gg + 1) * 256], lhsT=wcT[:, g, :], rhs=c.vn[:, g * 256:(g + 1) * 256], start=True, stop=True),
                             reads=[wcT, c.vn], writes=[ps])
                    for gg in range(2):
                        g = g2 * 2 + gg
                        P.op(DVE, lambda e: e.scalar_tensor_tensor(out=yb[:, g * 256:(g + 1) * 256], in0=ps[:, gg * 256:(gg + 1) * 256], scalar=bs[:, g:g + 1],
                                                                   in1=c.u[:, g * 256:(g + 1) * 256], op0=ALU.add, op1=ALU.mult),
                             reads=[ps, bs, c.u], writes=[yb])

            def M1b(c):
                yb = c.yb
                for half in range(2):
                    for cc8 in range(8):
                        cc = half * 8 + cc8
                        P.op(PE, lambda e: e.transpose(out=tp[:, cc8 * 128:(cc8 + 1) * 128], in_=yb[:, cc * 128:(cc + 1) * 128], identity=ident[:]),
                             reads=[yb, ident], writes=[tp])
                    P.op(ACT, lambda e: e.copy(out=c.yT[:, half * 8:(half + 1) * 8, :], in_=tp[:, 0:1024].rearrange("p (c t) -> p c t", c=8)),
                         reads=[tp], writes=[c.yT])

            def M2a(c):
                c.ps2 = [pp.get(), pp.get()]
                for hf in range(2):
                    for cc in range(16):
                        P.op(PE, lambda e: e.matmul(c.ps2[hf][:], lhsT=c.yT[:, cc, :], rhs=wo[:, cc, hf * 512:(hf + 1) * 512], start=(cc == 0), stop=(cc == 15)),
                             reads=[c.yT, wo], writes=[c.ps2[hf]])

            def M2b(c):
                outt = out_r.get()
                norm_residual_store(ph, c.ps2, g11, H1[c.rs, :], H1.b, H2[c.rs, :], H2, stC_r.get(), outt, hr_r.get(), outt)

            cx = {}
            cx[0] = F1a(0)
            F1b(cx[0])
            F2(cx[0], 0)
            F2(cx[0], 1)
            if NT > 1:
                cx[1] = F1a(1)
                F1b(cx[1])
            for i in range(NT + 2):
                if i < NT:
                    L(cx[i])
                if i + 2 < NT:
                    cx[i + 2] = F1a(i + 2)
                if 0 <= i - 2 < NT:
                    M2a(cx[i - 2])
                    M2b(cx[i - 2])
                if 0 <= i - 1 < NT:
                    M1a(cx[i - 1])
                if i + 1 < NT:
                    F2(cx[i + 1], 0)
                if 0 <= i - 1 < NT:
                    M1b(cx[i - 1])
                if i + 1 < NT:
                    F2(cx[i + 1], 1)
                if i + 2 < NT:
                    F1b(cx[i + 2])
                cx.pop(i - 2, None)
            P.barrier()
            P.emit()

    if "A" in phases:
        phase_A()
    if "B" in phases:
        phase_BC()
    if "D" in phases:
        phase_D()
    if "F" in phases:
        phase_FFN(0, H0, lambda rs: H1[rs, :], H1, "F0")
    if "E" in phases:
        phase_E()
    if "G" in phases:
        ytl = Tl(y_d, "y")
        phase_FFN(1, H2, lambda rs: y_d[rs, :], ytl, "F1")
    P.barrier()
    P.emit()
    top.close()
    return nc, P


WEIGHT_NAMES = ["norm_g", "ffn_w1", "ffn_w2", "e_w_in", "e_w_out", "gla_w_gate", "gla_b_gate", "gla_norm", "nsa_gate_b", "nsa_cmp_pos",
                "nsa_cmp_w1", "nsa_cmp_w2", "o_w_in", "o_ln_g", "o_ln_b", "o_w_s", "o_b_s", "o_w_out"]


def prep_weights(inputs):
    m = {}
    for k in WEIGHT_NAMES:
        a = np.asarray(inputs[k], dtype=np.float32)
        if k in ("ffn_w1", "ffn_w2", "norm_g"):
            m[k] = np.ascontiguousarray(a)
        elif k == "gla_norm":
            m[k] = np.ascontiguousarray(a[0].reshape(512))
        else:
            m[k] = np.ascontiguousarray(a[0])
    return m


def kernel(**inputs):
    x = np.asarray(inputs["x"], dtype=np.float32)
    B, S, _ = x.shape
    nc, _ = build(S)
    wm = prep_weights(inputs)
    wm.update(make_consts(S))
    in_maps = [dict(wm, x=np.ascontiguousarray(x[b])) for b in range(B)]
    res = run_bass_kernel_spmd(nc, in_maps, core_ids=list(range(B)))
    return np.stack([r["y"] for r in res.results], axis=0).astype(np.float32)
```

```python
import math
from contextlib import ExitStack

import numpy as np
import concourse.bass as bass
import concourse.mybir as mybir
from concourse.bass_utils import run_bass_kernel_spmd

F32 = mybir.dt.float32
BF16 = mybir.dt.bfloat16
AF = mybir.ActivationFunctionType
ALU = mybir.AluOpType

PE, ACT, DVE, POOL, SP = "tensor", "scalar", "vector", "gpsimd", "sync"
COMPUTE = (PE, ACT, DVE, POOL)
ALLENG = (PE, ACT, DVE, POOL, SP)

D = 1024
NEG = -30000.0
EPS = 1e-6


class Buf:
    __slots__ = ("name", "last_w", "reads")

    def __init__(self, name):
        self.name = name
        self.last_w = None
        self.reads = {}


class Tl:
    __slots__ = ("t", "b")

    def __init__(self, t, name):
        self.t = t
        self.b = Buf(name)

    def __getitem__(self, k):
        return self.t[k]


class _Rec:
    def __init__(self):
        self.call = None

    def __getattr__(self, name):
        def f(*a, **k):
            self.call = (name, a, k)
        return f


def _record(fn):
    r = _Rec()
    fn(r)
    assert r.call is not None
    return r.call


class Prog:
    def __init__(self, nc, n_dma_sems=8):
        self.nc = nc
        self.ops = {e: [] for e in ALLENG}
        self.count = {e: 0 for e in COMPUTE}
        self.waited = {e: {} for e in ALLENG}
        self.n_dma_sems = n_dma_sems
        self.dma_issued = {}
        self.dma_rr = {e: 0 for e in (SP, POOL, ACT)}
        self.sems = {}
        self.sem_stack = ExitStack()
        self.ninstr = 0

    def _need(self, eng, ev, waits, strict=False):
        if ev is None:
            return
        key, val = ev
        if not strict:
            if key == eng and eng == PE:
                return
            if key == eng and val <= self.count[eng] - 3:
                return
        if self.waited[eng].get(key, 0) >= val:
            return
        self.waited[eng][key] = val
        waits.append((key, val))

    def _deps(self, eng, reads, writes, waits, skip_same_war):
        strict = not skip_same_war
        for b in reads:
            self._need(eng, b.last_w, waits, strict)
        for b in writes:
            self._need(eng, b.last_w, waits, strict)
            for k, v in b.reads.items():
                if skip_same_war and k == eng:
                    continue
                self._need(eng, (k, v), waits, strict)

    def _mark(self, ev, reads, writes):
        for b in reads:
            b.reads[ev[0]] = ev[1]
        for b in writes:
            b.last_w = ev
            b.reads = {}

    def op(self, eng, fn, reads=(), writes=()):
        reads = [r.b if isinstance(r, Tl) else r for r in reads]
        writes = [w.b if isinstance(w, Tl) else w for w in writes]
        waits = []
        self._deps(eng, reads, writes, waits, True)
        self.count[eng] += 1
        ev = (eng, self.count[eng])
        self._mark(ev, reads, writes)
        self.ops[eng].append((waits, _record(fn), (eng, 1)))
        self.ninstr += 1
        return ev

    def dma(self, eng, fn, reads=(), writes=()):
        reads = [r.b if isinstance(r, Tl) else r for r in reads]
        writes = [w.b if isinstance(w, Tl) else w for w in writes]
        slot = self.dma_rr[eng] % (48 if eng == POOL else self.n_dma_sems)
        self.dma_rr[eng] += 1
        key = ("dma", eng, slot)
        issued = self.dma_issued.get(key, 0)
        waits = []
        if issued:
            self._need(eng, (key, 16 * issued), waits, True)
        self._deps(eng, reads, writes, waits, False)
        self.dma_issued[key] = issued + 1
        ev = (key, 16 * (issued + 1))
        self._mark(ev, reads, writes)
        self.ops[eng].append((waits, _record(fn), (key, 16)))
        self.ninstr += 1
        return ev

    def barrier(self):
        evs = [(e, self.count[e]) for e in COMPUTE if self.count[e]]
        evs += [(k, 16 * n) for k, n in self.dma_issued.items()]
        for eng in ALLENG:
            waits = []
            for key, val in evs:
                if key == eng:
                    continue
                if self.waited[eng].get(key, 0) >= val:
                    continue
                self.waited[eng][key] = val
                waits.append((key, val))
            if waits:
                self.ops[eng].append((waits, None, None))

    def emit(self):
        nc = self.nc
        keys = set()
        for e in self.ops:
            for waits, fn, inc in self.ops[e]:
                for k, _ in waits:
                    keys.add(k)
                if inc is not None:
                    keys.add(inc[0])
        for k in sorted(keys, key=str):
            if k in self.sems:
                continue
            nm = "s_" + "_".join(str(x) for x in (k if isinstance(k, tuple) else (k,)))
            self.sems[k] = self.sem_stack.enter_context(nc.semaphore(nm))
        sems = self.sems
        with nc.Block() as block:
            def run(engname):
                def body(eng):
                    for waits, fn, inc in self.ops[engname]:
                        for k, v in waits:
                            eng.wait_ge(sems[k], v)
                        if fn is not None:
                            getattr(eng, fn[0])(*fn[1], **fn[2]).then_inc(sems[inc[0]], inc[1])
                return body

            if self.ops[SP]:
                block.sync(run(SP))
            if self.ops[PE]:
                block.tensor(run(PE))
            if self.ops[ACT]:
                block.scalar(run(ACT))
            if self.ops[DVE]:
                block.vector(run(DVE))
            if self.ops[POOL]:
                block.gpsimd(run(POOL))
        for e in self.ops:
            self.ops[e] = []


class Rot:
    def __init__(self, tiles):
        self.tiles = tiles
        self.i = 0

    def get(self):
        t = self.tiles[self.i % len(self.tiles)]
        self.i += 1
        return t


def make_consts(S):
    NSEL = S // 64
    NCMP = S // 16 - 1
    NCC = (NCMP + 127) // 128
    p = np.arange(128)
    c = {}
    c["c_ident"] = np.eye(128, dtype=np.float32)
    U = (p[:, None] <= p[None, :]).astype(np.float32)
    c["c_U"] = U
    c["c_Ls"] = (p[:, None] > p[None, :]).astype(np.float32)
    c["c_ones"] = np.ones((128, 128), np.float32)
    c["c_U4"] = np.tile(U, (1, 4))
    c["c_CB4"] = np.tile(np.where(p[:, None] <= p[None, :], 0.0, NEG).astype(np.float32), (1, 4))
    c["c_WB4"] = np.tile(np.where(p[:, None] > p[None, :], 0.0, NEG).astype(np.float32), (1, 4))
    u = np.arange(S)
    c["c_G"] = np.where(16 * p[:, None] + 31 <= u[None, :], 0.0, NEG).astype(np.float32)
    jb = np.arange(NSEL)
    r64 = np.arange(64)
    c["c_Zaug"] = ((u[None, :] // 64) % 64 == r64[:, None]).astype(np.float32)
    ov = np.zeros((128, NCC, NSEL), np.float32)
    for cc in range(NCC):
        cidx = cc * 128 + p
        m = (cidx[:, None] >= 4 * jb[None, :] - 1) & (cidx[:, None] <= 4 * jb[None, :] + 3) & (cidx[:, None] < NCMP)
        ov[:, cc, :] = m
    c["c_OV"] = ov
    w = np.arange(2 * NSEL)
    jr = w[None, :] - NSEL
    h = (p[:, None] >= 64).astype(np.int64)
    TA = (jr <= h - 2).astype(np.float32)
    TB = np.where(jr == h, 1000.0, np.where(jr == h - 1, 1001.0, np.where(jr <= h - 2, 0.0, -1000.0 - w[None, :])))
    c["c_TA"] = TA
    c["c_TB"] = TB.astype(np.float32)
    return c


OFF = {}
_o = 0
for _n, _sz in (("gq", 256), ("gk", 256), ("gv", 512), ("glr", 16), ("gr", 512), ("nq", 512), ("kc", 128), ("vc", 128),
                ("ks", 128), ("vs", 128), ("kw", 128), ("vw", 128), ("ng", 24)):
    OFF[_n] = _o
    _o += _sz
WIN = _o


def build(S, debug=False, phases="ABCDEFG"):
    NT = S // 128
    NST = S // 512
    NSEL = S // 64
    NCMP = S // 16 - 1
    NCC = (NCMP + 127) // 128
    nc = bass.Bass("TRN2", target_bir_lowering=False)
    P = Prog(nc)

    def din(name, shape):
        return nc.dram_tensor(name, list(shape), F32, kind="ExternalInput").ap()

    x_d = din("x", [S, D])
    norm_g = din("norm_g", [2, 4, D])
    ffn_w1 = din("ffn_w1", [2, D, 4096])
    ffn_w2 = din("ffn_w2", [2, 4096, D])
    e_w_in = din("e_w_in", [D, WIN])
    e_w_out = din("e_w_out", [D, D])
    gla_w_gate = din("gla_w_gate", [16, 256])
    gla_b_gate = din("gla_b_gate", [256])
    gla_norm = din("gla_norm", [512])
    nsa_gate_b = din("nsa_gate_b", [24])
    nsa_cmp_pos = din("nsa_cmp_pos", [2, 32, 64])
    nsa_cmp_w1 = din("nsa_cmp_w1", [2, 2048, 128])
    nsa_cmp_w2 = din("nsa_cmp_w2", [2, 128, 64])
    o_w_in = din("o_w_in", [D, 4096])
    o_ln_g = din("o_ln_g", [2048])
    o_ln_b = din("o_ln_b", [2048])
    o_w_s = din("o_w_s", [8, 128, 128])
    o_b_s = din("o_b_s", [8, 128])
    o_w_out = din("o_w_out", [2048, D])
    cshapes = {k: v.shape for k, v in make_consts(S).items()}
    cd = {k: din(k, shp) for k, shp in cshapes.items()}

    y_d = nc.dram_tensor("y", [S, D], F32, kind="ExternalOutput").ap()

    def scratch(name, shape, dt):
        kind = "ExternalOutput" if debug else "Internal"
        t = Tl(None, name)
        t.t = nc.dram_tensor(name, list(shape), dt, kind=kind).ap()
        return t

    QT = scratch("QT", [8, 64, S], BF16)
    KT = scratch("KT", [8, 64, S], BF16)
    VT = scratch("VT", [S, 256], BF16)
    GT = scratch("GT", [S, 24], F32)
    OA = scratch("OA", [S, 512], BF16)
    OB = scratch("OB", [S, 512], BF16)
    H0 = scratch("H0", [S, D], F32)
    H1 = scratch("H1", [S, D], F32)
    H2 = scratch("H2", [S, D], F32)
    Bx = Buf("x")
    By = Buf("y")

    top = ExitStack()

    def sbt(st, name, shape, dt):
        return Tl(st.enter_context(nc.sbuf_tensor(name, list(shape), dt)), name)

    def pst(st, name, shape, dt=F32):
        return Tl(st.enter_context(nc.psum_tensor(name, list(shape), dt)), name)

    ident = sbt(top, "ident", [128, 128], BF16)
    P.dma(POOL, lambda e: e.dma_start(out=ident[:], in_=cd["c_ident"]), writes=[ident])

    def load_gain(st, l, j):
        t = sbt(st, f"gain{l}{j}", [128, D], F32)
        P.dma(SP, lambda e: e.dma_start(out=t[:], in_=norm_g[l, j].partition_broadcast(128)), writes=[t])
        return t

    def rms_rstd(st_tile, col, src_ap, src_tl, junk, n, lnexp=False):
        if lnexp:
            P.op(DVE, lambda e: e.memset(st_tile[:], 0.0), writes=[st_tile])
            P.op(ACT, lambda e: e.activation(out=junk[:, 0:n], in_=src_ap, func=AF.Square, accum_out=st_tile[:, col:col + 1]),
                 reads=[src_tl], writes=[junk, st_tile])
            P.op(DVE, lambda e: e.tensor_scalar(out=st_tile[:, col + 1:col + 2], in0=st_tile[:, col:col + 1], scalar1=1.0 / n, scalar2=EPS,
                                                op0=ALU.mult, op1=ALU.add), reads=[st_tile], writes=[st_tile])
            P.op(ACT, lambda e: e.activation(out=st_tile[:, col + 1:col + 2], in_=st_tile[:, col + 1:col + 2], func=AF.Ln), reads=[st_tile], writes=[st_tile])
            P.op(ACT, lambda e: e.activation(out=st_tile[:, col + 2:col + 3], in_=st_tile[:, col + 1:col + 2], func=AF.Exp, scale=-0.5),
                 reads=[st_tile], writes=[st_tile])
            return st_tile[:, col + 2:col + 3]
        P.op(DVE, lambda e: e.memset(st_tile[:], 0.0), writes=[st_tile])
        P.op(ACT, lambda e: e.activation(out=junk[:, 0:n], in_=src_ap, func=AF.Square, accum_out=st_tile[:, col:col + 1]),
             reads=[src_tl], writes=[junk, st_tile])
        P.op(ACT, lambda e: e.activation(out=st_tile[:, col + 1:col + 2], in_=st_tile[:, col:col + 1], func=AF.Sqrt,
                                         scale=1.0 / n, bias=EPS), reads=[st_tile], writes=[st_tile])
        P.op(DVE, lambda e: e.reciprocal(out=st_tile[:, col + 2:col + 3], in_=st_tile[:, col + 1:col + 2]),
             reads=[st_tile], writes=[st_tile])
        return st_tile[:, col + 2:col + 3]

    def transpose_to(tp, src, nchunk, dst_ap_fn, dst_tl):
        for c in range(nchunk):
            P.op(PE, (lambda c: lambda e: e.transpose(out=tp[:, c * 128:(c + 1) * 128], in_=src[:, c * 128:(c + 1) * 128],
                                                      identity=ident[:]))(c), reads=[src, ident], writes=[tp])
        P.op(ACT, lambda e: e.copy(out=dst_ap_fn(), in_=tp[:, 0:nchunk * 128].rearrange("p (c t) -> p c t", c=nchunk)),
             reads=[tp], writes=[dst_tl])

    def load_w_bf16(dst, src_ap, nk, name):
        v = src_ap.rearrange("(k p) n -> p k n", p=128)
        for k in range(0, nk, 8):
            k1 = min(nk, k + 8)
            P.dma(POOL, (lambda k, k1: lambda e: e.dma_start(out=dst[:, k:k1, :], in_=v[:, k:k1, :]))(k, k1), writes=[dst])

    def norm_residual_store(ph, m_ps2, gain, hin_ap, hin_buf, hout_ap, hout_tl, stt, junk, hres, outt):
        P.op(DVE, lambda e: e.memset(stt[:], 0.0), writes=[stt])
        for hf in range(2):
            P.op(ACT, (lambda hf: lambda e: e.activation(out=junk[:, 0:512], in_=m_ps2[hf][:], func=AF.Square,
                                                         accum_out=stt[:, hf:hf + 1]))(hf),
                 reads=[m_ps2[hf]], writes=[junk, stt])
        P.op(DVE, lambda e: e.tensor_tensor(out=stt[:, 2:3], in0=stt[:, 0:1], in1=stt[:, 1:2], op=ALU.add), reads=[stt], writes=[stt])
        P.op(ACT, lambda e: e.activation(out=stt[:, 3:4], in_=stt[:, 2:3], func=AF.Sqrt, scale=1.0 / D, bias=EPS),
             reads=[stt], writes=[stt])
        P.op(DVE, lambda e: e.reciprocal(out=stt[:, 4:5], in_=stt[:, 3:4]), reads=[stt], writes=[stt])
        P.dma(SP, lambda e: e.dma_start(out=hres[:], in_=hin_ap), reads=[hin_buf], writes=[hres])
        for hf in range(2):
            P.op(DVE, (lambda hf: lambda e: e.scalar_tensor_tensor(out=outt[:, hf * 512:(hf + 1) * 512], in0=m_ps2[hf][:],
                                                                   scalar=stt[:, 4:5], in1=gain[:, hf * 512:(hf + 1) * 512],
                                                                   op0=ALU.mult, op1=ALU.mult))(hf),
                 reads=[m_ps2[hf], stt, gain], writes=[outt])
        P.op(POOL, lambda e: e.tensor_tensor(out=outt[:], in0=outt[:], in1=hres[:], op=ALU.add), reads=[outt, hres], writes=[outt])
        P.dma(POOL, lambda e: e.dma_start(out=hout_ap, in_=outt[:]), reads=[outt], writes=[hout_tl])

    def phase_A():
        with ExitStack() as ph:
            g00 = load_gain(ph, 0, 0)
            w = sbt(ph, "A_w", [128, 8, WIN], BF16)
            load_w_bf16(w, e_w_in, 8, "w_in")
            wg = sbt(ph, "A_wg", [17, 256], BF16)
            P.dma(POOL, lambda e: e.dma_start(out=wg[0:16, :], in_=gla_w_gate), writes=[wg])
            P.dma(POOL, lambda e: e.dma_start(out=wg[16:17, :], in_=gla_b_gate.unsqueeze(0)), writes=[wg])
            U32 = sbt(ph, "A_U32", [128, 128], F32)
            L32 = sbt(ph, "A_L32", [128, 128], F32)
            U4 = sbt(ph, "A_U4", [128, 512], F32)
            P.dma(SP, lambda e: e.dma_start(out=U32[:], in_=cd["c_U"]), writes=[U32])
            P.dma(SP, lambda e: e.dma_start(out=L32[:], in_=cd["c_Ls"]), writes=[L32])
            P.dma(SP, lambda e: e.dma_start(out=U4[:], in_=cd["c_U4"]), writes=[U4])
            gnb = sbt(ph, "A_gnb", [128, 512], F32)
            P.dma(SP, lambda e: e.dma_start(out=gnb[:], in_=gla_norm.partition_broadcast(128)), writes=[gnb])
            ngb = sbt(ph, "A_ngb", [128, 24], F32)
            P.dma(SP, lambda e: e.dma_start(out=ngb[:], in_=nsa_gate_b.partition_broadcast(128)), writes=[ngb])

            xt_r = Rot([sbt(ph, f"A_xt{i}", [128, D], F32) for i in range(2)])
            stA_r = Rot([sbt(ph, f"A_stA{i}", [128, 4], F32) for i in range(4)])
            stG_r = Rot([sbt(ph, f"A_stG{i}", [128, 16], F32) for i in range(2)])
            xn_r = Rot([sbt(ph, f"A_xn{i}", [128, D], BF16) for i in range(3)])
            xnT_r = Rot([sbt(ph, f"A_xnT{i}", [128, 8, 512], BF16) for i in range(2)])
            qT = sbt(ph, "A_qT", [64, 4, 512], F32)
            kT = sbt(ph, "A_kT", [64, 4, 512], F32)
            glrT = sbt(ph, "A_glrT", [32, 512], BF16)
            P.op(DVE, lambda e: e.memset(glrT[:], 1.0), writes=[glrT])
            nq_st = Rot([sbt(ph, f"A_nq{i}", [128, 4, 512], BF16) for i in range(2)])
            kt_st = Rot([sbt(ph, f"A_kt{i}", [128, 4, 512], BF16) for i in range(2)])
            ktok_r = Rot([sbt(ph, f"A_ktok{i}", [128, 256], F32) for i in range(2)])
            v_r = Rot([sbt(ph, f"A_v{i}", [128, 512], BF16) for i in range(3)])
            sr_r = Rot([sbt(ph, f"A_sr{i}", [128, 512], F32) for i in range(3)])
            vt_r = Rot([sbt(ph, f"A_vt{i}", [128, 256], BF16) for i in range(2)])
            gt_r = Rot([sbt(ph, f"A_gt{i}", [128, 24], F32) for i in range(2)])
            ex_r = Rot([sbt(ph, f"A_ex{i}", [128, 256], F32) for i in range(2)])
            sp_r = Rot([sbt(ph, f"A_sp{i}", [128, 256], F32) for i in range(2)])
            eb_r = Rot([sbt(ph, f"A_eb{i}", [128, 256], F32) for i in range(2)])
            kend_r = Rot([sbt(ph, f"A_kend{i}", [128, 256], BF16) for i in range(2)])
            eq_r = Rot([sbt(ph, f"A_eq{i}", [64, 4, 128], F32) for i in range(2)])
            ek_r = Rot([sbt(ph, f"A_ek{i}", [64, 4, 128], F32) for i in range(2)])
            qtT_r = Rot([sbt(ph, f"A_qtT{i}", [128, 4, 128], BF16) for i in range(2)])
            ktT_r = Rot([sbt(ph, f"A_ktT{i}", [128, 4, 128], BF16) for i in range(2)])
            for t in qtT_r.tiles + ktT_r.tiles:
                P.op(DVE, (lambda t: lambda e: e.memset(t[:], 0.0))(t), writes=[t])
            AT_r = Rot([sbt(ph, f"A_AT{i}", [128, 512], BF16) for i in range(2)])
            S32 = sbt(ph, "A_S32", [64, 4, 128], F32)
            Sbf = sbt(ph, "A_Sbf", [128, 4, 128], BF16)
            P.op(DVE, lambda e: e.memset(S32[:], 0.0), writes=[S32])
            P.op(DVE, lambda e: e.memset(Sbf[:], 0.0), writes=[Sbf])
            on = sbt(ph, "A_on", [128, 512], F32)
            oa_r = Rot([sbt(ph, f"A_oa{i}", [128, 512], BF16) for i in range(2)])
            tp = pst(ph, "A_tp", [128, 1024], BF16)
            pp = Rot([pst(ph, f"A_pp{i}", [128, 512]) for i in range(7)])

            class Cx:
                pass

            def S1new(st):
                c = Cx()
                c.st = st
                c.xnT = xnT_r.get()
                c.xns = [None] * 4
                return c

            def S1a(c, tl):
                n = c.st * 4 + tl
                xt = xt_r.get()
                stt = stA_r.get()
                xn = xn_r.get()
                c.xns[tl] = xn
                P.dma(SP, lambda e: e.dma_start(out=xt[:], in_=x_d[n * 128:(n + 1) * 128, :]), reads=[Bx], writes=[xt])
                r = rms_rstd(stt, 0, xt[:], xt, xn, D, lnexp=True)
                P.op(DVE, lambda e: e.scalar_tensor_tensor(out=xn[:], in0=xt[:], scalar=r, in1=g00[:], op0=ALU.mult, op1=ALU.mult),
                     reads=[xt, stt, g00], writes=[xn])

            def S1b(c, tl):
                transpose_to(tp, c.xns[tl], 8, lambda: c.xnT[:, :, tl * 128:(tl + 1) * 128], c.xnT)

            def S2(c):
                st, xnT = c.st, c.xnT

                def fm_proj(col0, m):
                    ps = pp.get()
                    for kc in range(8):
                        P.op(PE, lambda e: e.matmul(ps[0:m, :], lhsT=w[:, kc, col0:col0 + m], rhs=xnT[:, kc, :], start=(kc == 0), stop=(kc == 7)),
                             reads=[w, xnT], writes=[ps])
                    return ps

                for h in range(4):
                    ps = fm_proj(OFF["gq"] + 64 * h, 64)
                    P.op(ACT, lambda e: e.mul(out=qT[:, h, :], in_=ps[0:64, :], mul=0.125), reads=[ps], writes=[qT])
                    ps = fm_proj(OFF["gk"] + 64 * h, 64)
                    P.op(DVE, lambda e: e.tensor_copy(out=kT[:, h, :], in_=ps[0:64, :]), reads=[ps], writes=[kT])
                ps = fm_proj(OFF["glr"], 16)
                P.op(DVE, lambda e: e.tensor_copy(out=glrT[0:16, :], in_=ps[0:16, :]), reads=[ps], writes=[glrT])
                nqs = nq_st.get()
                for hp in range(4):
                    ps = fm_proj(OFF["nq"] + 128 * hp, 128)
                    P.op(ACT, lambda e: e.mul(out=nqs[:, hp, :], in_=ps[:, :], mul=0.125), reads=[ps], writes=[nqs])
                P.dma(POOL, lambda e: e.dma_start(out=QT[:, :, st * 512:(st + 1) * 512].rearrange("(hp hh) d s -> (hh d) hp s", hh=2), in_=nqs[:]),
                      reads=[nqs], writes=[QT])
                kts = kt_st.get()
                for i, nm in enumerate(("kc", "vc", "ks", "kw")):
                    ps = fm_proj(OFF[nm], 128)
                    P.op(DVE, lambda e: e.tensor_copy(out=kts[:, i, :], in_=ps[:, :]), reads=[ps], writes=[kts])
                P.dma(POOL, lambda e: e.dma_start(out=KT[:, :, st * 512:(st + 1) * 512].rearrange("(i g) d s -> (g d) i s", g=2), in_=kts[:]),
                      reads=[kts], writes=[KT])

            def S3a(c, tl):
                t = Cx()
                t.n = c.st * 4 + tl
                t.tsl = slice(tl * 128, (tl + 1) * 128)
                t.xnT = c.xnT
                xnT, tsl = c.xnT, t.tsl

                def tm_proj(ps, pc0, col0, ncol):
                    for kc in range(8):
                        P.op(PE, lambda e: e.matmul(ps[:, pc0:pc0 + ncol], lhsT=xnT[:, kc, tsl], rhs=w[:, kc, col0:col0 + ncol], start=(kc == 0), stop=(kc == 7)),
                             reads=[w, xnT], writes=[ps])

                t.tm_proj = tm_proj
                t.ktok = ktok_r.get()
                ps = pp.get()
                tm_proj(ps, 0, OFF["gk"], 256)
                P.op(ACT, lambda e: e.copy(out=t.ktok[:], in_=ps[:, 0:256]), reads=[ps], writes=[t.ktok])
                t.v = v_r.get()
                ps = pp.get()
                tm_proj(ps, 0, OFF["gv"], 512)
                P.op(DVE, lambda e: e.tensor_copy(out=t.v[:], in_=ps[:]), reads=[ps], writes=[t.v])
                return t

            def S3b(t):
                n, tm_proj = t.n, t.tm_proj
                t.sr = sr_r.get()
                sr = t.sr
                ps = pp.get()
                tm_proj(ps, 0, OFF["gr"], 512)
                P.op(ACT, lambda e: e.activation(out=sr[:], in_=ps[:], func=AF.Exp, scale=-1.0), reads=[ps], writes=[sr])
                P.op(ACT, lambda e: e.activation(out=sr[:], in_=sr[:], func=AF.Ln, bias=1.0), reads=[sr], writes=[sr])
                P.op(ACT, lambda e: e.activation(out=sr[:], in_=sr[:], func=AF.Exp, scale=-1.0), reads=[sr], writes=[sr])
                P.op(DVE, lambda e: e.tensor_tensor(out=sr[:], in0=ps[:], in1=sr[:], op=ALU.mult), reads=[ps, sr], writes=[sr])
                ps = pp.get()
                tm_proj(ps, 0, OFF["vs"], 128)
                tm_proj(ps, 128, OFF["vw"], 128)
                tm_proj(ps, 256, OFF["ng"], 24)
                vt = vt_r.get()
                gt = gt_r.get()
                P.op(DVE, lambda e: e.tensor_copy(out=vt[:], in_=ps[:, 0:256]), reads=[ps], writes=[vt])
                P.op(DVE, lambda e: e.tensor_tensor(out=gt[:], in0=ps[:, 256:280], in1=ngb[:], op=ALU.add), reads=[ps, ngb], writes=[gt])
                P.op(ACT, lambda e: e.activation(out=gt[:], in_=gt[:], func=AF.Exp, scale=-1.0), reads=[gt], writes=[gt])
                P.op(DVE, lambda e: e.tensor_scalar(out=gt[:], in0=gt[:], scalar1=1.0, scalar2=None, op0=ALU.add), reads=[gt], writes=[gt])
                P.op(DVE, lambda e: e.reciprocal(out=gt[:], in_=gt[:]), reads=[gt], writes=[gt])
                P.dma(POOL, lambda e: e.dma_start(out=VT[n * 128:(n + 1) * 128, :], in_=vt[:]), reads=[vt], writes=[VT])
                P.dma(POOL, lambda e: e.dma_start(out=GT[n * 128:(n + 1) * 128, :], in_=gt[:]), reads=[gt], writes=[GT])

            def G1a(t):
                tsl = t.tsl
                t.ex, t.sp, t.eb = ex_r.get(), sp_r.get(), eb_r.get()
                t.kend, t.eq, t.ek = kend_r.get(), eq_r.get(), ek_r.get()
                t.qtT, t.ktT, t.AT = qtT_r.get(), ktT_r.get(), AT_r.get()
                ex, sp = t.ex, t.sp
                ps = pp.get()
                P.op(PE, lambda e: e.matmul(ps[:, 0:256], lhsT=glrT[0:17, tsl], rhs=wg[0:17, :], start=True, stop=True), reads=[glrT, wg], writes=[ps])
                P.op(ACT, lambda e: e.activation(out=ex[:], in_=ps[:, 0:256], func=AF.Exp, scale=-1.0), reads=[ps], writes=[ex])
                P.op(ACT, lambda e: e.activation(out=sp[:], in_=ex[:], func=AF.Ln, bias=1.0), reads=[ex], writes=[sp])

            def G1b(t):
                tsl, sp, eb, ek = t.tsl, t.sp, t.eb, t.ek
                ps = pp.get()
                P.op(PE, lambda e: e.matmul(ps[:, 0:256], lhsT=L32[:], rhs=sp[:], start=True, stop=True), reads=[L32, sp], writes=[ps])
                P.op(ACT, lambda e: e.activation(out=eb[:], in_=ps[:, 0:256], func=AF.Exp, scale=-1.0 / 16), reads=[ps], writes=[eb])
                P.op(DVE, lambda e: e.tensor_tensor(out=t.kend[:], in0=t.ktok[:], in1=eb[:], op=ALU.mult), reads=[t.ktok, eb], writes=[t.kend])
                ps = pp.get()
                for h in range(4):
                    P.op(PE, lambda e: e.matmul(ps[0:64, h * 128:(h + 1) * 128], lhsT=sp[:, 64 * h:64 * h + 64], rhs=U32[:], start=True, stop=True),
                         reads=[sp, U32], writes=[ps])
                P.op(ACT, lambda e: e.activation(out=t.eq[:].rearrange("p h t -> p (h t)"), in_=ps[0:64, :], func=AF.Exp, scale=-1.0 / 16), reads=[ps], writes=[t.eq])
                P.op(ACT, lambda e: e.activation(out=ek[:].rearrange("p h t -> p (h t)"), in_=ps[0:64, :], func=AF.Exp, scale=1.0 / 16), reads=[ps], writes=[ek])
                P.op(DVE, lambda e: e.tensor_tensor(out=t.qtT[0:64], in0=qT[:, :, tsl], in1=t.eq[:], op=ALU.mult), reads=[qT, t.eq], writes=[t.qtT])
                P.op(DVE, lambda e: e.tensor_tensor(out=t.ktT[0:64], in0=kT[:, :, tsl], in1=ek[:], op=ALU.mult), reads=[kT, ek], writes=[t.ktT])

            def G1c(t):
                ktT = t.ktT
                ps = pp.get()
                for h in range(4):
                    P.op(PE, lambda e: e.matmul(ps[:, h * 128:(h + 1) * 128], lhsT=ktT[:, h, :], rhs=t.qtT[:, h, :], start=True, stop=True),
                         reads=[ktT, t.qtT], writes=[ps])
                P.op(DVE, lambda e: e.tensor_tensor(out=t.AT[:], in0=ps[:], in1=U4[:], op=ALU.mult), reads=[ps, U4], writes=[t.AT])

            def G2(t):
                n, v, AT, qtT, kend, eq, sr = t.n, t.v, t.AT, t.qtT, t.kend, t.eq, t.sr
                po = pp.get()
                for h in range(4):
                    hs = slice(h * 128, (h + 1) * 128)
                    P.op(PE, lambda e: e.matmul(po[:, hs], lhsT=AT[:, hs], rhs=v[:, hs], start=True, stop=False), reads=[AT, v], writes=[po])
                    P.op(PE, lambda e: e.matmul(po[:, hs], lhsT=qtT[:, h, :], rhs=Sbf[:, h, :], start=False, stop=True), reads=[qtT, Sbf], writes=[po])
                pd = pp.get()
                for h in range(4):
                    hs = slice(h * 128, (h + 1) * 128)
                    P.op(PE, lambda e: e.matmul(pd[0:64, hs], lhsT=kend[:, 64 * h:64 * h + 64], rhs=v[:, hs], start=True, stop=True), reads=[kend, v], writes=[pd])
                for h in range(4):
                    hs = slice(h * 128, (h + 1) * 128)
                    P.op(DVE, lambda e: e.scalar_tensor_tensor(out=S32[:, h, :], in0=S32[:, h, :], scalar=eq[:, h, 127:128], in1=pd[0:64, hs],
                                                               op0=ALU.mult, op1=ALU.add), reads=[S32, eq, pd], writes=[S32])
                P.op(ACT, lambda e: e.copy(out=Sbf[0:64], in_=S32[:]), reads=[S32], writes=[Sbf])
                stt = stG_r.get()
                oa = oa_r.get()
                P.op(DVE, lambda e: e.memset(stt[:], 0.0), writes=[stt])
                for h in range(4):
                    hs = slice(h * 128, (h + 1) * 128)
                    P.op(ACT, lambda e: e.activation(out=oa[:, hs], in_=po[:, hs], func=AF.Square, accum_out=stt[:, h:h + 1]), reads=[po], writes=[oa, stt])
                P.op(DVE, lambda e: e.tensor_scalar(out=stt[:, 4:8], in0=stt[:, 0:4], scalar1=1.0 / 128, scalar2=EPS, op0=ALU.mult, op1=ALU.add),
                     reads=[stt], writes=[stt])
                P.op(ACT, lambda e: e.activation(out=stt[:, 4:8], in_=stt[:, 4:8], func=AF.Ln), reads=[stt], writes=[stt])
                P.op(ACT, lambda e: e.activation(out=stt[:, 8:12], in_=stt[:, 4:8], func=AF.Exp, scale=-0.5), reads=[stt], writes=[stt])
                for h in range(4):
                    hs = slice(h * 128, (h + 1) * 128)
                    P.op(DVE, lambda e: e.scalar_tensor_tensor(out=on[:, hs], in0=po[:, hs], scalar=stt[:, 8 + h:9 + h], in1=gnb[:, hs],
                                                               op0=ALU.mult, op1=ALU.mult), reads=[po, stt, gnb], writes=[on])
                P.op(DVE, lambda e: e.tensor_tensor(out=oa[:], in0=on[:], in1=sr[:], op=ALU.mult), reads=[on, sr], writes=[oa])
                P.dma(POOL, lambda e: e.dma_start(out=OA[n * 128:(n + 1) * 128, :], in_=oa[:]), reads=[oa], writes=[OA])

            cur = S1new(0)
            for tl in range(4):
                S1a(cur, tl)
                S1b(cur, tl)
            pend = None
            for st in range(NST):
                S2(cur)
                nxt = S1new(st + 1) if st + 1 < NST else None
                for tl in range(4):
                    if nxt is not None:
                        S1a(nxt, tl)
                    t = S3a(cur, tl)
                    G1a(t)
                    S3b(t)
                    if nxt is not None:
                        S1b(nxt, tl)
                    G1b(t)
                    if pend is not None:
                        G2(pend)
                    G1c(t)
                    pend = t
                cur = nxt
            G2(pend)
            P.barrier()
            P.emit()

    def phase_BC():
        with ExitStack() as ph:
            ones = sbt(ph, "C_ones", [128, 128], BF16)
            P.dma(POOL, lambda e: e.dma_start(out=ones[:], in_=cd["c_ones"]), writes=[ones])
            KCT = [sbt(ph, f"C_KCT{g}", [128, NCC * 128], BF16) for g in range(2)]
            VC = [sbt(ph, f"C_VC{g}", [128, NCC, 65], BF16) for g in range(2)]
            for g in range(2):
                P.op(DVE, (lambda g: lambda e: e.memset(KCT[g][:], 0.0))(g), writes=[KCT[g]])
                P.op(DVE, (lambda g: lambda e: e.memset(VC[g][:], 1.0))(g), writes=[VC[g]])
            with ExitStack() as pb:
                w1 = sbt(pb, "B_w1", [128, 2, 16, 128], BF16)
                w2 = sbt(pb, "B_w2", [128, 2, 64], BF16)
                posT = sbt(pb, "B_posT", [128, 2, 16], BF16)
                for i in range(2):
                    P.dma(POOL, (lambda i: lambda e: e.dma_start(out=w1[:, i, :, :], in_=nsa_cmp_w1[i].rearrange("(l q) h -> q l h", q=128)))(i),
                          writes=[w1])
                    P.dma(POOL, (lambda i: lambda e: e.dma_start(out=w2[:, i, :], in_=nsa_cmp_w2[i]))(i), writes=[w2])
                    P.dma(POOL, (lambda i: lambda e: e.dma_start(out=posT[:, i, :], in_=nsa_cmp_pos[i].rearrange("(l two) d -> (two d) l", two=2),
                                                                 allow_slow_non_contiguous=True))(i), writes=[posT])
                xT_r = Rot([sbt(pb, f"B_xT{i}", [128, S], BF16) for i in range(2)])
                for t in xT_r.tiles:
                    P.op(POOL, (lambda t: lambda e: e.memset(t[:], 0.0))(t), writes=[t])
                bias = sbt(pb, "B_bias", [128, 2], F32)
                gh_r = Rot([sbt(pb, f"B_gh{i}", [128, NCC * 128], BF16) for i in range(2)])
                for t in gh_r.tiles:
                    P.op(DVE, (lambda t: lambda e: e.memset(t[:], 0.0))(t), writes=[t])
                pp = Rot([pst(pb, f"B_pp{i}", [128, 512]) for i in range(4)])
                for i in range(2):
                    ps = pp.get()
                    for l in range(16):
                        P.op(PE, (lambda i, l, ps: lambda e: e.matmul(ps[:, 0:1], lhsT=w1[:, i, l, :], rhs=posT[:, i, l:l + 1],
                                                                      start=(l == 0), stop=(l == 15)))(i, l, ps), reads=[w1, posT], writes=[ps])
                    P.op(DVE, (lambda i, ps: lambda e: e.tensor_copy(out=bias[:, i:i + 1], in_=ps[:, 0:1]))(i, ps), reads=[ps], writes=[bias])
                for i in range(2):
                    for g in range(2):
                        xT = xT_r.get()
                        P.dma(SP, (lambda i, g, xT: lambda e: e.dma_start(out=xT[0:64, :], in_=KT[i * 2 + g, :, :]))(i, g, xT), reads=[KT], writes=[xT])
                        P.dma(SP, (lambda i, g, xT: lambda e: e.dma_start(out=xT[64:128, 0:S - 1], in_=KT[i * 2 + g, :, 1:S]))(i, g, xT), reads=[KT], writes=[xT])
                        gh = gh_r.get()
                        for c0 in range(0, NCMP, 512):
                            ncol = min(512, NCMP - c0)
                            ps = pp.get()
                            for l in range(16):
                                a0 = c0 * 16 + 2 * l
                                P.op(PE, (lambda i, l, ps, a0, ncol, xT: lambda e: e.matmul(
                                    ps[:, 0:ncol], lhsT=w1[:, i, l, :], rhs=xT[:, a0:a0 + 16 * (ncol - 1) + 1:16],
                                    start=(l == 0), stop=(l == 15)))(i, l, ps, a0, ncol, xT), reads=[w1, xT], writes=[ps])
                            P.op(ACT, (lambda i, ps, c0, ncol, gh: lambda e: e.activation(out=gh[:, c0:c0 + ncol], in_=ps[:, 0:ncol],
                                                                                          func=AF.Gelu_apprx_tanh, bias=bias[:, i:i + 1]))(i, ps, c0, ncol, gh),
                                 reads=[ps, bias], writes=[gh])
                        if i == 0:
                            for c0 in range(0, NCMP, 512):
                                ncol = min(512, NCMP - c0)
                                ps = pp.get()
                                P.op(PE, (lambda ps, c0, ncol, gh: lambda e: e.matmul(ps[0:64, 0:ncol], lhsT=w2[:, 0, :], rhs=gh[:, c0:c0 + ncol],
                                                                                      start=True, stop=True))(ps, c0, ncol, gh), reads=[w2, gh], writes=[ps])
                                P.op(DVE, (lambda g, ps, c0, ncol: lambda e: e.tensor_copy(out=KCT[g][0:64, c0:c0 + ncol], in_=ps[0:64, 0:ncol]))(g, ps, c0, ncol),
                                     reads=[ps], writes=[KCT[g]])
                        else:
                            ps = pp.get()
                            for cc in range(NCC):
                                P.op(PE, (lambda ps, cc, gh: lambda e: e.matmul(ps[:, cc * 64:(cc + 1) * 64], lhsT=gh[:, cc * 128:(cc + 1) * 128], rhs=w2[:, 1, :],
                                                                                start=True, stop=True))(ps, cc, gh), reads=[w2, gh], writes=[ps])
                            P.op(DVE, (lambda g, ps: lambda e: e.tensor_copy(out=VC[g][:, :, 0:64],
                                                                             in_=ps[:, 0:NCC * 64].rearrange("p (c d) -> p c d", c=NCC)))(g, ps),
                                 reads=[ps], writes=[VC[g]])
                P.barrier()
                P.emit()
            Gb = sbt(ph, "C_G", [128, S], BF16)
            OV = sbt(ph, "C_OV", [128, NCC, NSEL], BF16)
            CB4 = sbt(ph, "C_CB4", [128, 512], BF16)
            WB4 = sbt(ph, "C_WB4", [128, 512], BF16)
            TA = sbt(ph, "C_TA", [128, 2 * NSEL], F32)
            TB = sbt(ph, "C_TB", [128, 2 * NSEL], F32)
            P.dma(POOL, lambda e: e.dma_start(out=Gb[:], in_=cd["c_G"]), writes=[Gb])
            P.dma(POOL, lambda e: e.dma_start(out=OV[:], in_=cd["c_OV"]), writes=[OV])
            P.dma(POOL, lambda e: e.dma_start(out=CB4[:], in_=cd["c_CB4"]), writes=[CB4])
            P.dma(POOL, lambda e: e.dma_start(out=WB4[:], in_=cd["c_WB4"]), writes=[WB4])
            P.dma(SP, lambda e: e.dma_start(out=TA[:], in_=cd["c_TA"]), writes=[TA])
            P.dma(SP, lambda e: e.dma_start(out=TB[:], in_=cd["c_TB"]), writes=[TB])
            KS = sbt(ph, "C_KS", [128, S], BF16)
            KW = sbt(ph, "C_KW", [128, S], BF16)
            P.op(POOL, lambda e: e.memset(KW[:], 0.0), writes=[KW])
            P.dma(POOL, lambda e: e.dma_start(out=KS[64:128, :], in_=cd["c_Zaug"]), writes=[KS])
            VS = sbt(ph, "C_VS", [128, NT, 65], BF16)
            VW = sbt(ph, "C_VW", [128, NT, 65], BF16)
            P.op(POOL, lambda e: e.memset(VS[:], 1.0), writes=[VS])
            P.op(POOL, lambda e: e.memset(VW[:], 1.0), writes=[VW])
            NQ = 2 if NSEL > 64 else 1
            qa_r = Rot([sbt(ph, f"C_qa{i}", [128, NQ, 4, 128], BF16) for i in range(3)])
            for t in qa_r.tiles:
                P.op(POOL, (lambda t: lambda e: e.memset(t[:], 0.0))(t), writes=[t])
            gt_r = Rot([sbt(ph, f"C_gt{i}", [128, 24], F32) for i in range(3)])
            pc_r = Rot([sbt(ph, f"C_pc{i}", [128, 512], BF16) for i in range(2 * NCC + 1)])
            pe_r = Rot([sbt(ph, f"C_pe{i}", [128, 512], BF16) for i in range(5)])
            sc = sbt(ph, "C_sc", [128, NSEL], F32)
            wk = sbt(ph, "C_wk", [128, NSEL], F32)
            t8 = sbt(ph, "C_t8", [128, 16], F32)
            selm = sbt(ph, "C_selm", [128, NSEL], F32)
            okm = sbt(ph, "C_okm", [128, NSEL], F32)
            sb16 = sbt(ph, "C_sb16", [128, 256], BF16)
            P.op(DVE, lambda e: e.memset(sb16[:], 0.0), writes=[sb16])
            cf = sbt(ph, "C_cf", [128, 32], F32)
            acc_r = Rot([sbt(ph, f"C_acc{i}", [128, 4, 64], F32) for i in range(2)])
            ob_r = Rot([sbt(ph, f"C_ob{i}", [128, 4, 64], BF16) for i in range(2)])
            p_oc = pst(ph, "C_poc", [128, 512])
            p_os = pst(ph, "C_pos", [128, 512])
            p_ow = pst(ph, "C_pow", [128, 512])
            p_imp = pst(ph, "C_pimp", [128, 512])
            p_tp = pst(ph, "C_ptp", [128, 1024], BF16)
            p_s = Rot([pst(ph, f"C_ps{i}", [128, 512]) for i in range(3)])

            class Ctx:
                pass

            def pv(po, pt, vtile, vidx, first, last):
                for h in range(4):
                    P.op(PE, (lambda h: lambda e: e.matmul(po[:, h * 65:(h + 1) * 65], lhsT=pt[:, h * 128:(h + 1) * 128], rhs=vtile[:, vidx, :],
                                                           start=(first and h == 0), stop=last, skip_group_check=True))(h),
                         reads=[pt, vtile], writes=[po])

            def combine(c, po, x, firstb, lastb):
                g = c.g
                den = po[:, 0:260].rearrange("p (h c) -> p h c", h=4)[:, :, 64]
                rs_ = slice(8 * x, 8 * x + 4)
                cs = slice(8 * x + 4, 8 * x + 8)
                P.op(DVE, lambda e: e.tensor_scalar(out=cf[:, rs_], in0=den, scalar1=1e-30, scalar2=None, op0=ALU.max), reads=[po], writes=[cf])
                P.op(DVE, lambda e: e.reciprocal(out=cf[:, rs_], in_=cf[:, rs_]), reads=[cf], writes=[cf])
                gv = c.gt[:, 12 * g:12 * g + 12].rearrange("p (h x) -> p h x", x=3)[:, :, x]
                P.op(DVE, lambda e: e.tensor_tensor(out=cf[:, cs], in0=cf[:, rs_], in1=gv, op=ALU.mult), reads=[cf, c.gt], writes=[cf])
                for h in range(4):
                    src = po[:, h * 65:h * 65 + 64]
                    dst = c.ob[:, h, :] if lastb else c.acc[:, h, :]
                    col = 8 * x + 4 + h
                    if firstb:
                        P.op(DVE, lambda e: e.tensor_scalar(out=dst, in0=src, scalar1=cf[:, col:col + 1], scalar2=None, op0=ALU.mult),
                             reads=[po, cf], writes=[c.acc, c.ob])
                    else:
                        P.op(DVE, lambda e: e.scalar_tensor_tensor(out=dst, in0=src, scalar=cf[:, col:col + 1], in1=c.acc[:, h, :],
                                                                   op0=ALU.mult, op1=ALU.add), reads=[po, cf, c.acc], writes=[c.acc, c.ob])

            def comp_part(g, n):
                c = Ctx()
                c.g, c.n = g, n
                c.qa = qa_r.get()
                c.gt = gt_r.get()
                c.acc = acc_r.get()
                c.ob = ob_r.get()
                for qi in range(NQ if n >= 32 else 1):
                    P.dma(SP, (lambda qi: lambda e: e.dma_start(out=c.qa[0:64, qi], in_=QT[4 * g:4 * g + 4, :, n * 128:(n + 1) * 128].rearrange("h d s -> d h s")))(qi),
                          reads=[QT], writes=[c.qa])
                P.dma(SP, lambda e: e.dma_start(out=c.gt[:], in_=GT[n * 128:(n + 1) * 128, :]), reads=[GT], writes=[c.gt])
                c.q2 = [c.qa[:, qi].rearrange("p h t -> p (h t)") for qi in range(NQ)]
                c.ncc = min((128 * n + 127) // 2048 + 1, NCC)
                c.pcs = [None] * c.ncc
                return c

            def select_part(c):
                n, ncc = c.n, c.ncc
                for h in range(4):
                    for cc in range(ncc):
                        P.op(PE, lambda e: e.matmul(p_imp[:, h * NSEL:(h + 1) * NSEL], lhsT=c.pcs[cc][:, h * 128:(h + 1) * 128], rhs=OV[:, cc, :],
                                                    start=(cc == 0), stop=(cc == ncc - 1)), reads=[c.pcs[cc], OV], writes=[p_imp])
                combine(c, p_oc, 0, True, False)
                for h in range(4):
                    src = p_imp[:, h * NSEL:(h + 1) * NSEL]
                    if h == 0:
                        P.op(DVE, lambda e: e.tensor_scalar(out=sc[:], in0=src, scalar1=cf[:, 0:1], scalar2=None, op0=ALU.mult), reads=[p_imp, cf], writes=[sc])
                    else:
                        P.op(DVE, lambda e: e.scalar_tensor_tensor(out=sc[:], in0=src, scalar=cf[:, h:h + 1], in1=sc[:], op0=ALU.mult, op1=ALU.add),
                             reads=[p_imp, cf, sc], writes=[sc])
                w0 = NSEL - 2 * n
                P.op(DVE, lambda e: e.tensor_tensor(out=sc[:], in0=sc[:], in1=TA[:, w0:w0 + NSEL], op=ALU.mult), reads=[sc, TA], writes=[sc])
                P.op(DVE, lambda e: e.tensor_tensor(out=sc[:], in0=sc[:], in1=TB[:, w0:w0 + NSEL], op=ALU.add), reads=[sc, TB], writes=[sc])
                P.op(DVE, lambda e: e.memset(sc[:, 0:1], 1002.0), reads=[sc], writes=[sc])
                P.op(DVE, lambda e: e.max(out=t8[:, 0:8], in_=sc[:]), reads=[sc], writes=[t8])
                P.op(DVE, lambda e: e.match_replace(out=wk[:], in_to_replace=t8[:, 0:8], in_values=sc[:], imm_value=-1e9), reads=[sc, t8], writes=[wk])
                P.op(DVE, lambda e: e.max(out=t8[:, 8:16], in_=wk[:]), reads=[wk], writes=[t8])
                P.op(DVE, lambda e: e.tensor_scalar(out=selm[:], in0=sc[:], scalar1=t8[:, 15:16], scalar2=None, op0=ALU.is_ge), reads=[sc, t8], writes=[selm])
                P.op(DVE, lambda e: e.tensor_single_scalar(out=okm[:], in_=sc[:], scalar=-500.0, op=ALU.is_gt), reads=[sc], writes=[okm])
                P.op(DVE, lambda e: e.tensor_tensor(out=selm[:], in0=selm[:], in1=okm[:], op=ALU.mult), reads=[selm, okm], writes=[selm])
                lo = min(NSEL, 64)
                P.op(DVE, lambda e: e.tensor_scalar(out=sb16[:, 128 + 64:128 + 64 + lo], in0=selm[:, 0:lo], scalar1=-1.0, scalar2=-NEG, op0=ALU.add, op1=ALU.mult),
                     reads=[selm], writes=[sb16])
                P.op(PE, lambda e: e.transpose(out=p_tp[:, 0:128], in_=sb16[:, 128:256], identity=ident[:]), reads=[sb16, ident], writes=[p_tp])
                if NSEL > 64:
                    P.op(DVE, lambda e: e.tensor_scalar(out=sb16[:, 64:NSEL], in0=selm[:, 64:NSEL], scalar1=-1.0, scalar2=-NEG, op0=ALU.add, op1=ALU.mult),
                         reads=[selm], writes=[sb16])
                    if n >= 32:
                        P.op(PE, lambda e: e.transpose(out=p_tp[:, 128:256], in_=sb16[:, 0:128], identity=ident[:]), reads=[sb16, ident], writes=[p_tp])
                P.op(DVE, lambda e: e.tensor_copy(out=c.qa[64:128, 0], in_=p_tp[64:128, 0:128].unsqueeze(1).to_broadcast([64, 4, 128])),
                     reads=[p_tp], writes=[c.qa])
                if NSEL > 64 and n >= 32:
                    P.op(DVE, lambda e: e.tensor_copy(out=c.qa[64:128, 1], in_=p_tp[64:128, 128:256].unsqueeze(1).to_broadcast([64, 4, 128])),
                         reads=[p_tp], writes=[c.qa])

            def scores(item):
                kind, k, c = item
                n = c.n
                g = c.g
                ps = p_s.get()
                ksl = slice(k * 128, (k + 1) * 128)
                if kind == "c":
                    cc = k
                    off = 128 * n - 2048 * cc
                    need_mask = off < 2176
                    P.op(PE, lambda e: e.matmul(ps[:], lhsT=KCT[g][:, cc * 128:(cc + 1) * 128], rhs=c.q2[0], start=True, stop=(not need_mask)),
                         reads=[KCT[g], c.qa], writes=[ps])
                    if need_mask:
                        P.op(PE, lambda e: e.matmul(ps[:].rearrange("p (h t) -> p h t", h=4), lhsT=ident[:],
                                                    rhs=Gb[:, off:off + 128].unsqueeze(1).to_broadcast([128, 4, 128]), start=False, stop=True),
                             reads=[ident, Gb], writes=[ps])
                    pc = pc_r.get()
                    c.pcs[cc] = pc
                    P.op(ACT, lambda e: e.activation(out=pc[:], in_=ps[:], func=AF.Exp), reads=[ps], writes=[pc])
                    return pc
                if kind == "s":
                    q = c.q2[1] if k >= 32 else c.q2[0]
                    P.op(PE, lambda e: e.matmul(ps[:], lhsT=KS[:, ksl], rhs=q, start=True, stop=(k != n)), reads=[KS, c.qa], writes=[ps])
                    if k == n:
                        P.op(PE, lambda e: e.matmul(ps[:], lhsT=ident[:], rhs=CB4[:], start=False, stop=True), reads=[ident, CB4], writes=[ps])
                else:
                    edge = (k == n) or (k == n - 4)
                    P.op(PE, lambda e: e.matmul(ps[:], lhsT=KW[:, ksl], rhs=c.q2[0], start=True, stop=(not edge)), reads=[KW, c.qa], writes=[ps])
                    if k == n:
                        P.op(PE, lambda e: e.matmul(ps[:], lhsT=ident[:], rhs=CB4[:], start=False, stop=True), reads=[ident, CB4], writes=[ps])
                    elif k == n - 4:
                        P.op(PE, lambda e: e.matmul(ps[:], lhsT=ident[:], rhs=WB4[:], start=False, stop=True), reads=[ident, WB4], writes=[ps])
                pe = pe_r.get()
                P.op(ACT, lambda e: e.activation(out=pe[:], in_=ps[:], func=AF.Exp), reads=[ps], writes=[pe])
                return pe

            def finish(item, pe):
                kind, k, c = item
                n = c.n
                if kind == "c":
                    pv(p_oc, pe, VC[c.g], k, k == 0, k == c.ncc - 1)
                elif kind == "s":
                    pv(p_os, pe, VS, k, k == 0, k == n)
                    if k == n:
                        combine(c, p_os, 1, False, True)
                        P.dma(POOL, lambda e: e.dma_start(out=OB[n * 128:(n + 1) * 128, c.g * 256:(c.g + 1) * 256],
                                                          in_=c.ob[:].rearrange("p h d -> p (h d)")), reads=[c.ob], writes=[OB])
                else:
                    pv(p_ow, pe, VW, k, k == max(0, n - 4), k == n)
                    if k == n:
                        combine(c, p_ow, 2, False, False)

            LOOK = 2
            queue = []

            def push(item):
                queue.append((item, scores(item)))
                while len(queue) > LOOK:
                    it, pe = queue.pop(0)
                    finish(it, pe)

            def drain():
                while queue:
                    it, pe = queue.pop(0)
                    finish(it, pe)

            for g in range(2):
                P.dma(SP, (lambda g: lambda e: e.dma_start(out=KS[0:64, :], in_=KT[4 + g, :, :]))(g), reads=[KT], writes=[KS])
                P.dma(SP, (lambda g: lambda e: e.dma_start(out=KW[0:64, :], in_=KT[6 + g, :, :]))(g), reads=[KT], writes=[KW])
                P.dma(SP, (lambda g: lambda e: e.dma_start(out=VS[:, :, 0:64], in_=VT[:, g * 64:(g + 1) * 64].rearrange("(n p) d -> p n d", p=128)))(g),
                      reads=[VT], writes=[VS])
                P.dma(SP, (lambda g: lambda e: e.dma_start(out=VW[:, :, 0:64], in_=VT[:, 128 + g * 64:128 + (g + 1) * 64].rearrange("(n p) d -> p n d", p=128)))(g),
                      reads=[VT], writes=[VW])
                prev = None
                for n in range(NT + 1):
                    cur = comp_part(g, n) if n < NT else None
                    if cur is not None:
                        for cc in range(cur.ncc):
                            push(("c", cc, cur))
                    sel_items = [("s", k, prev) for k in range(prev.n + 1)] if prev is not None else []
                    half = len(sel_items) // 2
                    for it in sel_items[:half]:
                        push(it)
                    if cur is not None:
                        if any(it[0] == "c" and it[2] is cur for it, _ in queue):
                            drain()
                        select_part(cur)
                    for it in sel_items[half:]:
                        push(it)
                    if cur is not None:
                        for k in range(max(0, n - 4), n + 1):
                            push(("w", k, cur))
                    prev = cur
                drain()
            P.barrier()
            P.emit()

    def phase_D():
        with ExitStack() as ph:
            g01 = load_gain(ph, 0, 1)
            wo = sbt(ph, "D_wo", [128, 8, D], BF16)
            load_w_bf16(wo, e_w_out, 8, "wo")
            m_r = Rot([sbt(ph, f"D_m{i}", [128, D], BF16) for i in range(2)])
            mT_r = Rot([sbt(ph, f"D_mT{i}", [128, 8, 128], BF16) for i in range(2)])
            st_r = Rot([sbt(ph, f"D_st{i}", [128, 8], F32) for i in range(2)])
            hres_r = Rot([sbt(ph, f"D_hr{i}", [128, D], F32) for i in range(2)])
            out_r = Rot([sbt(ph, f"D_out{i}", [128, D], F32) for i in range(2)])
            tp = pst(ph, "D_tp", [128, 1024], BF16)
            pp = Rot([pst(ph, f"D_pp{i}", [128, 512]) for i in range(6)])
            def Da(n):
                m = m_r.get()
                rs = slice(n * 128, (n + 1) * 128)
                P.dma(SP, lambda e: e.dma_start(out=m[:, 0:512], in_=OA[rs, :]), reads=[OA], writes=[m])
                P.dma(SP, lambda e: e.dma_start(out=m[:, 512:1024], in_=OB[rs, :]), reads=[OB], writes=[m])
                mT = mT_r.get()
                transpose_to(tp, m, 8, lambda: mT[:], mT)
                return (n, mT)

            def Db(c):
                n, mT = c
                rs = slice(n * 128, (n + 1) * 128)
                ps2 = [pp.get(), pp.get()]
                for hf in range(2):
                    for kc in range(8):
                        P.op(PE, lambda e: e.matmul(ps2[hf][:], lhsT=mT[:, kc, :], rhs=wo[:, kc, hf * 512:(hf + 1) * 512], start=(kc == 0), stop=(kc == 7)),
                             reads=[mT, wo], writes=[ps2[hf]])
                outt = out_r.get()
                norm_residual_store(ph, ps2, g01, x_d[rs, :], Bx, H0[rs, :], H0, st_r.get(), outt, hres_r.get(), outt)

            cur = Da(0)
            for n in range(NT):
                nxt = Da(n + 1) if n + 1 < NT else None
                Db(cur)
                cur = nxt
            P.barrier()
            P.emit()

    def phase_FFN(layer, Hin, Hout_ap_fn, Hout_tl, tag):
        TS = 256
        NG = S // TS
        TPG = TS // 128
        with ExitStack() as ph:
            gpre = load_gain(ph, layer, 2)
            gpost = load_gain(ph, layer, 3)
            w1 = sbt(ph, tag + "_w1", [128, 8, 4096], BF16)
            w2 = sbt(ph, tag + "_w2", [128, 32, D], BF16)
            load_w_bf16(w1, ffn_w1[layer], 8, "w1")
            load_w_bf16(w2, ffn_w2[layer], 32, "w2")
            xt_r = Rot([sbt(ph, tag + f"_xt{i}", [128, D], F32) for i in range(2)])
            hr_r = Rot([sbt(ph, tag + f"_hr{i}", [128, D], F32) for i in range(1)])
            stA_r = Rot([sbt(ph, tag + f"_stA{i}", [128, 4], F32) for i in range(4)])
            stC_r = Rot([sbt(ph, tag + f"_stC{i}", [128, 8], F32) for i in range(2)])
            xn_r = Rot([sbt(ph, tag + f"_xn{i}", [128, D], BF16) for i in range(2)])
            xnT_r = Rot([sbt(ph, tag + f"_xnT{i}", [128, 8, TS], BF16) for i in range(2)])
            hT_r = Rot([sbt(ph, tag + f"_hT{i}", [128, 32, TS], BF16) for i in range(2)])
            rl_r = Rot([sbt(ph, tag + f"_rl{i}", [128, TS], F32) for i in range(3)])
            out_r = Rot([sbt(ph, tag + f"_out{i}", [128, D], F32) for i in range(2)])
            tp = pst(ph, tag + "_tp", [128, 1024], BF16)
            pp = Rot([pst(ph, tag + f"_pp{i}", [128, 512]) for i in range(7)])

            class Cx:
                pass

            def S1a(sg):
                c = Cx()
                c.sg = sg
                c.xnT = xnT_r.get()
                c.xns = []
                for tl in range(TPG):
                    n = sg * TPG + tl
                    rs = slice(n * 128, (n + 1) * 128)
                    xt = xt_r.get()
                    stt = stA_r.get()
                    xn = xn_r.get()
                    c.xns.append(xn)
                    P.dma(SP, lambda e: e.dma_start(out=xt[:], in_=Hin[rs, :]), reads=[Hin], writes=[xt])
                    r = rms_rstd(stt, 0, xt[:], xt, xn, D)
                    P.op(DVE, lambda e: e.scalar_tensor_tensor(out=xn[:], in0=xt[:], scalar=r, in1=gpre[:], op0=ALU.mult, op1=ALU.mult),
                         reads=[xt, stt, gpre], writes=[xn])
                return c

            def S1b(c):
                for tl in range(TPG):
                    transpose_to(tp, c.xns[tl], 8, lambda: c.xnT[:, :, tl * 128:(tl + 1) * 128], c.xnT)

            def S2(c):
                c.hT = hT_r.get()
                for fc in range(32):
                    ps = pp.get()
                    for kc in range(8):
                        P.op(PE, lambda e: e.matmul(ps[:, 0:TS], lhsT=w1[:, kc, fc * 128:(fc + 1) * 128], rhs=c.xnT[:, kc, :], start=(kc == 0), stop=(kc == 7)),
                             reads=[w1, c.xnT], writes=[ps])
                    rl = rl_r.get()
                    P.op(ACT, lambda e: e.activation(out=rl[:], in_=ps[:, 0:TS], func=AF.Relu), reads=[ps], writes=[rl])
                    P.op(DVE, lambda e: e.tensor_tensor(out=c.hT[:, fc, :], in0=rl[:], in1=rl[:], op=ALU.mult), reads=[rl], writes=[c.hT])

            def S3(c):
                for tl in range(TPG):
                    n = c.sg * TPG + tl
                    rs = slice(n * 128, (n + 1) * 128)
                    ps2 = [pp.get(), pp.get()]
                    for hf in range(2):
                        for fc in range(32):
                            P.op(PE, lambda e: e.matmul(ps2[hf][:], lhsT=c.hT[:, fc, tl * 128:(tl + 1) * 128], rhs=w2[:, fc, hf * 512:(hf + 1) * 512],
                                                        start=(fc == 0), stop=(fc == 31)), reads=[c.hT, w2], writes=[ps2[hf]])
                    outt = out_r.get()
                    norm_residual_store(ph, ps2, gpost, Hin[rs, :], Hin.b, Hout_ap_fn(rs), Hout_tl, stC_r.get(), outt, hr_r.get(), outt)

            cx = {0: S1a(0)}
            S1b(cx[0])
            for i in range(NG + 1):
                if i + 1 < NG:
                    cx[i + 1] = S1a(i + 1)
                if i < NG:
                    S2(cx[i])
                if i + 1 < NG:
                    S1b(cx[i + 1])
                if 0 <= i - 1 < NG:
                    S3(cx.pop(i - 1))
            P.barrier()
            P.emit()

    def phase_E():
        with ExitStack() as ph:
            g10 = load_gain(ph, 1, 0)
            g11 = load_gain(ph, 1, 1)
            wi = sbt(ph, "E_wi", [128, 8, 4096], BF16)
            wo = sbt(ph, "E_wo", [128, 16, D], BF16)
            load_w_bf16(wi, o_w_in, 8, "owi")
            load_w_bf16(wo, o_w_out, 16, "owo")
            U16 = sbt(ph, "E_U16", [128, 128], BF16)
            P.dma(POOL, lambda e: e.dma_start(out=U16[:], in_=cd["c_U"]), writes=[U16])
            wcT = sbt(ph, "E_wcT", [128, 8, 128], BF16)
            bs = sbt(ph, "E_bs", [128, 8], F32)
            P.dma(SP, lambda e: e.dma_start(out=bs[:], in_=o_b_s.rearrange("g t -> t g"), allow_slow_non_contiguous=True), writes=[bs])
            lng = sbt(ph, "E_lng", [128, 2048], F32)
            lnb = sbt(ph, "E_lnb", [128, 2048], F32)
            P.dma(SP, lambda e: e.dma_start(out=lng[:], in_=o_ln_g.partition_broadcast(128)), writes=[lng])
            P.dma(SP, lambda e: e.dma_start(out=lnb[:], in_=o_ln_b.partition_broadcast(128)), writes=[lnb])
            xt_r = Rot([sbt(ph, f"E_xt{i}", [128, D], F32) for i in range(2)])
            hr_r = Rot([sbt(ph, f"E_hr{i}", [128, D], F32) for i in range(1)])
            out_r = Rot([sbt(ph, f"E_out{i}", [128, D], F32) for i in range(1)])
            stA_r = Rot([sbt(ph, f"E_stA{i}", [128, 4], F32) for i in range(2)])
            stB_r = Rot([sbt(ph, f"E_stB{i}", [128, 16], F32) for i in range(3)])
            stC_r = Rot([sbt(ph, f"E_stC{i}", [128, 8], F32) for i in range(2)])
            xn_r = Rot([sbt(ph, f"E_xn{i}", [128, D], BF16) for i in range(2)])
            xnT_r = Rot([sbt(ph, f"E_xnT{i}", [128, 8, 128], BF16) for i in range(2)])
            u_r = Rot([sbt(ph, f"E_u{i}", [128, 2048], F32) for i in range(3)])
            v_r = Rot([sbt(ph, f"E_v{i}", [128, 2048], F32) for i in range(2)])
            vn_r = Rot([sbt(ph, f"E_vn{i}", [128, 2048], BF16) for i in range(2)])
            yb_r = Rot([sbt(ph, f"E_y{i}", [128, 2048], BF16) for i in range(1)])
            yT_r = Rot([sbt(ph, f"E_yT{i}", [128, 16, 128], BF16) for i in range(2)])
            tp = pst(ph, "E_tp", [128, 1024], BF16)
            pp = Rot([pst(ph, f"E_pp{i}", [128, 512]) for i in range(7)])
            ws = yb_r.tiles[0]
            P.dma(POOL, lambda e: e.dma_start(out=ws[:, 0:1024].rearrange("p (g s) -> p g s", g=8), in_=o_w_s.rearrange("g t s -> t g s")), writes=[ws])
            for g in range(8):
                P.op(PE, (lambda g: lambda e: e.transpose(out=tp[:, g * 128:(g + 1) * 128], in_=ws[:, g * 128:(g + 1) * 128], identity=ident[:]))(g),
                     reads=[ws, ident], writes=[tp])
            P.op(DVE, lambda e: e.tensor_tensor(out=wcT[:], in0=tp[:, 0:1024].rearrange("p (g t) -> p g t", g=8),
                                                in1=U16[:].unsqueeze(1).to_broadcast([128, 8, 128]), op=ALU.mult), reads=[tp, U16], writes=[wcT])

            class Cx:
                pass

            def F1a(n):
                c = Cx()
                c.n = n
                c.rs = slice(n * 128, (n + 1) * 128)
                xt = xt_r.get()
                stt = stA_r.get()
                c.xn = xn_r.get()
                c.xnT = xnT_r.get()
                P.dma(SP, lambda e: e.dma_start(out=xt[:], in_=H1[c.rs, :]), reads=[H1], writes=[xt])
                r = rms_rstd(stt, 0, xt[:], xt, c.xn, D)
                P.op(DVE, lambda e: e.scalar_tensor_tensor(out=c.xn[:], in0=xt[:], scalar=r, in1=g10[:], op0=ALU.mult, op1=ALU.mult),
                     reads=[xt, stt, g10], writes=[c.xn])
                return c

            def F1b(c):
                transpose_to(tp, c.xn, 8, lambda: c.xnT[:], c.xnT)

            def F2(c, part):
                if part == 0:
                    c.u = u_r.get()
                    c.v = v_r.get()
                    c.st = stB_r.get()
                    P.op(DVE, lambda e: e.memset(c.st[:], 0.0), writes=[c.st])
                for cg in range(4 * part, 4 * part + 4):
                    ps = pp.get()
                    for kc in range(8):
                        P.op(PE, lambda e: e.matmul(ps[:], lhsT=c.xnT[:, kc, :], rhs=wi[:, kc, cg * 512:(cg + 1) * 512], start=(kc == 0), stop=(kc == 7)),
                             reads=[wi, c.xnT], writes=[ps])
                    if cg < 4:
                        P.op(ACT, lambda e: e.activation(out=c.u[:, cg * 512:(cg + 1) * 512], in_=ps[:], func=AF.Gelu_apprx_tanh), reads=[ps], writes=[c.u])
                    else:
                        c2 = cg - 4
                        P.op(ACT, lambda e: e.activation(out=c.v[:, c2 * 512:(c2 + 1) * 512], in_=ps[:], func=AF.Gelu_apprx_tanh,
                                                         accum_out=c.st[:, 4 + c2:5 + c2]), reads=[ps], writes=[c.v, c.st])

            def L(c):
                stt, v32 = c.st, c.v
                c.vn = vn_r.get()
                vn = c.vn
                P.op(ACT, lambda e: e.activation(out=vn[:], in_=v32[:], func=AF.Square, accum_out=stt[:, 8:9]), reads=[v32], writes=[vn, stt])
                P.op(DVE, lambda e: e.tensor_tensor(out=stt[:, 9:11], in0=stt[:, 4:6], in1=stt[:, 6:8], op=ALU.add), reads=[stt], writes=[stt])
                P.op(DVE, lambda e: e.tensor_tensor(out=stt[:, 11:12], in0=stt[:, 9:10], in1=stt[:, 10:11], op=ALU.add), reads=[stt], writes=[stt])
                P.op(DVE, lambda e: e.tensor_scalar(out=stt[:, 12:13], in0=stt[:, 11:12], scalar1=1.0 / 2048, scalar2=None, op0=ALU.mult), reads=[stt], writes=[stt])
                P.op(DVE, lambda e: e.tensor_tensor(out=stt[:, 13:14], in0=stt[:, 12:13], in1=stt[:, 12:13], op=ALU.mult), reads=[stt], writes=[stt])
                P.op(DVE, lambda e: e.scalar_tensor_tensor(out=stt[:, 14:15], in0=stt[:, 8:9], scalar=1.0 / 2048, in1=stt[:, 13:14],
                                                           op0=ALU.mult, op1=ALU.subtract), reads=[stt], writes=[stt])
                P.op(ACT, lambda e: e.activation(out=stt[:, 15:16], in_=stt[:, 14:15], func=AF.Sqrt, bias=EPS), reads=[stt], writes=[stt])
                P.op(DVE, lambda e: e.reciprocal(out=stt[:, 15:16], in_=stt[:, 15:16]), reads=[stt], writes=[stt])
                P.op(DVE, lambda e: e.scalar_tensor_tensor(out=v32[:], in0=v32[:], scalar=stt[:, 12:13], in1=lng[:], op0=ALU.subtract, op1=ALU.mult),
                     reads=[v32, stt, lng], writes=[v32])
                P.op(DVE, lambda e: e.scalar_tensor_tensor(out=vn[:], in0=v32[:], scalar=stt[:, 15:16], in1=lnb[:], op0=ALU.mult, op1=ALU.add),
                     reads=[v32, stt, lnb], writes=[vn])

            def M1a(c):
                c.yb = yb_r.get()
                c.yT = yT_r.get()
                yb = c.yb
                for g2 in range(4):
                    ps = pp.get()
                    for gg in range(2):
                        g = g2 * 2 + gg
                        P.op(PE, lambda e: e.matmul(ps[:, gg * 256:(gg + 1) * 256], lhsT=wcT[:, g, :], rhs=c.vn[:, g * 256:(g + 1) * 256], start=True, stop=True),
                             reads=[wcT, c.vn], writes=[ps])
                    for gg in range(2):
                        g = g2 * 2 + gg
                        P.op(DVE, lambda e: e.scalar_tensor_tensor(out=yb[:, g * 256:(g + 1) * 256], in0=ps[:, gg * 256:(gg + 1) * 256], scalar=bs[:, g:g + 1],
                                                                   in1=c.u[:, g * 256:(g + 1) * 256], op0=ALU.add, op1=ALU.mult),
                             reads=[ps, bs, c.u], writes=[yb])

            def M1b(c):
                yb = c.yb
                for half in range(2):
                    for cc8 in range(8):
                        cc = half * 8 + cc8
                        P.op(PE, lambda e: e.transpose(out=tp[:, cc8 * 128:(cc8 + 1) * 128], in_=yb[:, cc * 128:(cc + 1) * 128], identity=ident[:]),
                             reads=[yb, ident], writes=[tp])
                    P.op(ACT, lambda e: e.copy(out=c.yT[:, half * 8:(half + 1) * 8, :], in_=tp[:, 0:1024].rearrange("p (c t) -> p c t", c=8)),
                         reads=[tp], writes=[c.yT])

            def M2a(c):
                c.ps2 = [pp.get(), pp.get()]
                for hf in range(2):
                    for cc in range(16):
                        P.op(PE, lambda e: e.matmul(c.ps2[hf][:], lhsT=c.yT[:, cc, :], rhs=wo[:, cc, hf * 512:(hf + 1) * 512], start=(cc == 0), stop=(cc == 15)),
                             reads=[c.yT, wo], writes=[c.ps2[hf]])

            def M2b(c):
                outt = out_r.get()
                norm_residual_store(ph, c.ps2, g11, H1[c.rs, :], H1.b, H2[c.rs, :], H2, stC_r.get(), outt, hr_r.get(), outt)

            cx = {}
            cx[0] = F1a(0)
            F1b(cx[0])
            F2(cx[0], 0)
            F2(cx[0], 1)
            if NT > 1:
                cx[1] = F1a(1)
                F1b(cx[1])
            for i in range(NT + 2):
                if i < NT:
                    L(cx[i])
                if i + 2 < NT:
                    cx[i + 2] = F1a(i + 2)
                if 0 <= i - 2 < NT:
                    M2a(cx[i - 2])
                    M2b(cx[i - 2])
                if 0 <= i - 1 < NT:
                    M1a(cx[i - 1])
                if i + 1 < NT:
                    F2(cx[i + 1], 0)
                if 0 <= i - 1 < NT:
                    M1b(cx[i - 1])
                if i + 1 < NT:
                    F2(cx[i + 1], 1)
                if i + 2 < NT:
                    F1b(cx[i + 2])
                cx.pop(i - 2, None)
            P.barrier()
            P.emit()

    if "A" in phases:
        phase_A()
    if "B" in phases:
        phase_BC()
    if "D" in phases:
        phase_D()
    if "F" in phases:
        phase_FFN(0, H0, lambda rs: H1[rs, :], H1, "F0")
    if "E" in phases:
        phase_E()
    if "G" in phases:
        ytl = Tl(y_d, "y")
        phase_FFN(1, H2, lambda rs: y_d[rs, :], ytl, "F1")
    P.barrier()
    P.emit()
    top.close()
    return nc, P


WEIGHT_NAMES = ["norm_g", "ffn_w1", "ffn_w2", "e_w_in", "e_w_out", "gla_w_gate", "gla_b_gate", "gla_norm", "nsa_gate_b", "nsa_cmp_pos",
                "nsa_cmp_w1", "nsa_cmp_w2", "o_w_in", "o_ln_g", "o_ln_b", "o_w_s", "o_b_s", "o_w_out"]


def prep_weights(inputs):
    m = {}
    for k in WEIGHT_NAMES:
        a = np.asarray(inputs[k], dtype=np.float32)
        if k in ("ffn_w1", "ffn_w2", "norm_g"):
            m[k] = np.ascontiguousarray(a)
        elif k == "gla_norm":
            m[k] = np.ascontiguousarray(a[0].reshape(512))
        else:
            m[k] = np.ascontiguousarray(a[0])
    return m


def kernel(**inputs):
    x = np.asarray(inputs["x"], dtype=np.float32)
    B, S, _ = x.shape
    nc, _ = build(S)
    wm = prep_weights(inputs)
    wm.update(make_consts(S))
    in_maps = [dict(wm, x=np.ascontiguousarray(x[b])) for b in range(B)]
    res = run_bass_kernel_spmd(nc, in_maps, core_ids=list(range(B)))
    return np.stack([r["y"] for r in res.results], axis=0).astype(np.float32)
```

```python
import math
from contextlib import ExitStack

import numpy as np
import concourse.bass as bass
import concourse.mybir as mybir
from concourse.bass_utils import run_bass_kernel_spmd

F32 = mybir.dt.float32
BF16 = mybir.dt.bfloat16
AF = mybir.ActivationFunctionType
ALU = mybir.AluOpType

PE, ACT, DVE, POOL, SP = "tensor", "scalar", "vector", "gpsimd", "sync"
COMPUTE = (PE, ACT, DVE, POOL)
ALLENG = (PE, ACT, DVE, POOL, SP)

D = 1024
NEG = -30000.0
EPS = 1e-6


class Buf:
    __slots__ = ("name", "last_w", "reads")

    def __init__(self, name):
        self.name = name
        self.last_w = None
        self.reads = {}


class Tl:
    __slots__ = ("t", "b")

    def __init__(self, t, name):
        self.t = t
        self.b = Buf(name)

    def __getitem__(self, k):
        return self.t[k]


class _Rec:
    def __init__(self):
        self.call = None

    def __getattr__(self, name):
        def f(*a, **k):
            self.call = (name, a, k)
        return f


def _record(fn):
    r = _Rec()
    fn(r)
    assert r.call is not None
    return r.call


class Prog:
    def __init__(self, nc, n_dma_sems=8):
        self.nc = nc
        self.ops = {e: [] for e in ALLENG}
        self.count = {e: 0 for e in COMPUTE}
        self.waited = {e: {} for e in ALLENG}
        self.n_dma_sems = n_dma_sems
        self.dma_issued = {}
        self.dma_rr = {e: 0 for e in (SP, POOL, ACT)}
        self.sems = {}
        self.sem_stack = ExitStack()
        self.ninstr = 0

    def _need(self, eng, ev, waits, strict=False):
        if ev is None:
            return
        key, val = ev
        if not strict:
            if key == eng and eng == PE:
                return
            if key == eng and val <= self.count[eng] - 3:
                return
        if self.waited[eng].get(key, 0) >= val:
            return
        self.waited[eng][key] = val
        waits.append((key, val))

    def _deps(self, eng, reads, writes, waits, skip_same_war):
        strict = not skip_same_war
        for b in reads:
            self._need(eng, b.last_w, waits, strict)
        for b in writes:
            self._need(eng, b.last_w, waits, strict)
            for k, v in b.reads.items():
                if skip_same_war and k == eng:
                    continue
                self._need(eng, (k, v), waits, strict)

    def _mark(self, ev, reads, writes):
        for b in reads:
            b.reads[ev[0]] = ev[1]
        for b in writes:
            b.last_w = ev
            b.reads = {}

    def op(self, eng, fn, reads=(), writes=()):
        reads = [r.b if isinstance(r, Tl) else r for r in reads]
        writes = [w.b if isinstance(w, Tl) else w for w in writes]
        waits = []
        self._deps(eng, reads, writes, waits, True)
        self.count[eng] += 1
        ev = (eng, self.count[eng])
        self._mark(ev, reads, writes)
        self.ops[eng].append((waits, _record(fn), (eng, 1)))
        self.ninstr += 1
        return ev

    def dma(self, eng, fn, reads=(), writes=()):
        reads = [r.b if isinstance(r, Tl) else r for r in reads]
        writes = [w.b if isinstance(w, Tl) else w for w in writes]
        slot = self.dma_rr[eng] % (48 if eng == POOL else self.n_dma_sems)
        self.dma_rr[eng] += 1
        key = ("dma", eng, slot)
        issued = self.dma_issued.get(key, 0)
        waits = []
        if issued:
            self._need(eng, (key, 16 * issued), waits, True)
        self._deps(eng, reads, writes, waits, False)
        self.dma_issued[key] = issued + 1
        ev = (key, 16 * (issued + 1))
        self._mark(ev, reads, writes)
        self.ops[eng].append((waits, _record(fn), (key, 16)))
        self.ninstr += 1
        return ev

    def barrier(self):
        evs = [(e, self.count[e]) for e in COMPUTE if self.count[e]]
        evs += [(k, 16 * n) for k, n in self.dma_issued.items()]
        for eng in ALLENG:
            waits = []
            for key, val in evs:
                if key == eng:
                    continue
                if self.waited[eng].get(key, 0) >= val:
                    continue
                self.waited[eng][key] = val
                waits.append((key, val))
            if waits:
                self.ops[eng].append((waits, None, None))

    def emit(self):
        nc = self.nc
        keys = set()
        for e in self.ops:
            for waits, fn, inc in self.ops[e]:
                for k, _ in waits:
                    keys.add(k)
                if inc is not None:
                    keys.add(inc[0])
        for k in sorted(keys, key=str):
            if k in self.sems:
                continue
            nm = "s_" + "_".join(str(x) for x in (k if isinstance(k, tuple) else (k,)))
            self.sems[k] = self.sem_stack.enter_context(nc.semaphore(nm))
        sems = self.sems
        with nc.Block() as block:
            def run(engname):
                def body(eng):
                    for waits, fn, inc in self.ops[engname]:
                        for k, v in waits:
                            eng.wait_ge(sems[k], v)
                        if fn is not None:
                            getattr(eng, fn[0])(*fn[1], **fn[2]).then_inc(sems[inc[0]], inc[1])
                return body

            if self.ops[SP]:
                block.sync(run(SP))
            if self.ops[PE]:
                block.tensor(run(PE))
            if self.ops[ACT]:
                block.scalar(run(ACT))
            if self.ops[DVE]:
                block.vector(run(DVE))
            if self.ops[POOL]:
                block.gpsimd(run(POOL))
        for e in self.ops:
            self.ops[e] = []


class Rot:
    def __init__(self, tiles):
        self.tiles = tiles
        self.i = 0

    def get(self):
        t = self.tiles[self.i % len(self.tiles)]
        self.i += 1
        return t


def make_consts(S):
    NSEL = S // 64
    NCMP = S // 16 - 1
    NCC = (NCMP + 127) // 128
    p = np.arange(128)
    c = {}
    c["c_ident"] = np.eye(128, dtype=np.float32)
    U = (p[:, None] <= p[None, :]).astype(np.float32)
    c["c_U"] = U
    c["c_Ls"] = (p[:, None] > p[None, :]).astype(np.float32)
    c["c_ones"] = np.ones((128, 128), np.float32)
    c["c_U4"] = np.tile(U, (1, 4))
    c["c_CB4"] = np.tile(np.where(p[:, None] <= p[None, :], 0.0, NEG).astype(np.float32), (1, 4))
    c["c_WB4"] = np.tile(np.where(p[:, None] > p[None, :], 0.0, NEG).astype(np.float32), (1, 4))
    u = np.arange(S)
    c["c_G"] = np.where(16 * p[:, None] + 31 <= u[None, :], 0.0, NEG).astype(np.float32)
    jb = np.arange(NSEL)
    r64 = np.arange(64)
    c["c_Zaug"] = ((u[None, :] // 64) % 64 == r64[:, None]).astype(np.float32)
    ov = np.zeros((128, NCC, NSEL), np.float32)
    for cc in range(NCC):
        cidx = cc * 128 + p
        m = (cidx[:, None] >= 4 * jb[None, :] - 1) & (cidx[:, None] <= 4 * jb[None, :] + 3) & (cidx[:, None] < NCMP)
        ov[:, cc, :] = m
    c["c_OV"] = ov
    w = np.arange(2 * NSEL)
    jr = w[None, :] - NSEL
    h = (p[:, None] >= 64).astype(np.int64)
    TA = (jr <= h - 2).astype(np.float32)
    TB = np.where(jr == h, 1000.0, np.where(jr == h - 1, 1001.0, np.where(jr <= h - 2, 0.0, -1000.0 - w[None, :])))
    c["c_TA"] = TA
    c["c_TB"] = TB.astype(np.float32)
    return c


OFF = {}
_o = 0
for _n, _sz in (("gq", 256), ("gk", 256), ("gv", 512), ("glr", 16), ("gr", 512), ("nq", 512), ("kc", 128), ("vc", 128),
                ("ks", 128), ("vs", 128), ("kw", 128), ("vw", 128), ("ng", 24)):
    OFF[_n] = _o
    _o += _sz
WIN = _o


def build(S, debug=False, phases="ABCDEFG"):
    NT = S // 128
    NST = S // 512
    NSEL = S // 64
    NCMP = S // 16 - 1
    NCC = (NCMP + 127) // 128
    nc = bass.Bass("TRN2", target_bir_lowering=False)
    P = Prog(nc)

    def din(name, shape):
        return nc.dram_tensor(name, list(shape), F32, kind="ExternalInput").ap()

    x_d = din("x", [S, D])
    norm_g = din("norm_g", [2, 4, D])
    ffn_w1 = din("ffn_w1", [2, D, 4096])
    ffn_w2 = din("ffn_w2", [2, 4096, D])
    e_w_in = din("e_w_in", [D, WIN])
    e_w_out = din("e_w_out", [D, D])
    gla_w_gate = din("gla_w_gate", [16, 256])
    gla_b_gate = din("gla_b_gate", [256])
    gla_norm = din("gla_norm", [512])
    nsa_gate_b = din("nsa_gate_b", [24])
    nsa_cmp_pos = din("nsa_cmp_pos", [2, 32, 64])
    nsa_cmp_w1 = din("nsa_cmp_w1", [2, 2048, 128])
    nsa_cmp_w2 = din("nsa_cmp_w2", [2, 128, 64])
    o_w_in = din("o_w_in", [D, 4096])
    o_ln_g = din("o_ln_g", [2048])
    o_ln_b = din("o_ln_b", [2048])
    o_w_s = din("o_w_s", [8, 128, 128])
    o_b_s = din("o_b_s", [8, 128])
    o_w_out = din("o_w_out", [2048, D])
    cshapes = {k: v.shape for k, v in make_consts(S).items()}
    cd = {k: din(k, shp) for k, shp in cshapes.items()}

    y_d = nc.dram_tensor("y", [S, D], F32, kind="ExternalOutput").ap()

    def scratch(name, shape, dt):
        kind = "ExternalOutput" if debug else "Internal"
        t = Tl(None, name)
        t.t = nc.dram_tensor(name, list(shape), dt, kind=kind).ap()
        return t

    QT = scratch("QT", [8, 64, S], BF16)
    KT = scratch("KT", [8, 64, S], BF16)
    VT = scratch("VT", [S, 256], BF16)
    GT = scratch("GT", [S, 24], F32)
    OA = scratch("OA", [S, 512], BF16)
    OB = scratch("OB", [S, 512], BF16)
    H0 = scratch("H0", [S, D], F32)
    H1 = scratch("H1", [S, D], F32)
    H2 = scratch("H2", [S, D], F32)
    Bx = Buf("x")
    By = Buf("y")

    top = ExitStack()

    def sbt(st, name, shape, dt):
        return Tl(st.enter_context(nc.sbuf_tensor(name, list(shape), dt)), name)

    def pst(st, name, shape, dt=F32):
        return Tl(st.enter_context(nc.psum_tensor(name, list(shape), dt)), name)

    ident = sbt(top, "ident", [128, 128], BF16)
    P.dma(POOL, lambda e: e.dma_start(out=ident[:], in_=cd["c_ident"]), writes=[ident])

    def load_gain(st, l, j):
        t = sbt(st, f"gain{l}{j}", [128, D], F32)
        P.dma(SP, lambda e: e.dma_start(out=t[:], in_=norm_g[l, j].partition_broadcast(128)), writes=[t])
        return t

    def rms_rstd(st_tile, col, src_ap, src_tl, junk, n, lnexp=False):
        if lnexp:
            P.op(DVE, lambda e: e.memset(st_tile[:], 0.0), writes=[st_tile])
            P.op(ACT, lambda e: e.activation(out=junk[:, 0:n], in_=src_ap, func=AF.Square, accum_out=st_tile[:, col:col + 1]),
                 reads=[src_tl], writes=[junk, st_tile])
            P.op(DVE, lambda e: e.tensor_scalar(out=st_tile[:, col + 1:col + 2], in0=st_tile[:, col:col + 1], scalar1=1.0 / n, scalar2=EPS,
                                                op0=ALU.mult, op1=ALU.add), reads=[st_tile], writes=[st_tile])
            P.op(ACT, lambda e: e.activation(out=st_tile[:, col + 1:col + 2], in_=st_tile[:, col + 1:col + 2], func=AF.Ln), reads=[st_tile], writes=[st_tile])
            P.op(ACT, lambda e: e.activation(out=st_tile[:, col + 2:col + 3], in_=st_tile[:, col + 1:col + 2], func=AF.Exp, scale=-0.5),
                 reads=[st_tile], writes=[st_tile])
            return st_tile[:, col + 2:col + 3]
        P.op(DVE, lambda e: e.memset(st_tile[:], 0.0), writes=[st_tile])
        P.op(ACT, lambda e: e.activation(out=junk[:, 0:n], in_=src_ap, func=AF.Square, accum_out=st_tile[:, col:col + 1]),
             reads=[src_tl], writes=[junk, st_tile])
        P.op(ACT, lambda e: e.activation(out=st_tile[:, col + 1:col + 2], in_=st_tile[:, col:col + 1], func=AF.Sqrt,
                                         scale=1.0 / n, bias=EPS), reads=[st_tile], writes=[st_tile])
        P.op(DVE, lambda e: e.reciprocal(out=st_tile[:, col + 2:col + 3], in_=st_tile[:, col + 1:col + 2]),
             reads=[st_tile], writes=[st_tile])
        return st_tile[:, col + 2:col + 3]

    def transpose_to(tp, src, nchunk, dst_ap_fn, dst_tl):
        for c in range(nchunk):
            P.op(PE, (lambda c: lambda e: e.transpose(out=tp[:, c * 128:(c + 1) * 128], in_=src[:, c * 128:(c + 1) * 128],
                                                      identity=ident[:]))(c), reads=[src, ident], writes=[tp])
        P.op(ACT, lambda e: e.copy(out=dst_ap_fn(), in_=tp[:, 0:nchunk * 128].rearrange("p (c t) -> p c t", c=nchunk)),
             reads=[tp], writes=[dst_tl])

    def load_w_bf16(dst, src_ap, nk, name):
        v = src_ap.rearrange("(k p) n -> p k n", p=128)
        for k in range(0, nk, 8):
            k1 = min(nk, k + 8)
            P.dma(POOL, (lambda k, k1: lambda e: e.dma_start(out=dst[:, k:k1, :], in_=v[:, k:k1, :]))(k, k1), writes=[dst])

    def norm_residual_store(ph, m_ps2, gain, hin_ap, hin_buf, hout_ap, hout_tl, stt, junk, hres, outt):
        P.op(DVE, lambda e: e.memset(stt[:], 0.0), writes=[stt])
        for hf in range(2):
            P.op(ACT, (lambda hf: lambda e: e.activation(out=junk[:, 0:512], in_=m_ps2[hf][:], func=AF.Square,
                                                         accum_out=stt[:, hf:hf + 1]))(hf),
                 reads=[m_ps2[hf]], writes=[junk, stt])
        P.op(DVE, lambda e: e.tensor_tensor(out=stt[:, 2:3], in0=stt[:, 0:1], in1=stt[:, 1:2], op=ALU.add), reads=[stt], writes=[stt])
        P.op(ACT, lambda e: e.activation(out=stt[:, 3:4], in_=stt[:, 2:3], func=AF.Sqrt, scale=1.0 / D, bias=EPS),
             reads=[stt], writes=[stt])
        P.op(DVE, lambda e: e.reciprocal(out=stt[:, 4:5], in_=stt[:, 3:4]), reads=[stt], writes=[stt])
        P.dma(SP, lambda e: e.dma_start(out=hres[:], in_=hin_ap), reads=[hin_buf], writes=[hres])
        for hf in range(2):
            P.op(DVE, (lambda hf: lambda e: e.scalar_tensor_tensor(out=outt[:, hf * 512:(hf + 1) * 512], in0=m_ps2[hf][:],
                                                                   scalar=stt[:, 4:5], in1=gain[:, hf * 512:(hf + 1) * 512],
                                                                   op0=ALU.mult, op1=ALU.mult))(hf),
                 reads=[m_ps2[hf], stt, gain], writes=[outt])
        P.op(POOL, lambda e: e.tensor_tensor(out=outt[:], in0=outt[:], in1=hres[:], op=ALU.add), reads=[outt, hres], writes=[outt])
        P.dma(POOL, lambda e: e.dma_start(out=hout_ap, in_=outt[:]), reads=[outt], writes=[hout_tl])

    def phase_A():
        with ExitStack() as ph:
            g00 = load_gain(ph, 0, 0)
            w = sbt(ph, "A_w", [128, 8, WIN], BF16)
            load_w_bf16(w, e_w_in, 8, "w_in")
            wg = sbt(ph, "A_wg", [17, 256], BF16)
            P.dma(POOL, lambda e: e.dma_start(out=wg[0:16, :], in_=gla_w_gate), writes=[wg])
            P.dma(POOL, lambda e: e.dma_start(out=wg[16:17, :], in_=gla_b_gate.unsqueeze(0)), writes=[wg])
            U32 = sbt(ph, "A_U32", [128, 128], F32)
            L32 = sbt(ph, "A_L32", [128, 128], F32)
            U4 = sbt(ph, "A_U4", [128, 512], F32)
            P.dma(SP, lambda e: e.dma_start(out=U32[:], in_=cd["c_U"]), writes=[U32])
            P.dma(SP, lambda e: e.dma_start(out=L32[:], in_=cd["c_Ls"]), writes=[L32])
            P.dma(SP, lambda e: e.dma_start(out=U4[:], in_=cd["c_U4"]), writes=[U4])
            gnb = sbt(ph, "A_gnb", [128, 512], F32)
            P.dma(SP, lambda e: e.dma_start(out=gnb[:], in_=gla_norm.partition_broadcast(128)), writes=[gnb])
            ngb = sbt(ph, "A_ngb", [128, 24], F32)
            P.dma(SP, lambda e: e.dma_start(out=ngb[:], in_=nsa_gate_b.partition_broadcast(128)), writes=[ngb])

            xt_r = Rot([sbt(ph, f"A_xt{i}", [128, D], F32) for i in range(2)])
            stA_r = Rot([sbt(ph, f"A_stA{i}", [128, 4], F32) for i in range(4)])
            stG_r = Rot([sbt(ph, f"A_stG{i}", [128, 16], F32) for i in range(2)])
            xn_r = Rot([sbt(ph, f"A_xn{i}", [128, D], BF16) for i in range(3)])
            xnT_r = Rot([sbt(ph, f"A_xnT{i}", [128, 8, 512], BF16) for i in range(2)])
            qT = sbt(ph, "A_qT", [64, 4, 512], F32)
            kT = sbt(ph, "A_kT", [64, 4, 512], F32)
            glrT = sbt(ph, "A_glrT", [32, 512], BF16)
            P.op(DVE, lambda e: e.memset(glrT[:], 1.0), writes=[glrT])
            nq_st = Rot([sbt(ph, f"A_nq{i}", [128, 4, 512], BF16) for i in range(2)])
            kt_st = Rot([sbt(ph, f"A_kt{i}", [128, 4, 512], BF16) for i in range(2)])
            ktok_r = Rot([sbt(ph, f"A_ktok{i}", [128, 256], F32) for i in range(2)])
            v_r = Rot([sbt(ph, f"A_v{i}", [128, 512], BF16) for i in range(3)])
            sr_r = Rot([sbt(ph, f"A_sr{i}", [128, 512], F32) for i in range(3)])
            vt_r = Rot([sbt(ph, f"A_vt{i}", [128, 256], BF16) for i in range(2)])
            gt_r = Rot([sbt(ph, f"A_gt{i}", [128, 24], F32) for i in range(2)])
            ex_r = Rot([sbt(ph, f"A_ex{i}", [128, 256], F32) for i in range(2)])
            sp_r = Rot([sbt(ph, f"A_sp{i}", [128, 256], F32) for i in range(2)])
            eb_r = Rot([sbt(ph, f"A_eb{i}", [128, 256], F32) for i in range(2)])
            kend_r = Rot([sbt(ph, f"A_kend{i}", [128, 256], BF16) for i in range(2)])
            eq_r = Rot([sbt(ph, f"A_eq{i}", [64, 4, 128], F32) for i in range(2)])
            ek_r = Rot([sbt(ph, f"A_ek{i}", [64, 4, 128], F32) for i in range(2)])
            qtT_r = Rot([sbt(ph, f"A_qtT{i}", [128, 4, 128], BF16) for i in range(2)])
            ktT_r = Rot([sbt(ph, f"A_ktT{i}", [128, 4, 128], BF16) for i in range(2)])
            for t in qtT_r.tiles + ktT_r.tiles:
                P.op(DVE, (lambda t: lambda e: e.memset(t[:], 0.0))(t), writes=[t])
            AT_r = Rot([sbt(ph, f"A_AT{i}", [128, 512], BF16) for i in range(2)])
            S32 = sbt(ph, "A_S32", [64, 4, 128], F32)
            Sbf = sbt(ph, "A_Sbf", [128, 4, 128], BF16)
            P.op(DVE, lambda e: e.memset(S32[:], 0.0), writes=[S32])
            P.op(DVE, lambda e: e.memset(Sbf[:], 0.0), writes=[Sbf])
            on = sbt(ph, "A_on", [128, 512], F32)
            oa_r = Rot([sbt(ph, f"A_oa{i}", [128, 512], BF16) for i in range(2)])
            tp = pst(ph, "A_tp", [128, 1024], BF16)
            pp = Rot([pst(ph, f"A_pp{i}", [128, 512]) for i in range(7)])

            class Cx:
                pass

            def S1new(st):
                c = Cx()
                c.st = st
                c.xnT = xnT_r.get()
                c.xns = [None] * 4
                return c

            def S1a(c, tl):
                n = c.st * 4 + tl
                xt = xt_r.get()
                stt = stA_r.get()
                xn = xn_r.get()
                c.xns[tl] = xn
                P.dma(SP, lambda e: e.dma_start(out=xt[:], in_=x_d[n * 128:(n + 1) * 128, :]), reads=[Bx], writes=[xt])
                r = rms_rstd(stt, 0, xt[:], xt, xn, D, lnexp=True)
                P.op(DVE, lambda e: e.scalar_tensor_tensor(out=xn[:], in0=xt[:], scalar=r, in1=g00[:], op0=ALU.mult, op1=ALU.mult),
                     reads=[xt, stt, g00], writes=[xn])

            def S1b(c, tl):
                transpose_to(tp, c.xns[tl], 8, lambda: c.xnT[:, :, tl * 128:(tl + 1) * 128], c.xnT)

            def S2(c):
                st, xnT = c.st, c.xnT

                def fm_proj(col0, m):
                    ps = pp.get()
                    for kc in range(8):
                        P.op(PE, lambda e: e.matmul(ps[0:m, :], lhsT=w[:, kc, col0:col0 + m], rhs=xnT[:, kc, :], start=(kc == 0), stop=(kc == 7)),
                             reads=[w, xnT], writes=[ps])
                    return ps

                for h in range(4):
                    ps = fm_proj(OFF["gq"] + 64 * h, 64)
                    P.op(ACT, lambda e: e.mul(out=qT[:, h, :], in_=ps[0:64, :], mul=0.125), reads=[ps], writes=[qT])
                    ps = fm_proj(OFF["gk"] + 64 * h, 64)
                    P.op(DVE, lambda e: e.tensor_copy(out=kT[:, h, :], in_=ps[0:64, :]), reads=[ps], writes=[kT])
                ps = fm_proj(OFF["glr"], 16)
                P.op(DVE, lambda e: e.tensor_copy(out=glrT[0:16, :], in_=ps[0:16, :]), reads=[ps], writes=[glrT])
                nqs = nq_st.get()
                for hp in range(4):
                    ps = fm_proj(OFF["nq"] + 128 * hp, 128)
                    P.op(ACT, lambda e: e.mul(out=nqs[:, hp, :], in_=ps[:, :], mul=0.125), reads=[ps], writes=[nqs])
                P.dma(POOL, lambda e: e.dma_start(out=QT[:, :, st * 512:(st + 1) * 512].rearrange("(hp hh) d s -> (hh d) hp s", hh=2), in_=nqs[:]),
                      reads=[nqs], writes=[QT])
                kts = kt_st.get()
                for i, nm in enumerate(("kc", "vc", "ks", "kw")):
                    ps = fm_proj(OFF[nm], 128)
                    P.op(DVE, lambda e: e.tensor_copy(out=kts[:, i, :], in_=ps[:, :]), reads=[ps], writes=[kts])
                P.dma(POOL, lambda e: e.dma_start(out=KT[:, :, st * 512:(st + 1) * 512].rearrange("(i g) d s -> (g d) i s", g=2), in_=kts[:]),
                      reads=[kts], writes=[KT])

            def S3a(c, tl):
                t = Cx()
                t.n = c.st * 4 + tl
                t.tsl = slice(tl * 128, (tl + 1) * 128)
                t.xnT = c.xnT
                xnT, tsl = c.xnT, t.tsl

                def tm_proj(ps, pc0, col0, ncol):
                    for kc in range(8):
                        P.op(PE, lambda e: e.matmul(ps[:, pc0:pc0 + ncol], lhsT=xnT[:, kc, tsl], rhs=w[:, kc, col0:col0 + ncol], start=(kc == 0), stop=(kc == 7)),
                             reads=[w, xnT], writes=[ps])

                t.tm_proj = tm_proj
                t.ktok = ktok_r.get()
                ps = pp.get()
                tm_proj(ps, 0, OFF["gk"], 256)
                P.op(ACT, lambda e: e.copy(out=t.ktok[:], in_=ps[:, 0:256]), reads=[ps], writes=[t.ktok])
                t.v = v_r.get()
                ps = pp.get()
                tm_proj(ps, 0, OFF["gv"], 512)
                P.op(DVE, lambda e: e.tensor_copy(out=t.v[:], in_=ps[:]), reads=[ps], writes=[t.v])
                return t

            def S3b(t):
                n, tm_proj = t.n, t.tm_proj
                t.sr = sr_r.get()
                sr = t.sr
                ps = pp.get()
                tm_proj(ps, 0, OFF["gr"], 512)
                P.op(ACT, lambda e: e.activation(out=sr[:], in_=ps[:], func=AF.Exp, scale=-1.0), reads=[ps], writes=[sr])
                P.op(ACT, lambda e: e.activation(out=sr[:], in_=sr[:], func=AF.Ln, bias=1.0), reads=[sr], writes=[sr])
                P.op(ACT, lambda e: e.activation(out=sr[:], in_=sr[:], func=AF.Exp, scale=-1.0), reads=[sr], writes=[sr])
                P.op(DVE, lambda e: e.tensor_tensor(out=sr[:], in0=ps[:], in1=sr[:], op=ALU.mult), reads=[ps, sr], writes=[sr])
                ps = pp.get()
                tm_proj(ps, 0, OFF["vs"], 128)
                tm_proj(ps, 128, OFF["vw"], 128)
                tm_proj(ps, 256, OFF["ng"], 24)
                vt = vt_r.get()
                gt = gt_r.get()
                P.op(DVE, lambda e: e.tensor_copy(out=vt[:], in_=ps[:, 0:256]), reads=[ps], writes=[vt])
                P.op(DVE, lambda e: e.tensor_tensor(out=gt[:], in0=ps[:, 256:280], in1=ngb[:], op=ALU.add), reads=[ps, ngb], writes=[gt])
                P.op(ACT, lambda e: e.activation(out=gt[:], in_=gt[:], func=AF.Exp, scale=-1.0), reads=[gt], writes=[gt])
                P.op(DVE, lambda e: e.tensor_scalar(out=gt[:], in0=gt[:], scalar1=1.0, scalar2=None, op0=ALU.add), reads=[gt], writes=[gt])
                P.op(DVE, lambda e: e.reciprocal(out=gt[:], in_=gt[:]), reads=[gt], writes=[gt])
                P.dma(POOL, lambda e: e.dma_start(out=VT[n * 128:(n + 1) * 128, :], in_=vt[:]), reads=[vt], writes=[VT])
                P.dma(POOL, lambda e: e.dma_start(out=GT[n * 128:(n + 1) * 128, :], in_=gt[:]), reads=[gt], writes=[GT])

            def G1a(t):
                tsl = t.tsl
                t.ex, t.sp, t.eb = ex_r.get(), sp_r.get(), eb_r.get()
                t.kend, t.eq, t.ek = kend_r.get(), eq_r.get(), ek_r.get()
                t.qtT, t.ktT, t.AT = qtT_r.get(), ktT_r.get(), AT_r.get()
                ex, sp = t.ex, t.sp
                ps = pp.get()
                P.op(PE, lambda e: e.matmul(ps[:, 0:256], lhsT=glrT[0:17, tsl], rhs=wg[0:17, :], start=True, stop=True), reads=[glrT, wg], writes=[ps])
                P.op(ACT, lambda e: e.activation(out=ex[:], in_=ps[:, 0:256], func=AF.Exp, scale=-1.0), reads=[ps], writes=[ex])
                P.op(ACT, lambda e: e.activation(out=sp[:], in_=ex[:], func=AF.Ln, bias=1.0), reads=[ex], writes=[sp])

            def G1b(t):
                tsl, sp, eb, ek = t.tsl, t.sp, t.eb, t.ek
                ps = pp.get()
                P.op(PE, lambda e: e.matmul(ps[:, 0:256], lhsT=L32[:], rhs=sp[:], start=True, stop=True), reads=[L32, sp], writes=[ps])
                P.op(ACT, lambda e: e.activation(out=eb[:], in_=ps[:, 0:256], func=AF.Exp, scale=-1.0 / 16), reads=[ps], writes=[eb])
                P.op(DVE, lambda e: e.tensor_tensor(out=t.kend[:], in0=t.ktok[:], in1=eb[:], op=ALU.mult), reads=[t.ktok, eb], writes=[t.kend])
                ps = pp.get()
                for h in range(4):
                    P.op(PE, lambda e: e.matmul(ps[0:64, h * 128:(h + 1) * 128], lhsT=sp[:, 64 * h:64 * h + 64], rhs=U32[:], start=True, stop=True),
                         reads=[sp, U32], writes=[ps])
                P.op(ACT, lambda e: e.activation(out=t.eq[:].rearrange("p h t -> p (h t)"), in_=ps[0:64, :], func=AF.Exp, scale=-1.0 / 16), reads=[ps], writes=[t.eq])
                P.op(ACT, lambda e: e.activation(out=ek[:].rearrange("p h t -> p (h t)"), in_=ps[0:64, :], func=AF.Exp, scale=1.0 / 16), reads=[ps], writes=[ek])
                P.op(DVE, lambda e: e.tensor_tensor(out=t.qtT[0:64], in0=qT[:, :, tsl], in1=t.eq[:], op=ALU.mult), reads=[qT, t.eq], writes=[t.qtT])
                P.op(DVE, lambda e: e.tensor_tensor(out=t.ktT[0:64], in0=kT[:, :, tsl], in1=ek[:], op=ALU.mult), reads=[kT, ek], writes=[t.ktT])

            def G1c(t):
                ktT = t.ktT
                ps = pp.get()
                for h in range(4):
                    P.op(PE, lambda e: e.matmul(ps[:, h * 128:(h + 1) * 128], lhsT=ktT[:, h, :], rhs=t.qtT[:, h, :], start=True, stop=True),
                         reads=[ktT, t.qtT], writes=[ps])
                P.op(DVE, lambda e: e.tensor_tensor(out=t.AT[:], in0=ps[:], in1=U4[:], op=ALU.mult), reads=[ps, U4], writes=[t.AT])

            def G2(t):
                n, v, AT, qtT, kend, eq, sr = t.n, t.v, t.AT, t.qtT, t.kend, t.eq, t.sr
                po = pp.get()
                for h in range(4):
                    hs = slice(h * 128, (h + 1) * 128)
                    P.op(PE, lambda e: e.matmul(po[:, hs], lhsT=AT[:, hs], rhs=v[:, hs], start=True, stop=False), reads=[AT, v], writes=[po])
                    P.op(PE, lambda e: e.matmul(po[:, hs], lhsT=qtT[:, h, :], rhs=Sbf[:, h, :], start=False, stop=True), reads=[qtT, Sbf], writes=[po])
                pd = pp.get()
                for h in range(4):
                    hs = slice(h * 128, (h + 1) * 128)
                    P.op(PE, lambda e: e.matmul(pd[0:64, hs], lhsT=kend[:, 64 * h:64 * h + 64], rhs=v[:, hs], start=True, stop=True), reads=[kend, v], writes=[pd])
                for h in range(4):
                    hs = slice(h * 128, (h + 1) * 128)
                    P.op(DVE, lambda e: e.scalar_tensor_tensor(out=S32[:, h, :], in0=S32[:, h, :], scalar=eq[:, h, 127:128], in1=pd[0:64, hs],
                                                               op0=ALU.mult, op1=ALU.add), reads=[S32, eq, pd], writes=[S32])
                P.op(ACT, lambda e: e.copy(out=Sbf[0:64], in_=S32[:]), reads=[S32], writes=[Sbf])
                stt = stG_r.get()
                oa = oa_r.get()
                P.op(DVE, lambda e: e.memset(stt[:], 0.0), writes=[stt])
                for h in range(4):
                    hs = slice(h * 128, (h + 1) * 128)
                    P.op(ACT, lambda e: e.activation(out=oa[:, hs], in_=po[:, hs], func=AF.Square, accum_out=stt[:, h:h + 1]), reads=[po], writes=[oa, stt])
                P.op(DVE, lambda e: e.tensor_scalar(out=stt[:, 4:8], in0=stt[:, 0:4], scalar1=1.0 / 128, scalar2=EPS, op0=ALU.mult, op1=ALU.add),
                     reads=[stt], writes=[stt])
                P.op(ACT, lambda e: e.activation(out=stt[:, 4:8], in_=stt[:, 4:8], func=AF.Ln), reads=[stt], writes=[stt])
                P.op(ACT, lambda e: e.activation(out=stt[:, 8:12], in_=stt[:, 4:8], func=AF.Exp, scale=-0.5), reads=[stt], writes=[stt])
                for h in range(4):
                    hs = slice(h * 128, (h + 1) * 128)
                    P.op(DVE, lambda e: e.scalar_tensor_tensor(out=on[:, hs], in0=po[:, hs], scalar=stt[:, 8 + h:9 + h], in1=gnb[:, hs],
                                                               op0=ALU.mult, op1=ALU.mult), reads=[po, stt, gnb], writes=[on])
                P.op(DVE, lambda e: e.tensor_tensor(out=oa[:], in0=on[:], in1=sr[:], op=ALU.mult), reads=[on, sr], writes=[oa])
                P.dma(POOL, lambda e: e.dma_start(out=OA[n * 128:(n + 1) * 128, :], in_=oa[:]), reads=[oa], writes=[OA])

            cur = S1new(0)
            for tl in range(4):
                S1a(cur, tl)
                S1b(cur, tl)
            pend = None
            for st in range(NST):
                S2(cur)
                nxt = S1new(st + 1) if st + 1 < NST else None
                for tl in range(4):
                    if nxt is not None:
                        S1a(nxt, tl)
                    t = S3a(cur, tl)
                    G1a(t)
                    S3b(t)
                    if nxt is not None:
                        S1b(nxt, tl)
                    G1b(t)
                    if pend is not None:
                        G2(pend)
                    G1c(t)
                    pend = t
                cur = nxt
            G2(pend)
            P.barrier()
            P.emit()

    def phase_BC():
        with ExitStack() as ph:
            ones = sbt(ph, "C_ones", [128, 128], BF16)
            P.dma(POOL, lambda e: e.dma_start(out=ones[:], in_=cd["c_ones"]), writes=[ones])
            KCT = [sbt(ph, f"C_KCT{g}", [128, NCC * 128], BF16) for g in range(2)]
            VC = [sbt(ph, f"C_VC{g}", [128, NCC, 65], BF16) for g in range(2)]
            for g in range(2):
                P.op(DVE, (lambda g: lambda e: e.memset(KCT[g][:], 0.0))(g), writes=[KCT[g]])
                P.op(DVE, (lambda g: lambda e: e.memset(VC[g][:], 1.0))(g), writes=[VC[g]])
            with ExitStack() as pb:
                w1 = sbt(pb, "B_w1", [128, 2, 16, 128], BF16)
                w2 = sbt(pb, "B_w2", [128, 2, 64], BF16)
                posT = sbt(pb, "B_posT", [128, 2, 16], BF16)
                for i in range(2):
                    P.dma(POOL, (lambda i: lambda e: e.dma_start(out=w1[:, i, :, :], in_=nsa_cmp_w1[i].rearrange("(l q) h -> q l h", q=128)))(i),
                          writes=[w1])
                    P.dma(POOL, (lambda i: lambda e: e.dma_start(out=w2[:, i, :], in_=nsa_cmp_w2[i]))(i), writes=[w2])
                    P.dma(POOL, (lambda i: lambda e: e.dma_start(out=posT[:, i, :], in_=nsa_cmp_pos[i].rearrange("(l two) d -> (two d) l", two=2),
                                                                 allow_slow_non_contiguous=True))(i), writes=[posT])
                xT_r = Rot([sbt(pb, f"B_xT{i}", [128, S], BF16) for i in range(2)])
                for t in xT_r.tiles:
                    P.op(POOL, (lambda t: lambda e: e.memset(t[:], 0.0))(t), writes=[t])
                bias = sbt(pb, "B_bias", [128, 2], F32)
                gh_r = Rot([sbt(pb, f"B_gh{i}", [128, NCC * 128], BF16) for i in range(2)])
                for t in gh_r.tiles:
                    P.op(DVE, (lambda t: lambda e: e.memset(t[:], 0.0))(t), writes=[t])
                pp = Rot([pst(pb, f"B_pp{i}", [128, 512]) for i in range(4)])
                for i in range(2):
                    ps = pp.get()
                    for l in range(16):
                        P.op(PE, (lambda i, l, ps: lambda e: e.matmul(ps[:, 0:1], lhsT=w1[:, i, l, :], rhs=posT[:, i, l:l + 1],
                                                                      start=(l == 0), stop=(l == 15)))(i, l, ps), reads=[w1, posT], writes=[ps])
                    P.op(DVE, (lambda i, ps: lambda e: e.tensor_copy(out=bias[:, i:i + 1], in_=ps[:, 0:1]))(i, ps), reads=[ps], writes=[bias])
                for i in range(2):
                    for g in range(2):
                        xT = xT_r.get()
                        P.dma(SP, (lambda i, g, xT: lambda e: e.dma_start(out=xT[0:64, :], in_=KT[i * 2 + g, :, :]))(i, g, xT), reads=[KT], writes=[xT])
                        P.dma(SP, (lambda i, g, xT: lambda e: e.dma_start(out=xT[64:128, 0:S - 1], in_=KT[i * 2 + g, :, 1:S]))(i, g, xT), reads=[KT], writes=[xT])
                        gh = gh_r.get()
                        for c0 in range(0, NCMP, 512):
                            ncol = min(512, NCMP - c0)
                            ps = pp.get()
                            for l in range(16):
                                a0 = c0 * 16 + 2 * l
                                P.op(PE, (lambda i, l, ps, a0, ncol, xT: lambda e: e.matmul(
                                    ps[:, 0:ncol], lhsT=w1[:, i, l, :], rhs=xT[:, a0:a0 + 16 * (ncol - 1) + 1:16],
                                    start=(l == 0), stop=(l == 15)))(i, l, ps, a0, ncol, xT), reads=[w1, xT], writes=[ps])
                            P.op(ACT, (lambda i, ps, c0, ncol, gh: lambda e: e.activation(out=gh[:, c0:c0 + ncol], in_=ps[:, 0:ncol],
                                                                                          func=AF.Gelu_apprx_tanh, bias=bias[:, i:i + 1]))(i, ps, c0, ncol, gh),
                                 reads=[ps, bias], writes=[gh])
                        if i == 0:
                            for c0 in range(0, NCMP, 512):
                                ncol = min(512, NCMP - c0)
                                ps = pp.get()
                                P.op(PE, (lambda ps, c0, ncol, gh: lambda e: e.matmul(ps[0:64, 0:ncol], lhsT=w2[:, 0, :], rhs=gh[:, c0:c0 + ncol],
                                                                                      start=True, stop=True))(ps, c0, ncol, gh), reads=[w2, gh], writes=[ps])
                                P.op(DVE, (lambda g, ps, c0, ncol: lambda e: e.tensor_copy(out=KCT[g][0:64, c0:c0 + ncol], in_=ps[0:64, 0:ncol]))(g, ps, c0, ncol),
                                     reads=[ps], writes=[KCT[g]])
                        else:
                            ps = pp.get()
                            for cc in range(NCC):
                                P.op(PE, (lambda ps, cc, gh: lambda e: e.matmul(ps[:, cc * 64:(cc + 1) * 64], lhsT=gh[:, cc * 128:(cc + 1) * 128], rhs=w2[:, 1, :],
                                                                                start=True, stop=True))(ps, cc, gh), reads=[w2, gh], writes=[ps])
                            P.op(DVE, (lambda g, ps: lambda e: e.tensor_copy(out=VC[g][:, :, 0:64],
                                                                             in_=ps[:, 0:NCC * 64].rearrange("p (c d) -> p c d", c=NCC)))(g, ps),
                                 reads=[ps], writes=[VC[g]])
                P.barrier()
                P.emit()
            Gb = sbt(ph, "C_G", [128, S], BF16)
            OV = sbt(ph, "C_OV", [128, NCC, NSEL], BF16)
            CB4 = sbt(ph, "C_CB4", [128, 512], BF16)
            WB4 = sbt(ph, "C_WB4", [128, 512], BF16)
            TA = sbt(ph, "C_TA", [128, 2 * NSEL], F32)
            TB = sbt(ph, "C_TB", [128, 2 * NSEL], F32)
            P.dma(POOL, lambda e: e.dma_start(out=Gb[:], in_=cd["c_G"]), writes=[Gb])
            P.dma(POOL, lambda e: e.dma_start(out=OV[:], in_=cd["c_OV"]), writes=[OV])
            P.dma(POOL, lambda e: e.dma_start(out=CB4[:], in_=cd["c_CB4"]), writes=[CB4])
            P.dma(POOL, lambda e: e.dma_start(out=WB4[:], in_=cd["c_WB4"]), writes=[WB4])
            P.dma(SP, lambda e: e.dma_start(out=TA[:], in_=cd["c_TA"]), writes=[TA])
            P.dma(SP, lambda e: e.dma_start(out=TB[:], in_=cd["c_TB"]), writes=[TB])
            KS = sbt(ph, "C_KS", [128, S], BF16)
            KW = sbt(ph, "C_KW", [128, S], BF16)
            P.op(POOL, lambda e: e.memset(KW[:], 0.0), writes=[KW])
            P.dma(POOL, lambda e: e.dma_start(out=KS[64:128, :], in_=cd["c_Zaug"]), writes=[KS])
            VS = sbt(ph, "C_VS", [128, NT, 65], BF16)
            VW = sbt(ph, "C_VW", [128, NT, 65], BF16)
            P.op(POOL, lambda e: e.memset(VS[:], 1.0), writes=[VS])
            P.op(POOL, lambda e: e.memset(VW[:], 1.0), writes=[VW])
            NQ = 2 if NSEL > 64 else 1
            qa_r = Rot([sbt(ph, f"C_qa{i}", [128, NQ, 4, 128], BF16) for i in range(3)])
            for t in qa_r.tiles:
                P.op(POOL, (lambda t: lambda e: e.memset(t[:], 0.0))(t), writes=[t])
            gt_r = Rot([sbt(ph, f"C_gt{i}", [128, 24], F32) for i in range(3)])
            pc_r = Rot([sbt(ph, f"C_pc{i}", [128, 512], BF16) for i in range(2 * NCC + 1)])
            pe_r = Rot([sbt(ph, f"C_pe{i}", [128, 512], BF16) for i in range(6)])
            sc = sbt(ph, "C_sc", [128, NSEL], F32)
            wk = sbt(ph, "C_wk", [128, NSEL], F32)
            t8 = sbt(ph, "C_t8", [128, 16], F32)
            selm = sbt(ph, "C_selm", [128, NSEL], F32)
            okm = sbt(ph, "C_okm", [128, NSEL], F32)
            sb16 = sbt(ph, "C_sb16", [128, 256], BF16)
            P.op(DVE, lambda e: e.memset(sb16[:], 0.0), writes=[sb16])
            cf = sbt(ph, "C_cf", [128, 32], F32)
            acc_r = Rot([sbt(ph, f"C_acc{i}", [128, 4, 64], F32) for i in range(2)])
            ob_r = Rot([sbt(ph, f"C_ob{i}", [128, 4, 64], BF16) for i in range(2)])
            p_oc = pst(ph, "C_poc", [128, 512])
            p_os = pst(ph, "C_pos", [128, 512])
            p_ow = pst(ph, "C_pow", [128, 512])
            p_imp = pst(ph, "C_pimp", [128, 512])
            p_s = Rot([pst(ph, f"C_ps{i}", [128, 512]) for i in range(4)])
            tpv = p_ow[:, 264:392].bitcast(BF16)

            class Ctx:
                pass

            def pv(po, pt, vtile, vidx, first, last):
                for h in range(4):
                    P.op(PE, (lambda h: lambda e: e.matmul(po[:, h * 65:(h + 1) * 65], lhsT=pt[:, h * 128:(h + 1) * 128], rhs=vtile[:, vidx, :],
                                                           start=(first and h == 0), stop=last, skip_group_check=True))(h),
                         reads=[pt, vtile], writes=[po])

            def combine(c, po, x, firstb, lastb):
                g = c.g
                den = po[:, 0:260].rearrange("p (h c) -> p h c", h=4)[:, :, 64]
                rs_ = slice(8 * x, 8 * x + 4)
                cs = slice(8 * x + 4, 8 * x + 8)
                P.op(DVE, lambda e: e.tensor_scalar(out=cf[:, rs_], in0=den, scalar1=1e-30, scalar2=None, op0=ALU.max), reads=[po], writes=[cf])
                P.op(DVE, lambda e: e.reciprocal(out=cf[:, rs_], in_=cf[:, rs_]), reads=[cf], writes=[cf])
                gv = c.gt[:, 12 * g:12 * g + 12].rearrange("p (h x) -> p h x", x=3)[:, :, x]
                P.op(DVE, lambda e: e.tensor_tensor(out=cf[:, cs], in0=cf[:, rs_], in1=gv, op=ALU.mult), reads=[cf, c.gt], writes=[cf])
                for h in range(4):
                    src = po[:, h * 65:h * 65 + 64]
                    dst = c.ob[:, h, :] if lastb else c.acc[:, h, :]
                    col = 8 * x + 4 + h
                    if firstb:
                        P.op(DVE, lambda e: e.tensor_scalar(out=dst, in0=src, scalar1=cf[:, col:col + 1], scalar2=None, op0=ALU.mult),
                             reads=[po, cf], writes=[c.acc, c.ob])
                    else:
                        P.op(DVE, lambda e: e.scalar_tensor_tensor(out=dst, in0=src, scalar=cf[:, col:col + 1], in1=c.acc[:, h, :],
                                                                   op0=ALU.mult, op1=ALU.add), reads=[po, cf, c.acc], writes=[c.acc, c.ob])

            def comp_part(g, n):
                c = Ctx()
                c.g, c.n = g, n
                c.qa = qa_r.get()
                c.gt = gt_r.get()
                c.acc = acc_r.get()
                c.ob = ob_r.get()
                for qi in range(NQ if n >= 32 else 1):
                    P.dma(SP, (lambda qi: lambda e: e.dma_start(out=c.qa[0:64, qi], in_=QT[4 * g:4 * g + 4, :, n * 128:(n + 1) * 128].rearrange("h d s -> d h s")))(qi),
                          reads=[QT], writes=[c.qa])
                P.dma(SP, lambda e: e.dma_start(out=c.gt[:], in_=GT[n * 128:(n + 1) * 128, :]), reads=[GT], writes=[c.gt])
                c.q2 = [c.qa[:, qi].rearrange("p h t -> p (h t)") for qi in range(NQ)]
                c.ncc = min((128 * n + 127) // 2048 + 1, NCC)
                c.pcs = [None] * c.ncc
                return c

            def select_part(c):
                n, ncc = c.n, c.ncc
                for h in range(4):
                    for cc in range(ncc):
                        P.op(PE, lambda e: e.matmul(p_imp[:, h * NSEL:(h + 1) * NSEL], lhsT=c.pcs[cc][:, h * 128:(h + 1) * 128], rhs=OV[:, cc, :],
                                                    start=(cc == 0), stop=(cc == ncc - 1)), reads=[c.pcs[cc], OV], writes=[p_imp])
                combine(c, p_oc, 0, True, False)
                for h in range(4):
                    src = p_imp[:, h * NSEL:(h + 1) * NSEL]
                    if h == 0:
                        P.op(DVE, lambda e: e.tensor_scalar(out=sc[:], in0=src, scalar1=cf[:, 0:1], scalar2=None, op0=ALU.mult), reads=[p_imp, cf], writes=[sc])
                    else:
                        P.op(DVE, lambda e: e.scalar_tensor_tensor(out=sc[:], in0=src, scalar=cf[:, h:h + 1], in1=sc[:], op0=ALU.mult, op1=ALU.add),
                             reads=[p_imp, cf, sc], writes=[sc])
                w0 = NSEL - 2 * n
                P.op(DVE, lambda e: e.tensor_tensor(out=sc[:], in0=sc[:], in1=TA[:, w0:w0 + NSEL], op=ALU.mult), reads=[sc, TA], writes=[sc])
                P.op(DVE, lambda e: e.tensor_tensor(out=sc[:], in0=sc[:], in1=TB[:, w0:w0 + NSEL], op=ALU.add), reads=[sc, TB], writes=[sc])
                P.op(DVE, lambda e: e.memset(sc[:, 0:1], 1002.0), reads=[sc], writes=[sc])
                P.op(DVE, lambda e: e.max(out=t8[:, 0:8], in_=sc[:]), reads=[sc], writes=[t8])
                P.op(DVE, lambda e: e.match_replace(out=wk[:], in_to_replace=t8[:, 0:8], in_values=sc[:], imm_value=-1e9), reads=[sc, t8], writes=[wk])
                P.op(DVE, lambda e: e.max(out=t8[:, 8:16], in_=wk[:]), reads=[wk], writes=[t8])
                P.op(DVE, lambda e: e.tensor_scalar(out=selm[:], in0=sc[:], scalar1=t8[:, 15:16], scalar2=None, op0=ALU.is_ge), reads=[sc, t8], writes=[selm])
                P.op(DVE, lambda e: e.tensor_single_scalar(out=okm[:], in_=sc[:], scalar=-500.0, op=ALU.is_gt), reads=[sc], writes=[okm])
                P.op(DVE, lambda e: e.tensor_tensor(out=selm[:], in0=selm[:], in1=okm[:], op=ALU.mult), reads=[selm, okm], writes=[selm])
                lo = min(NSEL, 64)
                P.op(DVE, lambda e: e.tensor_scalar(out=sb16[:, 128 + 64:128 + 64 + lo], in0=selm[:, 0:lo], scalar1=-1.0, scalar2=-NEG, op0=ALU.add, op1=ALU.mult),
                     reads=[selm], writes=[sb16])
                P.op(PE, lambda e: e.transpose(out=tpv[:, 0:128], in_=sb16[:, 128:256], identity=ident[:]), reads=[sb16, ident], writes=[p_ow])
                if NSEL > 64:
                    P.op(DVE, lambda e: e.tensor_scalar(out=sb16[:, 64:NSEL], in0=selm[:, 64:NSEL], scalar1=-1.0, scalar2=-NEG, op0=ALU.add, op1=ALU.mult),
                         reads=[selm], writes=[sb16])
                    if n >= 32:
                        P.op(PE, lambda e: e.transpose(out=tpv[:, 128:256], in_=sb16[:, 0:128], identity=ident[:]), reads=[sb16, ident], writes=[p_ow])
                P.op(DVE, lambda e: e.tensor_copy(out=c.qa[64:128, 0], in_=tpv[64:128, 0:128].unsqueeze(1).to_broadcast([64, 4, 128])),
                     reads=[p_ow], writes=[c.qa])
                if NSEL > 64 and n >= 32:
                    P.op(DVE, lambda e: e.tensor_copy(out=c.qa[64:128, 1], in_=tpv[64:128, 128:256].unsqueeze(1).to_broadcast([64, 4, 128])),
                         reads=[p_ow], writes=[c.qa])

            def scores(item):
                kind, k, c = item
                n = c.n
                g = c.g
                ps = p_s.get()
                ksl = slice(k * 128, (k + 1) * 128)
                if kind == "c":
                    cc = k
                    off = 128 * n - 2048 * cc
                    need_mask = off < 2176
                    P.op(PE, lambda e: e.matmul(ps[:], lhsT=KCT[g][:, cc * 128:(cc + 1) * 128], rhs=c.q2[0], start=True, stop=(not need_mask)),
                         reads=[KCT[g], c.qa], writes=[ps])
                    if need_mask:
                        P.op(PE, lambda e: e.matmul(ps[:].rearrange("p (h t) -> p h t", h=4), lhsT=ident[:],
                                                    rhs=Gb[:, off:off + 128].unsqueeze(1).to_broadcast([128, 4, 128]), start=False, stop=True),
                             reads=[ident, Gb], writes=[ps])
                    pc = pc_r.get()
                    c.pcs[cc] = pc
                    P.op(ACT, lambda e: e.activation(out=pc[:], in_=ps[:], func=AF.Exp), reads=[ps], writes=[pc])
                    return pc
                if kind == "s":
                    q = c.q2[1] if k >= 32 else c.q2[0]
                    P.op(PE, lambda e: e.matmul(ps[:], lhsT=KS[:, ksl], rhs=q, start=True, stop=(k != n)), reads=[KS, c.qa], writes=[ps])
                    if k == n:
                        P.op(PE, lambda e: e.matmul(ps[:], lhsT=ident[:], rhs=CB4[:], start=False, stop=True), reads=[ident, CB4], writes=[ps])
                else:
                    edge = (k == n) or (k == n - 4)
                    P.op(PE, lambda e: e.matmul(ps[:], lhsT=KW[:, ksl], rhs=c.q2[0], start=True, stop=(not edge)), reads=[KW, c.qa], writes=[ps])
                    if k == n:
                        P.op(PE, lambda e: e.matmul(ps[:], lhsT=ident[:], rhs=CB4[:], start=False, stop=True), reads=[ident, CB4], writes=[ps])
                    elif k == n - 4:
                        P.op(PE, lambda e: e.matmul(ps[:], lhsT=ident[:], rhs=WB4[:], start=False, stop=True), reads=[ident, WB4], writes=[ps])
                pe = pe_r.get()
                P.op(ACT, lambda e: e.activation(out=pe[:], in_=ps[:], func=AF.Exp), reads=[ps], writes=[pe])
                return pe

            def finish(item, pe):
                kind, k, c = item
                n = c.n
                if kind == "c":
                    pv(p_oc, pe, VC[c.g], k, k == 0, k == c.ncc - 1)
                elif kind == "s":
                    pv(p_os, pe, VS, k, k == 0, k == n)
                    if k == n:
                        combine(c, p_os, 1, False, True)
                        P.dma(POOL, lambda e: e.dma_start(out=OB[n * 128:(n + 1) * 128, c.g * 256:(c.g + 1) * 256],
                                                          in_=c.ob[:].rearrange("p h d -> p (h d)")), reads=[c.ob], writes=[OB])
                else:
                    pv(p_ow, pe, VW, k, k == max(0, n - 4), k == n)
                    if k == n:
                        combine(c, p_ow, 2, False, False)

            LOOK = 3
            queue = []

            def push(item):
                queue.append((item, scores(item)))
                while len(queue) > LOOK:
                    it, pe = queue.pop(0)
                    finish(it, pe)

            def drain():
                while queue:
                    it, pe = queue.pop(0)
                    finish(it, pe)

            for g in range(2):
                P.dma(SP, (lambda g: lambda e: e.dma_start(out=KS[0:64, :], in_=KT[4 + g, :, :]))(g), reads=[KT], writes=[KS])
                P.dma(SP, (lambda g: lambda e: e.dma_start(out=KW[0:64, :], in_=KT[6 + g, :, :]))(g), reads=[KT], writes=[KW])
                P.dma(SP, (lambda g: lambda e: e.dma_start(out=VS[:, :, 0:64], in_=VT[:, g * 64:(g + 1) * 64].rearrange("(n p) d -> p n d", p=128)))(g),
                      reads=[VT], writes=[VS])
                P.dma(SP, (lambda g: lambda e: e.dma_start(out=VW[:, :, 0:64], in_=VT[:, 128 + g * 64:128 + (g + 1) * 64].rearrange("(n p) d -> p n d", p=128)))(g),
                      reads=[VT], writes=[VW])
                prev = None
                for n in range(NT + 1):
                    cur = comp_part(g, n) if n < NT else None
                    if cur is not None:
                        for cc in range(cur.ncc):
                            push(("c", cc, cur))
                    sel_items = [("s", k, prev) for k in range(prev.n + 1)] if prev is not None else []
                    half = len(sel_items) // 2
                    for it in sel_items[:half]:
                        push(it)
                    if cur is not None:
                        if any(it[0] == "c" and it[2] is cur for it, _ in queue):
                            drain()
                        select_part(cur)
                    for it in sel_items[half:]:
                        push(it)
                    if cur is not None:
                        for k in range(max(0, n - 4), n + 1):
                            push(("w", k, cur))
                    prev = cur
                drain()
            P.barrier()
            P.emit()

    def phase_D():
        with ExitStack() as ph:
            g01 = load_gain(ph, 0, 1)
            wo = sbt(ph, "D_wo", [128, 8, D], BF16)
            load_w_bf16(wo, e_w_out, 8, "wo")
            m_r = Rot([sbt(ph, f"D_m{i}", [128, D], BF16) for i in range(2)])
            mT_r = Rot([sbt(ph, f"D_mT{i}", [128, 8, 128], BF16) for i in range(2)])
            st_r = Rot([sbt(ph, f"D_st{i}", [128, 8], F32) for i in range(2)])
            hres_r = Rot([sbt(ph, f"D_hr{i}", [128, D], F32) for i in range(2)])
            out_r = Rot([sbt(ph, f"D_out{i}", [128, D], F32) for i in range(2)])
            tp = pst(ph, "D_tp", [128, 1024], BF16)
            pp = Rot([pst(ph, f"D_pp{i}", [128, 512]) for i in range(6)])
            def Da(n):
                m = m_r.get()
                rs = slice(n * 128, (n + 1) * 128)
                P.dma(SP, lambda e: e.dma_start(out=m[:, 0:512], in_=OA[rs, :]), reads=[OA], writes=[m])
                P.dma(SP, lambda e: e.dma_start(out=m[:, 512:1024], in_=OB[rs, :]), reads=[OB], writes=[m])
                mT = mT_r.get()
                transpose_to(tp, m, 8, lambda: mT[:], mT)
                return (n, mT)

            def Db(c):
                n, mT = c
                rs = slice(n * 128, (n + 1) * 128)
                ps2 = [pp.get(), pp.get()]
                for hf in range(2):
                    for kc in range(8):
                        P.op(PE, lambda e: e.matmul(ps2[hf][:], lhsT=mT[:, kc, :], rhs=wo[:, kc, hf * 512:(hf + 1) * 512], start=(kc == 0), stop=(kc == 7)),
                             reads=[mT, wo], writes=[ps2[hf]])
                outt = out_r.get()
                norm_residual_store(ph, ps2, g01, x_d[rs, :], Bx, H0[rs, :], H0, st_r.get(), outt, hres_r.get(), outt)

            cur = Da(0)
            for n in range(NT):
                nxt = Da(n + 1) if n + 1 < NT else None
                Db(cur)
                cur = nxt
            P.barrier()
            P.emit()

    def phase_FFN(layer, Hin, Hout_ap_fn, Hout_tl, tag):
        TS = 256
        NG = S // TS
        TPG = TS // 128
        with ExitStack() as ph:
            gpre = load_gain(ph, layer, 2)
            gpost = load_gain(ph, layer, 3)
            w1 = sbt(ph, tag + "_w1", [128, 8, 4096], BF16)
            w2 = sbt(ph, tag + "_w2", [128, 32, D], BF16)
            load_w_bf16(w1, ffn_w1[layer], 8, "w1")
            load_w_bf16(w2, ffn_w2[layer], 32, "w2")
            xt_r = Rot([sbt(ph, tag + f"_xt{i}", [128, D], F32) for i in range(2)])
            hr_r = Rot([sbt(ph, tag + f"_hr{i}", [128, D], F32) for i in range(1)])
            stA_r = Rot([sbt(ph, tag + f"_stA{i}", [128, 4], F32) for i in range(4)])
            stC_r = Rot([sbt(ph, tag + f"_stC{i}", [128, 8], F32) for i in range(2)])
            xn_r = Rot([sbt(ph, tag + f"_xn{i}", [128, D], BF16) for i in range(2)])
            xnT_r = Rot([sbt(ph, tag + f"_xnT{i}", [128, 8, TS], BF16) for i in range(2)])
            hT_r = Rot([sbt(ph, tag + f"_hT{i}", [128, 32, TS], BF16) for i in range(2)])
            rl_r = Rot([sbt(ph, tag + f"_rl{i}", [128, TS], F32) for i in range(3)])
            out_r = Rot([sbt(ph, tag + f"_out{i}", [128, D], F32) for i in range(2)])
            tp = pst(ph, tag + "_tp", [128, 1024], BF16)
            pp = Rot([pst(ph, tag + f"_pp{i}", [128, 512]) for i in range(7)])

            class Cx:
                pass

            def S1a(sg):
                c = Cx()
                c.sg = sg
                c.xnT = xnT_r.get()
                c.xns = []
                for tl in range(TPG):
                    n = sg * TPG + tl
                    rs = slice(n * 128, (n + 1) * 128)
                    xt = xt_r.get()
                    stt = stA_r.get()
                    xn = xn_r.get()
                    c.xns.append(xn)
                    P.dma(SP, lambda e: e.dma_start(out=xt[:], in_=Hin[rs, :]), reads=[Hin], writes=[xt])
                    r = rms_rstd(stt, 0, xt[:], xt, xn, D)
                    P.op(DVE, lambda e: e.scalar_tensor_tensor(out=xn[:], in0=xt[:], scalar=r, in1=gpre[:], op0=ALU.mult, op1=ALU.mult),
                         reads=[xt, stt, gpre], writes=[xn])
                return c

            def S1b(c):
                for tl in range(TPG):
                    transpose_to(tp, c.xns[tl], 8, lambda: c.xnT[:, :, tl * 128:(tl + 1) * 128], c.xnT)

            def S2(c):
                c.hT = hT_r.get()
                for fc in range(32):
                    ps = pp.get()
                    for kc in range(8):
                        P.op(PE, lambda e: e.matmul(ps[:, 0:TS], lhsT=w1[:, kc, fc * 128:(fc + 1) * 128], rhs=c.xnT[:, kc, :], start=(kc == 0), stop=(kc == 7)),
                             reads=[w1, c.xnT], writes=[ps])
                    rl = rl_r.get()
                    P.op(ACT, lambda e: e.activation(out=rl[:], in_=ps[:, 0:TS], func=AF.Relu), reads=[ps], writes=[rl])
                    P.op(DVE, lambda e: e.tensor_tensor(out=c.hT[:, fc, :], in0=rl[:], in1=rl[:], op=ALU.mult), reads=[rl], writes=[c.hT])

            def S3(c):
                for tl in range(TPG):
                    n = c.sg * TPG + tl
                    rs = slice(n * 128, (n + 1) * 128)
                    ps2 = [pp.get(), pp.get()]
                    for hf in range(2):
                        for fc in range(32):
                            P.op(PE, lambda e: e.matmul(ps2[hf][:], lhsT=c.hT[:, fc, tl * 128:(tl + 1) * 128], rhs=w2[:, fc, hf * 512:(hf + 1) * 512],
                                                        start=(fc == 0), stop=(fc == 31)), reads=[c.hT, w2], writes=[ps2[hf]])
                    outt = out_r.get()
                    norm_residual_store(ph, ps2, gpost, Hin[rs, :], Hin.b, Hout_ap_fn(rs), Hout_tl, stC_r.get(), outt, hr_r.get(), outt)

            cx = {0: S1a(0)}
            S1b(cx[0])
            for i in range(NG + 1):
                if i + 1 < NG:
                    cx[i + 1] = S1a(i + 1)
                if i < NG:
                    S2(cx[i])
                if i + 1 < NG:
                    S1b(cx[i + 1])
                if 0 <= i - 1 < NG:
                    S3(cx.pop(i - 1))
            P.barrier()
            P.emit()

    def phase_E():
        with ExitStack() as ph:
            g10 = load_gain(ph, 1, 0)
            g11 = load_gain(ph, 1, 1)
            wi = sbt(ph, "E_wi", [128, 8, 4096], BF16)
            wo = sbt(ph, "E_wo", [128, 16, D], BF16)
            load_w_bf16(wi, o_w_in, 8, "owi")
            load_w_bf16(wo, o_w_out, 16, "owo")
            U16 = sbt(ph, "E_U16", [128, 128], BF16)
            P.dma(POOL, lambda e: e.dma_start(out=U16[:], in_=cd["c_U"]), writes=[U16])
            wcT = sbt(ph, "E_wcT", [128, 8, 128], BF16)
            bs = sbt(ph, "E_bs", [128, 8], F32)
            P.dma(SP, lambda e: e.dma_start(out=bs[:], in_=o_b_s.rearrange("g t -> t g"), allow_slow_non_contiguous=True), writes=[bs])
            lng = sbt(ph, "E_lng", [128, 2048], F32)
            lnb = sbt(ph, "E_lnb", [128, 2048], F32)
            P.dma(SP, lambda e: e.dma_start(out=lng[:], in_=o_ln_g.partition_broadcast(128)), writes=[lng])
            P.dma(SP, lambda e: e.dma_start(out=lnb[:], in_=o_ln_b.partition_broadcast(128)), writes=[lnb])
            xt_r = Rot([sbt(ph, f"E_xt{i}", [128, D], F32) for i in range(2)])
            hr_r = Rot([sbt(ph, f"E_hr{i}", [128, D], F32) for i in range(1)])
            out_r = Rot([sbt(ph, f"E_out{i}", [128, D], F32) for i in range(1)])
            stA_r = Rot([sbt(ph, f"E_stA{i}", [128, 4], F32) for i in range(2)])
            stB_r = Rot([sbt(ph, f"E_stB{i}", [128, 16], F32) for i in range(3)])
            stC_r = Rot([sbt(ph, f"E_stC{i}", [128, 8], F32) for i in range(2)])
            xn_r = Rot([sbt(ph, f"E_xn{i}", [128, D], BF16) for i in range(2)])
            xnT_r = Rot([sbt(ph, f"E_xnT{i}", [128, 8, 128], BF16) for i in range(2)])
            u_r = Rot([sbt(ph, f"E_u{i}", [128, 2048], F32) for i in range(3)])
            v_r = Rot([sbt(ph, f"E_v{i}", [128, 2048], F32) for i in range(2)])
            vn_r = Rot([sbt(ph, f"E_vn{i}", [128, 2048], BF16) for i in range(2)])
            yb_r = Rot([sbt(ph, f"E_y{i}", [128, 2048], BF16) for i in range(1)])
            yT_r = Rot([sbt(ph, f"E_yT{i}", [128, 16, 128], BF16) for i in range(2)])
            tp = pst(ph, "E_tp", [128, 1024], BF16)
            pp = Rot([pst(ph, f"E_pp{i}", [128, 512]) for i in range(7)])
            ws = yb_r.tiles[0]
            P.dma(POOL, lambda e: e.dma_start(out=ws[:, 0:1024].rearrange("p (g s) -> p g s", g=8), in_=o_w_s.rearrange("g t s -> t g s")), writes=[ws])
            for g in range(8):
                P.op(PE, (lambda g: lambda e: e.transpose(out=tp[:, g * 128:(g + 1) * 128], in_=ws[:, g * 128:(g + 1) * 128], identity=ident[:]))(g),
                     reads=[ws, ident], writes=[tp])
            P.op(DVE, lambda e: e.tensor_tensor(out=wcT[:], in0=tp[:, 0:1024].rearrange("p (g t) -> p g t", g=8),
                                                in1=U16[:].unsqueeze(1).to_broadcast([128, 8, 128]), op=ALU.mult), reads=[tp, U16], writes=[wcT])

            class Cx:
                pass

            def F1a(n):
                c = Cx()
                c.n = n
                c.rs = slice(n * 128, (n + 1) * 128)
                xt = xt_r.get()
                stt = stA_r.get()
                c.xn = xn_r.get()
                c.xnT = xnT_r.get()
                P.dma(SP, lambda e: e.dma_start(out=xt[:], in_=H1[c.rs, :]), reads=[H1], writes=[xt])
                r = rms_rstd(stt, 0, xt[:], xt, c.xn, D)
                P.op(DVE, lambda e: e.scalar_tensor_tensor(out=c.xn[:], in0=xt[:], scalar=r, in1=g10[:], op0=ALU.mult, op1=ALU.mult),
                     reads=[xt, stt, g10], writes=[c.xn])
                return c

            def F1b(c):
                transpose_to(tp, c.xn, 8, lambda: c.xnT[:], c.xnT)

            def F2(c, part):
                if part == 0:
                    c.u = u_r.get()
                    c.v = v_r.get()
                    c.st = stB_r.get()
                    P.op(DVE, lambda e: e.memset(c.st[:], 0.0), writes=[c.st])
                for cg in range(4 * part, 4 * part + 4):
                    ps = pp.get()
                    for kc in range(8):
                        P.op(PE, lambda e: e.matmul(ps[:], lhsT=c.xnT[:, kc, :], rhs=wi[:, kc, cg * 512:(cg + 1) * 512], start=(kc == 0), stop=(kc == 7)),
                             reads=[wi, c.xnT], writes=[ps])
                    if cg < 4:
                        P.op(ACT, lambda e: e.activation(out=c.u[:, cg * 512:(cg + 1) * 512], in_=ps[:], func=AF.Gelu_apprx_tanh), reads=[ps], writes=[c.u])
                    else:
                        c2 = cg - 4
                        P.op(ACT, lambda e: e.activation(out=c.v[:, c2 * 512:(c2 + 1) * 512], in_=ps[:], func=AF.Gelu_apprx_tanh,
                                                         accum_out=c.st[:, 4 + c2:5 + c2]), reads=[ps], writes=[c.v, c.st])

            def L(c):
                stt, v32 = c.st, c.v
                c.vn = vn_r.get()
                vn = c.vn
                P.op(ACT, lambda e: e.activation(out=vn[:], in_=v32[:], func=AF.Square, accum_out=stt[:, 8:9]), reads=[v32], writes=[vn, stt])
                P.op(DVE, lambda e: e.tensor_tensor(out=stt[:, 9:11], in0=stt[:, 4:6], in1=stt[:, 6:8], op=ALU.add), reads=[stt], writes=[stt])
                P.op(DVE, lambda e: e.tensor_tensor(out=stt[:, 11:12], in0=stt[:, 9:10], in1=stt[:, 10:11], op=ALU.add), reads=[stt], writes=[stt])
                P.op(DVE, lambda e: e.tensor_scalar(out=stt[:, 12:13], in0=stt[:, 11:12], scalar1=1.0 / 2048, scalar2=None, op0=ALU.mult), reads=[stt], writes=[stt])
                P.op(DVE, lambda e: e.tensor_tensor(out=stt[:, 13:14], in0=stt[:, 12:13], in1=stt[:, 12:13], op=ALU.mult), reads=[stt], writes=[stt])
                P.op(DVE, lambda e: e.scalar_tensor_tensor(out=stt[:, 14:15], in0=stt[:, 8:9], scalar=1.0 / 2048, in1=stt[:, 13:14],
                                                           op0=ALU.mult, op1=ALU.subtract), reads=[stt], writes=[stt])
                P.op(ACT, lambda e: e.activation(out=stt[:, 15:16], in_=stt[:, 14:15], func=AF.Sqrt, bias=EPS), reads=[stt], writes=[stt])
                P.op(DVE, lambda e: e.reciprocal(out=stt[:, 15:16], in_=stt[:, 15:16]), reads=[stt], writes=[stt])
                P.op(DVE, lambda e: e.scalar_tensor_tensor(out=v32[:], in0=v32[:], scalar=stt[:, 12:13], in1=lng[:], op0=ALU.subtract, op1=ALU.mult),
                     reads=[v32, stt, lng], writes=[v32])
                P.op(DVE, lambda e: e.scalar_tensor_tensor(out=vn[:], in0=v32[:], scalar=stt[:, 15:16], in1=lnb[:], op0=ALU.mult, op1=ALU.add),
                     reads=[v32, stt, lnb], writes=[vn])

            def M1a(c):
                c.yb = yb_r.get()
                c.yT = yT_r.get()
                yb = c.yb
                for g2 in range(4):
                    ps = pp.get()
                    for gg in range(2):
                        g = g2 * 2 + gg
                        P.op(PE, lambda e: e.matmul(ps[:, gg * 256:(gg + 1) * 256], lhsT=wcT[:, g, :], rhs=c.vn[:, g * 256:(g + 1) * 256], start=True, stop=True),
                             reads=[wcT, c.vn], writes=[ps])
                    for gg in range(2):
                        g = g2 * 2 + gg
                        P.op(DVE, lambda e: e.scalar_tensor_tensor(out=yb[:, g * 256:(g + 1) * 256], in0=ps[:, gg * 256:(gg + 1) * 256], scalar=bs[:, g:g + 1],
                                                                   in1=c.u[:, g * 256:(g + 1) * 256], op0=ALU.add, op1=ALU.mult),
                             reads=[ps, bs, c.u], writes=[yb])

            def M1b(c):
                yb = c.yb
                for half in range(2):
                    for cc8 in range(8):
                        cc = half * 8 + cc8
                        P.op(PE, lambda e: e.transpose(out=tp[:, cc8 * 128:(cc8 + 1) * 128], in_=yb[:, cc * 128:(cc + 1) * 128], identity=ident[:]),
                             reads=[yb, ident], writes=[tp])
                    P.op(ACT, lambda e: e.copy(out=c.yT[:, half * 8:(half + 1) * 8, :], in_=tp[:, 0:1024].rearrange("p (c t) -> p c t", c=8)),
                         reads=[tp], writes=[c.yT])

            def M2a(c):
                c.ps2 = [pp.get(), pp.get()]
                for hf in range(2):
                    for cc in range(16):
                        P.op(PE, lambda e: e.matmul(c.ps2[hf][:], lhsT=c.yT[:, cc, :], rhs=wo[:, cc, hf * 512:(hf + 1) * 512], start=(cc == 0), stop=(cc == 15)),
                             reads=[c.yT, wo], writes=[c.ps2[hf]])

            def M2b(c):
                outt = out_r.get()
                norm_residual_store(ph, c.ps2, g11, H1[c.rs, :], H1.b, H2[c.rs, :], H2, stC_r.get(), outt, hr_r.get(), outt)

            cx = {}
            cx[0] = F1a(0)
            F1b(cx[0])
            F2(cx[0], 0)
            F2(cx[0], 1)
            if NT > 1:
                cx[1] = F1a(1)
                F1b(cx[1])
            for i in range(NT + 2):
                if i < NT:
                    L(cx[i])
                if i + 2 < NT:
                    cx[i + 2] = F1a(i + 2)
                if 0 <= i - 2 < NT:
                    M2a(cx[i - 2])
                    M2b(cx[i - 2])
                if 0 <= i - 1 < NT:
                    M1a(cx[i - 1])
                if i + 1 < NT:
                    F2(cx[i + 1], 0)
                if 0 <= i - 1 < NT:
                    M1b(cx[i - 1])
                if i + 1 < NT:
                    F2(cx[i + 1], 1)
                if i + 2 < NT:
                    F1b(cx[i + 2])
                cx.pop(i - 2, None)
            P.barrier()
            P.emit()

    if "A" in phases:
        phase_A()
    if "B" in phases:
        phase_BC()
    if "D" in phases:
        phase_D()
    if "F" in phases:
        phase_FFN(0, H0, lambda rs: H1[rs, :], H1, "F0")
    if "E" in phases:
        phase_E()
    if "G" in phases:
        ytl = Tl(y_d, "y")
        phase_FFN(1, H2, lambda rs: y_d[rs, :], ytl, "F1")
    P.barrier()
    P.emit()
    top.close()
    return nc, P


WEIGHT_NAMES = ["norm_g", "ffn_w1", "ffn_w2", "e_w_in", "e_w_out", "gla_w_gate", "gla_b_gate", "gla_norm", "nsa_gate_b", "nsa_cmp_pos",
                "nsa_cmp_w1", "nsa_cmp_w2", "o_w_in", "o_ln_g", "o_ln_b", "o_w_s", "o_b_s", "o_w_out"]


def prep_weights(inputs):
    m = {}
    for k in WEIGHT_NAMES:
        a = np.asarray(inputs[k], dtype=np.float32)
        if k in ("ffn_w1", "ffn_w2", "norm_g"):
            m[k] = np.ascontiguousarray(a)
        elif k == "gla_norm":
            m[k] = np.ascontiguousarray(a[0].reshape(512))
        else:
            m[k] = np.ascontiguousarray(a[0])
    return m


def kernel(**inputs):
    x = np.asarray(inputs["x"], dtype=np.float32)
    B, S, _ = x.shape
    nc, _ = build(S)
    wm = prep_weights(inputs)
    wm.update(make_consts(S))
    in_maps = [dict(wm, x=np.ascontiguousarray(x[b])) for b in range(B)]
    res = run_bass_kernel_spmd(nc, in_maps, core_ids=list(range(B)))
    return np.stack([r["y"] for r in res.results], axis=0).astype(np.float32)
```

```python
import math
from contextlib import ExitStack

import numpy as np
import concourse.bass as bass
import concourse.mybir as mybir
from concourse.bass_utils import run_bass_kernel_spmd

F32 = mybir.dt.float32
BF16 = mybir.dt.bfloat16
AF = mybir.ActivationFunctionType
ALU = mybir.AluOpType

PE, ACT, DVE, POOL, SP = "tensor", "scalar", "vector", "gpsimd", "sync"
COMPUTE = (PE, ACT, DVE, POOL)
ALLENG = (PE, ACT, DVE, POOL, SP)

D = 1024
NEG = -30000.0
EPS = 1e-6


class Buf:
    __slots__ = ("name", "last_w", "reads")

    def __init__(self, name):
        self.name = name
        self.last_w = None
        self.reads = {}


class Tl:
    __slots__ = ("t", "b")

    def __init__(self, t, name):
        self.t = t
        self.b = Buf(name)

    def __getitem__(self, k):
        return self.t[k]


class _Rec:
    def __init__(self):
        self.call = None

    def __getattr__(self, name):
        def f(*a, **k):
            self.call = (name, a, k)
        return f


def _record(fn):
    r = _Rec()
    fn(r)
    assert r.call is not None
    return r.call


class Prog:
    def __init__(self, nc, n_dma_sems=8):
        self.nc = nc
        self.ops = {e: [] for e in ALLENG}
        self.count = {e: 0 for e in COMPUTE}
        self.waited = {e: {} for e in ALLENG}
        self.n_dma_sems = n_dma_sems
        self.dma_issued = {}
        self.dma_rr = {e: 0 for e in (SP, POOL, ACT)}
        self.sems = {}
        self.sem_stack = ExitStack()
        self.ninstr = 0

    def _need(self, eng, ev, waits, strict=False):
        if ev is None:
            return
        key, val = ev
        if not strict:
            if key == eng and eng == PE:
                return
            if key == eng and val <= self.count[eng] - 3:
                return
        if self.waited[eng].get(key, 0) >= val:
            return
        self.waited[eng][key] = val
        waits.append((key, val))

    def _deps(self, eng, reads, writes, waits, skip_same_war):
        strict = not skip_same_war
        for b in reads:
            self._need(eng, b.last_w, waits, strict)
        for b in writes:
            self._need(eng, b.last_w, waits, strict)
            for k, v in b.reads.items():
                if skip_same_war and k == eng:
                    continue
                self._need(eng, (k, v), waits, strict)

    def _mark(self, ev, reads, writes):
        for b in reads:
            b.reads[ev[0]] = ev[1]
        for b in writes:
            b.last_w = ev
            b.reads = {}

    def op(self, eng, fn, reads=(), writes=()):
        reads = [r.b if isinstance(r, Tl) else r for r in reads]
        writes = [w.b if isinstance(w, Tl) else w for w in writes]
        waits = []
        self._deps(eng, reads, writes, waits, True)
        self.count[eng] += 1
        ev = (eng, self.count[eng])
        self._mark(ev, reads, writes)
        self.ops[eng].append((waits, _record(fn), (eng, 1)))
        self.ninstr += 1
        return ev

    def dma(self, eng, fn, reads=(), writes=()):
        reads = [r.b if isinstance(r, Tl) else r for r in reads]
        writes = [w.b if isinstance(w, Tl) else w for w in writes]
        slot = self.dma_rr[eng] % (48 if eng == POOL else self.n_dma_sems)
        self.dma_rr[eng] += 1
        key = ("dma", eng, slot)
        issued = self.dma_issued.get(key, 0)
        waits = []
        if issued:
            self._need(eng, (key, 16 * issued), waits, True)
        self._deps(eng, reads, writes, waits, False)
        self.dma_issued[key] = issued + 1
        ev = (key, 16 * (issued + 1))
        self._mark(ev, reads, writes)
        self.ops[eng].append((waits, _record(fn), (key, 16)))
        self.ninstr += 1
        return ev

    def barrier(self):
        evs = [(e, self.count[e]) for e in COMPUTE if self.count[e]]
        evs += [(k, 16 * n) for k, n in self.dma_issued.items()]
        for eng in ALLENG:
            waits = []
            for key, val in evs:
                if key == eng:
                    continue
                if self.waited[eng].get(key, 0) >= val:
                    continue
                self.waited[eng][key] = val
                waits.append((key, val))
            if waits:
                self.ops[eng].append((waits, None, None))

    def emit(self):
        nc = self.nc
        keys = set()
        for e in self.ops:
            for waits, fn, inc in self.ops[e]:
                for k, _ in waits:
                    keys.add(k)
                if inc is not None:
                    keys.add(inc[0])
        for k in sorted(keys, key=str):
            if k in self.sems:
                continue
            nm = "s_" + "_".join(str(x) for x in (k if isinstance(k, tuple) else (k,)))
            self.sems[k] = self.sem_stack.enter_context(nc.semaphore(nm))
        sems = self.sems
        with nc.Block() as block:
            def run(engname):
                def body(eng):
                    for waits, fn, inc in self.ops[engname]:
                        for k, v in waits:
                            eng.wait_ge(sems[k], v)
                        if fn is not None:
                            getattr(eng, fn[0])(*fn[1], **fn[2]).then_inc(sems[inc[0]], inc[1])
                return body

            if self.ops[SP]:
                block.sync(run(SP))
            if self.ops[PE]:
                block.tensor(run(PE))
            if self.ops[ACT]:
                block.scalar(run(ACT))
            if self.ops[DVE]:
                block.vector(run(DVE))
            if self.ops[POOL]:
                block.gpsimd(run(POOL))
        for e in self.ops:
            self.ops[e] = []


class Rot:
    def __init__(self, tiles):
        self.tiles = tiles
        self.i = 0

    def get(self):
        t = self.tiles[self.i % len(self.tiles)]
        self.i += 1
        return t


def make_consts(S):
    NSEL = S // 64
    NCMP = S // 16 - 1
    NCC = (NCMP + 127) // 128
    p = np.arange(128)
    c = {}
    c["c_ident"] = np.eye(128, dtype=np.float32)
    U = (p[:, None] <= p[None, :]).astype(np.float32)
    c["c_U"] = U
    c["c_Ls"] = (p[:, None] > p[None, :]).astype(np.float32)
    c["c_ones"] = np.ones((128, 128), np.float32)
    c["c_U4"] = np.tile(U, (1, 4))
    c["c_CB4"] = np.tile(np.where(p[:, None] <= p[None, :], 0.0, NEG).astype(np.float32), (1, 4))
    c["c_WB4"] = np.tile(np.where(p[:, None] > p[None, :], 0.0, NEG).astype(np.float32), (1, 4))
    u = np.arange(S)
    c["c_G"] = np.where(16 * p[:, None] + 31 <= u[None, :], 0.0, NEG).astype(np.float32)
    jb = np.arange(NSEL)
    r64 = np.arange(64)
    c["c_Zaug"] = ((u[None, :] // 64) % 64 == r64[:, None]).astype(np.float32)
    ov = np.zeros((128, NCC, NSEL), np.float32)
    for cc in range(NCC):
        cidx = cc * 128 + p
        m = (cidx[:, None] >= 4 * jb[None, :] - 1) & (cidx[:, None] <= 4 * jb[None, :] + 3) & (cidx[:, None] < NCMP)
        ov[:, cc, :] = m
    c["c_OV"] = ov
    w = np.arange(2 * NSEL)
    jr = w[None, :] - NSEL
    h = (p[:, None] >= 64).astype(np.int64)
    TA = (jr <= h - 2).astype(np.float32)
    TB = np.where(jr == h, 1000.0, np.where(jr == h - 1, 1001.0, np.where(jr <= h - 2, 0.0, -1000.0 - w[None, :])))
    c["c_TA"] = TA
    c["c_TB"] = TB.astype(np.float32)
    return c


OFF = {}
_o = 0
for _n, _sz in (("gq", 256), ("gk", 256), ("gv", 512), ("glr", 16), ("gr", 512), ("nq", 512), ("kc", 128), ("vc", 128),
                ("ks", 128), ("vs", 128), ("kw", 128), ("vw", 128), ("ng", 24)):
    OFF[_n] = _o
    _o += _sz
WIN = _o


def build(S, debug=False, phases="ABCDEFG"):
    NT = S // 128
    NST = S // 512
    NSEL = S // 64
    NCMP = S // 16 - 1
    NCC = (NCMP + 127) // 128
    nc = bass.Bass("TRN2", target_bir_lowering=False)
    P = Prog(nc)

    def din(name, shape):
        return nc.dram_tensor(name, list(shape), F32, kind="ExternalInput").ap()

    x_d = din("x", [S, D])
    norm_g = din("norm_g", [2, 4, D])
    ffn_w1 = din("ffn_w1", [2, D, 4096])
    ffn_w2 = din("ffn_w2", [2, 4096, D])
    e_w_in = din("e_w_in", [D, WIN])
    e_w_out = din("e_w_out", [D, D])
    gla_w_gate = din("gla_w_gate", [16, 256])
    gla_b_gate = din("gla_b_gate", [256])
    gla_norm = din("gla_norm", [512])
    nsa_gate_b = din("nsa_gate_b", [24])
    nsa_cmp_pos = din("nsa_cmp_pos", [2, 32, 64])
    nsa_cmp_w1 = din("nsa_cmp_w1", [2, 2048, 128])
    nsa_cmp_w2 = din("nsa_cmp_w2", [2, 128, 64])
    o_w_in = din("o_w_in", [D, 4096])
    o_ln_g = din("o_ln_g", [2048])
    o_ln_b = din("o_ln_b", [2048])
    o_w_s = din("o_w_s", [8, 128, 128])
    o_b_s = din("o_b_s", [8, 128])
    o_w_out = din("o_w_out", [2048, D])
    cshapes = {k: v.shape for k, v in make_consts(S).items()}
    cd = {k: din(k, shp) for k, shp in cshapes.items()}

    y_d = nc.dram_tensor("y", [S, D], F32, kind="ExternalOutput").ap()

    def scratch(name, shape, dt):
        kind = "ExternalOutput" if debug else "Internal"
        t = Tl(None, name)
        t.t = nc.dram_tensor(name, list(shape), dt, kind=kind).ap()
        return t

    QT = scratch("QT", [8, 64, S], BF16)
    KT = scratch("KT", [8, 64, S], BF16)
    VT = scratch("VT", [S, 256], BF16)
    GT = scratch("GT", [S, 24], F32)
    OA = scratch("OA", [S, 512], BF16)
    OB = scratch("OB", [S, 512], BF16)
    H0 = scratch("H0", [S, D], F32)
    H1 = scratch("H1", [S, D], F32)
    H2 = scratch("H2", [S, D], F32)
    Bx = Buf("x")
    By = Buf("y")

    top = ExitStack()

    def sbt(st, name, shape, dt):
        return Tl(st.enter_context(nc.sbuf_tensor(name, list(shape), dt)), name)

    def pst(st, name, shape, dt=F32):
        return Tl(st.enter_context(nc.psum_tensor(name, list(shape), dt)), name)

    ident = sbt(top, "ident", [128, 128], BF16)
    P.dma(POOL, lambda e: e.dma_start(out=ident[:], in_=cd["c_ident"]), writes=[ident])

    def load_gain(st, l, j):
        t = sbt(st, f"gain{l}{j}", [128, D], F32)
        P.dma(SP, lambda e: e.dma_start(out=t[:], in_=norm_g[l, j].partition_broadcast(128)), writes=[t])
        return t

    def rms_rstd(st_tile, col, src_ap, src_tl, junk, n, lnexp=False):
        if lnexp:
            P.op(DVE, lambda e: e.memset(st_tile[:], 0.0), writes=[st_tile])
            P.op(ACT, lambda e: e.activation(out=junk[:, 0:n], in_=src_ap, func=AF.Square, accum_out=st_tile[:, col:col + 1]),
                 reads=[src_tl], writes=[junk, st_tile])
            P.op(DVE, lambda e: e.tensor_scalar(out=st_tile[:, col + 1:col + 2], in0=st_tile[:, col:col + 1], scalar1=1.0 / n, scalar2=EPS,
                                                op0=ALU.mult, op1=ALU.add), reads=[st_tile], writes=[st_tile])
            P.op(ACT, lambda e: e.activation(out=st_tile[:, col + 1:col + 2], in_=st_tile[:, col + 1:col + 2], func=AF.Ln), reads=[st_tile], writes=[st_tile])
            P.op(ACT, lambda e: e.activation(out=st_tile[:, col + 2:col + 3], in_=st_tile[:, col + 1:col + 2], func=AF.Exp, scale=-0.5),
                 reads=[st_tile], writes=[st_tile])
            return st_tile[:, col + 2:col + 3]
        P.op(DVE, lambda e: e.memset(st_tile[:], 0.0), writes=[st_tile])
        P.op(ACT, lambda e: e.activation(out=junk[:, 0:n], in_=src_ap, func=AF.Square, accum_out=st_tile[:, col:col + 1]),
             reads=[src_tl], writes=[junk, st_tile])
        P.op(ACT, lambda e: e.activation(out=st_tile[:, col + 1:col + 2], in_=st_tile[:, col:col + 1], func=AF.Sqrt,
                                         scale=1.0 / n, bias=EPS), reads=[st_tile], writes=[st_tile])
        P.op(DVE, lambda e: e.reciprocal(out=st_tile[:, col + 2:col + 3], in_=st_tile[:, col + 1:col + 2]),
             reads=[st_tile], writes=[st_tile])
        return st_tile[:, col + 2:col + 3]

    def transpose_to(tp, src, nchunk, dst_ap_fn, dst_tl):
        for c in range(nchunk):
            P.op(PE, (lambda c: lambda e: e.transpose(out=tp[:, c * 128:(c + 1) * 128], in_=src[:, c * 128:(c + 1) * 128],
                                                      identity=ident[:]))(c), reads=[src, ident], writes=[tp])
        P.op(ACT, lambda e: e.copy(out=dst_ap_fn(), in_=tp[:, 0:nchunk * 128].rearrange("p (c t) -> p c t", c=nchunk)),
             reads=[tp], writes=[dst_tl])

    def load_w_bf16(dst, src_ap, nk, name):
        v = src_ap.rearrange("(k p) n -> p k n", p=128)
        for k in range(0, nk, 8):
            k1 = min(nk, k + 8)
            P.dma(POOL, (lambda k, k1: lambda e: e.dma_start(out=dst[:, k:k1, :], in_=v[:, k:k1, :]))(k, k1), writes=[dst])

    def norm_residual_store(ph, m_ps2, gain, hin_ap, hin_buf, hout_ap, hout_tl, stt, junk, hres, outt):
        P.op(DVE, lambda e: e.memset(stt[:], 0.0), writes=[stt])
        for hf in range(2):
            P.op(ACT, (lambda hf: lambda e: e.activation(out=junk[:, 0:512], in_=m_ps2[hf][:], func=AF.Square,
                                                         accum_out=stt[:, hf:hf + 1]))(hf),
                 reads=[m_ps2[hf]], writes=[junk, stt])
        P.op(DVE, lambda e: e.tensor_tensor(out=stt[:, 2:3], in0=stt[:, 0:1], in1=stt[:, 1:2], op=ALU.add), reads=[stt], writes=[stt])
        P.op(ACT, lambda e: e.activation(out=stt[:, 3:4], in_=stt[:, 2:3], func=AF.Sqrt, scale=1.0 / D, bias=EPS),
             reads=[stt], writes=[stt])
        P.op(DVE, lambda e: e.reciprocal(out=stt[:, 4:5], in_=stt[:, 3:4]), reads=[stt], writes=[stt])
        P.dma(SP, lambda e: e.dma_start(out=hres[:], in_=hin_ap), reads=[hin_buf], writes=[hres])
        for hf in range(2):
            P.op(DVE, (lambda hf: lambda e: e.scalar_tensor_tensor(out=outt[:, hf * 512:(hf + 1) * 512], in0=m_ps2[hf][:],
                                                                   scalar=stt[:, 4:5], in1=gain[:, hf * 512:(hf + 1) * 512],
                                                                   op0=ALU.mult, op1=ALU.mult))(hf),
                 reads=[m_ps2[hf], stt, gain], writes=[outt])
        P.op(POOL, lambda e: e.tensor_tensor(out=outt[:], in0=outt[:], in1=hres[:], op=ALU.add), reads=[outt, hres], writes=[outt])
        P.dma(POOL, lambda e: e.dma_start(out=hout_ap, in_=outt[:]), reads=[outt], writes=[hout_tl])

    def phase_A():
        with ExitStack() as ph:
            g00 = load_gain(ph, 0, 0)
            w = sbt(ph, "A_w", [128, 8, WIN], BF16)
            load_w_bf16(w, e_w_in, 8, "w_in")
            wg = sbt(ph, "A_wg", [17, 256], BF16)
            P.dma(POOL, lambda e: e.dma_start(out=wg[0:16, :], in_=gla_w_gate), writes=[wg])
            P.dma(POOL, lambda e: e.dma_start(out=wg[16:17, :], in_=gla_b_gate.unsqueeze(0)), writes=[wg])
            U32 = sbt(ph, "A_U32", [128, 128], F32)
            L32 = sbt(ph, "A_L32", [128, 128], F32)
            U4 = sbt(ph, "A_U4", [128, 512], F32)
            P.dma(SP, lambda e: e.dma_start(out=U32[:], in_=cd["c_U"]), writes=[U32])
            P.dma(SP, lambda e: e.dma_start(out=L32[:], in_=cd["c_Ls"]), writes=[L32])
            P.dma(SP, lambda e: e.dma_start(out=U4[:], in_=cd["c_U4"]), writes=[U4])
            gnb = sbt(ph, "A_gnb", [128, 512], F32)
            P.dma(SP, lambda e: e.dma_start(out=gnb[:], in_=gla_norm.partition_broadcast(128)), writes=[gnb])
            ngb = sbt(ph, "A_ngb", [128, 24], F32)
            P.dma(SP, lambda e: e.dma_start(out=ngb[:], in_=nsa_gate_b.partition_broadcast(128)), writes=[ngb])

            xt_r = Rot([sbt(ph, f"A_xt{i}", [128, D], F32) for i in range(2)])
            stA_r = Rot([sbt(ph, f"A_stA{i}", [128, 4], F32) for i in range(4)])
            stG_r = Rot([sbt(ph, f"A_stG{i}", [128, 16], F32) for i in range(2)])
            xn_r = Rot([sbt(ph, f"A_xn{i}", [128, D], BF16) for i in range(3)])
            xnT_r = Rot([sbt(ph, f"A_xnT{i}", [128, 8, 512], BF16) for i in range(2)])
            qT = sbt(ph, "A_qT", [64, 4, 512], F32)
            kT = sbt(ph, "A_kT", [64, 4, 512], F32)
            glrT = sbt(ph, "A_glrT", [32, 512], BF16)
            P.op(DVE, lambda e: e.memset(glrT[:], 1.0), writes=[glrT])
            nq_st = Rot([sbt(ph, f"A_nq{i}", [128, 4, 512], BF16) for i in range(2)])
            kt_st = Rot([sbt(ph, f"A_kt{i}", [128, 4, 512], BF16) for i in range(2)])
            ktok_r = Rot([sbt(ph, f"A_ktok{i}", [128, 256], F32) for i in range(2)])
            v_r = Rot([sbt(ph, f"A_v{i}", [128, 512], BF16) for i in range(3)])
            sr_r = Rot([sbt(ph, f"A_sr{i}", [128, 512], F32) for i in range(3)])
            vt_r = Rot([sbt(ph, f"A_vt{i}", [128, 256], BF16) for i in range(2)])
            gt_r = Rot([sbt(ph, f"A_gt{i}", [128, 24], F32) for i in range(2)])
            ex_r = Rot([sbt(ph, f"A_ex{i}", [128, 256], F32) for i in range(2)])
            sp_r = Rot([sbt(ph, f"A_sp{i}", [128, 256], F32) for i in range(2)])
            eb_r = Rot([sbt(ph, f"A_eb{i}", [128, 256], F32) for i in range(2)])
            kend_r = Rot([sbt(ph, f"A_kend{i}", [128, 256], BF16) for i in range(2)])
            eq_r = Rot([sbt(ph, f"A_eq{i}", [64, 4, 128], F32) for i in range(2)])
            ek_r = Rot([sbt(ph, f"A_ek{i}", [64, 4, 128], F32) for i in range(2)])
            qtT_r = Rot([sbt(ph, f"A_qtT{i}", [128, 4, 128], BF16) for i in range(2)])
            ktT_r = Rot([sbt(ph, f"A_ktT{i}", [128, 4, 128], BF16) for i in range(2)])
            for t in qtT_r.tiles + ktT_r.tiles:
                P.op(DVE, (lambda t: lambda e: e.memset(t[:], 0.0))(t), writes=[t])
            AT_r = Rot([sbt(ph, f"A_AT{i}", [128, 512], BF16) for i in range(2)])
            S32 = sbt(ph, "A_S32", [64, 4, 128], F32)
            Sbf = sbt(ph, "A_Sbf", [128, 4, 128], BF16)
            P.op(DVE, lambda e: e.memset(S32[:], 0.0), writes=[S32])
            P.op(DVE, lambda e: e.memset(Sbf[:], 0.0), writes=[Sbf])
            on = sbt(ph, "A_on", [128, 512], F32)
            oa_r = Rot([sbt(ph, f"A_oa{i}", [128, 512], BF16) for i in range(2)])
            tp = pst(ph, "A_tp", [128, 1024], BF16)
            pp = Rot([pst(ph, f"A_pp{i}", [128, 512]) for i in range(7)])

            class Cx:
                pass

            def S1new(st):
                c = Cx()
                c.st = st
                c.xnT = xnT_r.get()
                c.xns = [None] * 4
                return c

            def S1a(c, tl):
                n = c.st * 4 + tl
                xt = xt_r.get()
                stt = stA_r.get()
                xn = xn_r.get()
                c.xns[tl] = xn
                P.dma(SP, lambda e: e.dma_start(out=xt[:], in_=x_d[n * 128:(n + 1) * 128, :]), reads=[Bx], writes=[xt])
                r = rms_rstd(stt, 0, xt[:], xt, xn, D, lnexp=True)
                P.op(DVE, lambda e: e.scalar_tensor_tensor(out=xn[:], in0=xt[:], scalar=r, in1=g00[:], op0=ALU.mult, op1=ALU.mult),
                     reads=[xt, stt, g00], writes=[xn])

            def S1b(c, tl):
                transpose_to(tp, c.xns[tl], 8, lambda: c.xnT[:, :, tl * 128:(tl + 1) * 128], c.xnT)

            def S2(c):
                st, xnT = c.st, c.xnT

                def fm_proj(col0, m):
                    ps = pp.get()
                    for kc in range(8):
                        P.op(PE, lambda e: e.matmul(ps[0:m, :], lhsT=w[:, kc, col0:col0 + m], rhs=xnT[:, kc, :], start=(kc == 0), stop=(kc == 7)),
                             reads=[w, xnT], writes=[ps])
                    return ps

                for h in range(4):
                    ps = fm_proj(OFF["gq"] + 64 * h, 64)
                    P.op(ACT, lambda e: e.mul(out=qT[:, h, :], in_=ps[0:64, :], mul=0.125), reads=[ps], writes=[qT])
                    ps = fm_proj(OFF["gk"] + 64 * h, 64)
                    P.op(DVE, lambda e: e.tensor_copy(out=kT[:, h, :], in_=ps[0:64, :]), reads=[ps], writes=[kT])
                ps = fm_proj(OFF["glr"], 16)
                P.op(DVE, lambda e: e.tensor_copy(out=glrT[0:16, :], in_=ps[0:16, :]), reads=[ps], writes=[glrT])
                nqs = nq_st.get()
                for hp in range(4):
                    ps = fm_proj(OFF["nq"] + 128 * hp, 128)
                    P.op(ACT, lambda e: e.mul(out=nqs[:, hp, :], in_=ps[:, :], mul=0.125), reads=[ps], writes=[nqs])
                P.dma(POOL, lambda e: e.dma_start(out=QT[:, :, st * 512:(st + 1) * 512].rearrange("(hp hh) d s -> (hh d) hp s", hh=2), in_=nqs[:]),
                      reads=[nqs], writes=[QT])
                kts = kt_st.get()
                for i, nm in enumerate(("kc", "vc", "ks", "kw")):
                    ps = fm_proj(OFF[nm], 128)
                    P.op(DVE, lambda e: e.tensor_copy(out=kts[:, i, :], in_=ps[:, :]), reads=[ps], writes=[kts])
                P.dma(POOL, lambda e: e.dma_start(out=KT[:, :, st * 512:(st + 1) * 512].rearrange("(i g) d s -> (g d) i s", g=2), in_=kts[:]),
                      reads=[kts], writes=[KT])

            def S3a(c, tl):
                t = Cx()
                t.n = c.st * 4 + tl
                t.tsl = slice(tl * 128, (tl + 1) * 128)
                t.xnT = c.xnT
                xnT, tsl = c.xnT, t.tsl

                def tm_proj(ps, pc0, col0, ncol):
                    for kc in range(8):
                        P.op(PE, lambda e: e.matmul(ps[:, pc0:pc0 + ncol], lhsT=xnT[:, kc, tsl], rhs=w[:, kc, col0:col0 + ncol], start=(kc == 0), stop=(kc == 7)),
                             reads=[w, xnT], writes=[ps])

                t.tm_proj = tm_proj
                t.ktok = ktok_r.get()
                ps = pp.get()
                tm_proj(ps, 0, OFF["gk"], 256)
                P.op(ACT, lambda e: e.copy(out=t.ktok[:], in_=ps[:, 0:256]), reads=[ps], writes=[t.ktok])
                t.v = v_r.get()
                ps = pp.get()
                tm_proj(ps, 0, OFF["gv"], 512)
                P.op(DVE, lambda e: e.tensor_copy(out=t.v[:], in_=ps[:]), reads=[ps], writes=[t.v])
                return t

            def S3b(t):
                n, tm_proj = t.n, t.tm_proj
                t.sr = sr_r.get()
                sr = t.sr
                ps = pp.get()
                tm_proj(ps, 0, OFF["gr"], 512)
                P.op(ACT, lambda e: e.activation(out=sr[:], in_=ps[:], func=AF.Exp, scale=-1.0), reads=[ps], writes=[sr])
                P.op(ACT, lambda e: e.activation(out=sr[:], in_=sr[:], func=AF.Ln, bias=1.0), reads=[sr], writes=[sr])
                P.op(ACT, lambda e: e.activation(out=sr[:], in_=sr[:], func=AF.Exp, scale=-1.0), reads=[sr], writes=[sr])
                P.op(DVE, lambda e: e.tensor_tensor(out=sr[:], in0=ps[:], in1=sr[:], op=ALU.mult), reads=[ps, sr], writes=[sr])
                ps = pp.get()
                tm_proj(ps, 0, OFF["vs"], 128)
                tm_proj(ps, 128, OFF["vw"], 128)
                tm_proj(ps, 256, OFF["ng"], 24)
                vt = vt_r.get()
                gt = gt_r.get()
                P.op(DVE, lambda e: e.tensor_copy(out=vt[:], in_=ps[:, 0:256]), reads=[ps], writes=[vt])
                P.op(DVE, lambda e: e.tensor_tensor(out=gt[:], in0=ps[:, 256:280], in1=ngb[:], op=ALU.add), reads=[ps, ngb], writes=[gt])
                P.op(ACT, lambda e: e.activation(out=gt[:], in_=gt[:], func=AF.Exp, scale=-1.0), reads=[gt], writes=[gt])
                P.op(DVE, lambda e: e.tensor_scalar(out=gt[:], in0=gt[:], scalar1=1.0, scalar2=None, op0=ALU.add), reads=[gt], writes=[gt])
                P.op(DVE, lambda e: e.reciprocal(out=gt[:], in_=gt[:]), reads=[gt], writes=[gt])
                P.dma(POOL, lambda e: e.dma_start(out=VT[n * 128:(n + 1) * 128, :], in_=vt[:]), reads=[vt], writes=[VT])
                P.dma(POOL, lambda e: e.dma_start(out=GT[n * 128:(n + 1) * 128, :], in_=gt[:]), reads=[gt], writes=[GT])

            def G1a(t):
                tsl = t.tsl
                t.ex, t.sp, t.eb = ex_r.get(), sp_r.get(), eb_r.get()
                t.kend, t.eq, t.ek = kend_r.get(), eq_r.get(), ek_r.get()
                t.qtT, t.ktT, t.AT = qtT_r.get(), ktT_r.get(), AT_r.get()
                ex, sp = t.ex, t.sp
                ps = pp.get()
                P.op(PE, lambda e: e.matmul(ps[:, 0:256], lhsT=glrT[0:17, tsl], rhs=wg[0:17, :], start=True, stop=True), reads=[glrT, wg], writes=[ps])
                P.op(ACT, lambda e: e.activation(out=ex[:], in_=ps[:, 0:256], func=AF.Exp, scale=-1.0), reads=[ps], writes=[ex])
                P.op(ACT, lambda e: e.activation(out=sp[:], in_=ex[:], func=AF.Ln, bias=1.0), reads=[ex], writes=[sp])

            def G1b(t):
                tsl, sp, eb, ek = t.tsl, t.sp, t.eb, t.ek
                ps = pp.get()
                P.op(PE, lambda e: e.matmul(ps[:, 0:256], lhsT=L32[:], rhs=sp[:], start=True, stop=True), reads=[L32, sp], writes=[ps])
                P.op(ACT, lambda e: e.activation(out=eb[:], in_=ps[:, 0:256], func=AF.Exp, scale=-1.0 / 16), reads=[ps], writes=[eb])
                P.op(DVE, lambda e: e.tensor_tensor(out=t.kend[:], in0=t.ktok[:], in1=eb[:], op=ALU.mult), reads=[t.ktok, eb], writes=[t.kend])
                ps = pp.get()
                for h in range(4):
                    P.op(PE, lambda e: e.matmul(ps[0:64, h * 128:(h + 1) * 128], lhsT=sp[:, 64 * h:64 * h + 64], rhs=U32[:], start=True, stop=True),
                         reads=[sp, U32], writes=[ps])
                P.op(ACT, lambda e: e.activation(out=t.eq[:].rearrange("p h t -> p (h t)"), in_=ps[0:64, :], func=AF.Exp, scale=-1.0 / 16), reads=[ps], writes=[t.eq])
                P.op(ACT, lambda e: e.activation(out=ek[:].rearrange("p h t -> p (h t)"), in_=ps[0:64, :], func=AF.Exp, scale=1.0 / 16), reads=[ps], writes=[ek])
                P.op(DVE, lambda e: e.tensor_tensor(out=t.qtT[0:64], in0=qT[:, :, tsl], in1=t.eq[:], op=ALU.mult), reads=[qT, t.eq], writes=[t.qtT])
                P.op(DVE, lambda e: e.tensor_tensor(out=t.ktT[0:64], in0=kT[:, :, tsl], in1=ek[:], op=ALU.mult), reads=[kT, ek], writes=[t.ktT])

            def G1c(t):
                ktT = t.ktT
                ps = pp.get()
                for h in range(4):
                    P.op(PE, lambda e: e.matmul(ps[:, h * 128:(h + 1) * 128], lhsT=ktT[:, h, :], rhs=t.qtT[:, h, :], start=True, stop=True),
                         reads=[ktT, t.qtT], writes=[ps])
                P.op(DVE, lambda e: e.tensor_tensor(out=t.AT[:], in0=ps[:], in1=U4[:], op=ALU.mult), reads=[ps, U4], writes=[t.AT])

            def G2(t):
                n, v, AT, qtT, kend, eq, sr = t.n, t.v, t.AT, t.qtT, t.kend, t.eq, t.sr
                po = pp.get()
                for h in range(4):
                    hs = slice(h * 128, (h + 1) * 128)
                    P.op(PE, lambda e: e.matmul(po[:, hs], lhsT=AT[:, hs], rhs=v[:, hs], start=True, stop=False), reads=[AT, v], writes=[po])
                    P.op(PE, lambda e: e.matmul(po[:, hs], lhsT=qtT[:, h, :], rhs=Sbf[:, h, :], start=False, stop=True), reads=[qtT, Sbf], writes=[po])
                pd = pp.get()
                for h in range(4):
                    hs = slice(h * 128, (h + 1) * 128)
                    P.op(PE, lambda e: e.matmul(pd[0:64, hs], lhsT=kend[:, 64 * h:64 * h + 64], rhs=v[:, hs], start=True, stop=True), reads=[kend, v], writes=[pd])
                for h in range(4):
                    hs = slice(h * 128, (h + 1) * 128)
                    P.op(DVE, lambda e: e.scalar_tensor_tensor(out=S32[:, h, :], in0=S32[:, h, :], scalar=eq[:, h, 127:128], in1=pd[0:64, hs],
                                                               op0=ALU.mult, op1=ALU.add), reads=[S32, eq, pd], writes=[S32])
                P.op(ACT, lambda e: e.copy(out=Sbf[0:64], in_=S32[:]), reads=[S32], writes=[Sbf])
                stt = stG_r.get()
                oa = oa_r.get()
                P.op(DVE, lambda e: e.memset(stt[:], 0.0), writes=[stt])
                for h in range(4):
                    hs = slice(h * 128, (h + 1) * 128)
                    P.op(ACT, lambda e: e.activation(out=oa[:, hs], in_=po[:, hs], func=AF.Square, accum_out=stt[:, h:h + 1]), reads=[po], writes=[oa, stt])
                P.op(DVE, lambda e: e.tensor_scalar(out=stt[:, 4:8], in0=stt[:, 0:4], scalar1=1.0 / 128, scalar2=EPS, op0=ALU.mult, op1=ALU.add),
                     reads=[stt], writes=[stt])
                P.op(ACT, lambda e: e.activation(out=stt[:, 4:8], in_=stt[:, 4:8], func=AF.Ln), reads=[stt], writes=[stt])
                P.op(ACT, lambda e: e.activation(out=stt[:, 8:12], in_=stt[:, 4:8], func=AF.Exp, scale=-0.5), reads=[stt], writes=[stt])
                for h in range(4):
                    hs = slice(h * 128, (h + 1) * 128)
                    P.op(DVE, lambda e: e.scalar_tensor_tensor(out=on[:, hs], in0=po[:, hs], scalar=stt[:, 8 + h:9 + h], in1=gnb[:, hs],
                                                               op0=ALU.mult, op1=ALU.mult), reads=[po, stt, gnb], writes=[on])
                P.op(DVE, lambda e: e.tensor_tensor(out=oa[:], in0=on[:], in1=sr[:], op=ALU.mult), reads=[on, sr], writes=[oa])
                P.dma(POOL, lambda e: e.dma_start(out=OA[n * 128:(n + 1) * 128, :], in_=oa[:]), reads=[oa], writes=[OA])

            cur = S1new(0)
            for tl in range(4):
                S1a(cur, tl)
                S1b(cur, tl)
            pend = None
            for st in range(NST):
                S2(cur)
                nxt = S1new(st + 1) if st + 1 < NST else None
                for tl in range(4):
                    if nxt is not None:
                        S1a(nxt, tl)
                    t = S3a(cur, tl)
                    G1a(t)
                    S3b(t)
                    if nxt is not None:
                        S1b(nxt, tl)
                    G1b(t)
                    if pend is not None:
                        G2(pend)
                    G1c(t)
                    pend = t
                cur = nxt
            G2(pend)
            P.barrier()
            P.emit()

    def phase_BC():
        with ExitStack() as ph:
            ones = sbt(ph, "C_ones", [128, 128], BF16)
            P.dma(POOL, lambda e: e.dma_start(out=ones[:], in_=cd["c_ones"]), writes=[ones])
            KCT = [sbt(ph, f"C_KCT{g}", [128, NCC * 128], BF16) for g in range(2)]
            VC = [sbt(ph, f"C_VC{g}", [128, NCC, 65], BF16) for g in range(2)]
            for g in range(2):
                P.op(DVE, (lambda g: lambda e: e.memset(KCT[g][:], 0.0))(g), writes=[KCT[g]])
                P.op(DVE, (lambda g: lambda e: e.memset(VC[g][:], 1.0))(g), writes=[VC[g]])
            Gb = sbt(ph, "C_G", [128, S], BF16)
            OV = sbt(ph, "C_OV", [128, NCC, NSEL], BF16)
            CB4 = sbt(ph, "C_CB4", [128, 512], BF16)
            WB4 = sbt(ph, "C_WB4", [128, 512], BF16)
            TA = sbt(ph, "C_TA", [128, 2 * NSEL], F32)
            TB = sbt(ph, "C_TB", [128, 2 * NSEL], F32)
            KS = sbt(ph, "C_KS", [128, S], BF16)
            KW = sbt(ph, "C_KW", [128, S], BF16)
            VS = sbt(ph, "C_VS", [128, NT, 65], BF16)
            VW = sbt(ph, "C_VW", [128, NT, 65], BF16)
            NQ = 2 if NSEL > 64 else 1
            qa_r = Rot([sbt(ph, f"C_qa{i}", [128, NQ, 4, 128], BF16) for i in range(3)])

            def c_loads():
                P.dma(POOL, lambda e: e.dma_start(out=Gb[:], in_=cd["c_G"]), writes=[Gb])
                P.dma(POOL, lambda e: e.dma_start(out=OV[:], in_=cd["c_OV"]), writes=[OV])
                P.dma(POOL, lambda e: e.dma_start(out=CB4[:], in_=cd["c_CB4"]), writes=[CB4])
                P.dma(POOL, lambda e: e.dma_start(out=WB4[:], in_=cd["c_WB4"]), writes=[WB4])
                P.dma(SP, lambda e: e.dma_start(out=TA[:], in_=cd["c_TA"]), writes=[TA])
                P.dma(SP, lambda e: e.dma_start(out=TB[:], in_=cd["c_TB"]), writes=[TB])
                P.op(POOL, lambda e: e.memset(KW[:], 0.0), writes=[KW])
                P.dma(POOL, lambda e: e.dma_start(out=KS[64:128, :], in_=cd["c_Zaug"]), writes=[KS])
                P.op(POOL, lambda e: e.memset(VS[:], 1.0), writes=[VS])
                P.op(POOL, lambda e: e.memset(VW[:], 1.0), writes=[VW])
                for t in qa_r.tiles:
                    P.op(POOL, (lambda t: lambda e: e.memset(t[:], 0.0))(t), writes=[t])
                P.dma(SP, lambda e: e.dma_start(out=KS[0:64, :], in_=KT[4, :, :]), reads=[KT], writes=[KS])
                P.dma(SP, lambda e: e.dma_start(out=KW[0:64, :], in_=KT[6, :, :]), reads=[KT], writes=[KW])
                P.dma(SP, lambda e: e.dma_start(out=VS[:, :, 0:64], in_=VT[:, 0:64].rearrange("(n p) d -> p n d", p=128)), reads=[VT], writes=[VS])
                P.dma(SP, lambda e: e.dma_start(out=VW[:, :, 0:64], in_=VT[:, 128:192].rearrange("(n p) d -> p n d", p=128)), reads=[VT], writes=[VW])

            with ExitStack() as pb:
                w1 = sbt(pb, "B_w1", [128, 2, 16, 128], BF16)
                w2 = sbt(pb, "B_w2", [128, 2, 64], BF16)
                posT = sbt(pb, "B_posT", [128, 2, 16], BF16)
                for i in range(2):
                    P.dma(POOL, (lambda i: lambda e: e.dma_start(out=w1[:, i, :, :], in_=nsa_cmp_w1[i].rearrange("(l q) h -> q l h", q=128)))(i),
                          writes=[w1])
                    P.dma(POOL, (lambda i: lambda e: e.dma_start(out=w2[:, i, :], in_=nsa_cmp_w2[i]))(i), writes=[w2])
                    P.dma(POOL, (lambda i: lambda e: e.dma_start(out=posT[:, i, :], in_=nsa_cmp_pos[i].rearrange("(l two) d -> (two d) l", two=2),
                                                                 allow_slow_non_contiguous=True))(i), writes=[posT])
                c_loads()
                xT_r = Rot([sbt(pb, f"B_xT{i}", [128, S], BF16) for i in range(2)])
                for t in xT_r.tiles:
                    P.op(POOL, (lambda t: lambda e: e.memset(t[:], 0.0))(t), writes=[t])
                bias = sbt(pb, "B_bias", [128, 2], F32)
                gh_r = Rot([sbt(pb, f"B_gh{i}", [128, NCC * 128], BF16) for i in range(2)])
                for t in gh_r.tiles:
                    P.op(DVE, (lambda t: lambda e: e.memset(t[:], 0.0))(t), writes=[t])
                pp = Rot([pst(pb, f"B_pp{i}", [128, 512]) for i in range(4)])
                for i in range(2):
                    ps = pp.get()
                    for l in range(16):
                        P.op(PE, (lambda i, l, ps: lambda e: e.matmul(ps[:, 0:1], lhsT=w1[:, i, l, :], rhs=posT[:, i, l:l + 1],
                                                                      start=(l == 0), stop=(l == 15)))(i, l, ps), reads=[w1, posT], writes=[ps])
                    P.op(DVE, (lambda i, ps: lambda e: e.tensor_copy(out=bias[:, i:i + 1], in_=ps[:, 0:1]))(i, ps), reads=[ps], writes=[bias])
                for i in range(2):
                    for g in range(2):
                        xT = xT_r.get()
                        P.dma(SP, (lambda i, g, xT: lambda e: e.dma_start(out=xT[0:64, :], in_=KT[i * 2 + g, :, :]))(i, g, xT), reads=[KT], writes=[xT])
                        P.dma(SP, (lambda i, g, xT: lambda e: e.dma_start(out=xT[64:128, 0:S - 1], in_=KT[i * 2 + g, :, 1:S]))(i, g, xT), reads=[KT], writes=[xT])
                        gh = gh_r.get()
                        for c0 in range(0, NCMP, 512):
                            ncol = min(512, NCMP - c0)
                            ps = pp.get()
                            for l in range(16):
                                a0 = c0 * 16 + 2 * l
                                P.op(PE, (lambda i, l, ps, a0, ncol, xT: lambda e: e.matmul(
                                    ps[:, 0:ncol], lhsT=w1[:, i, l, :], rhs=xT[:, a0:a0 + 16 * (ncol - 1) + 1:16],
                                    start=(l == 0), stop=(l == 15)))(i, l, ps, a0, ncol, xT), reads=[w1, xT], writes=[ps])
                            P.op(ACT, (lambda i, ps, c0, ncol, gh: lambda e: e.activation(out=gh[:, c0:c0 + ncol], in_=ps[:, 0:ncol],
                                                                                          func=AF.Gelu_apprx_tanh, bias=bias[:, i:i + 1]))(i, ps, c0, ncol, gh),
                                 reads=[ps, bias], writes=[gh])
                        if i == 0:
                            for c0 in range(0, NCMP, 512):
                                ncol = min(512, NCMP - c0)
                                ps = pp.get()
                                P.op(PE, (lambda ps, c0, ncol, gh: lambda e: e.matmul(ps[0:64, 0:ncol], lhsT=w2[:, 0, :], rhs=gh[:, c0:c0 + ncol],
                                                                                      start=True, stop=True))(ps, c0, ncol, gh), reads=[w2, gh], writes=[ps])
                                P.op(DVE, (lambda g, ps, c0, ncol: lambda e: e.tensor_copy(out=KCT[g][0:64, c0:c0 + ncol], in_=ps[0:64, 0:ncol]))(g, ps, c0, ncol),
                                     reads=[ps], writes=[KCT[g]])
                        else:
                            ps = pp.get()
                            for cc in range(NCC):
                                P.op(PE, (lambda ps, cc, gh: lambda e: e.matmul(ps[:, cc * 64:(cc + 1) * 64], lhsT=gh[:, cc * 128:(cc + 1) * 128], rhs=w2[:, 1, :],
                                                                                start=True, stop=True))(ps, cc, gh), reads=[w2, gh], writes=[ps])
                            P.op(DVE, (lambda g, ps: lambda e: e.tensor_copy(out=VC[g][:, :, 0:64],
                                                                             in_=ps[:, 0:NCC * 64].rearrange("p (c d) -> p c d", c=NCC)))(g, ps),
                                 reads=[ps], writes=[VC[g]])
                P.barrier()
                P.emit()
            gt_r = Rot([sbt(ph, f"C_gt{i}", [128, 24], F32) for i in range(3)])
            pc_r = Rot([sbt(ph, f"C_pc{i}", [128, 512], BF16) for i in range(2 * NCC + 1)])
            pe_r = Rot([sbt(ph, f"C_pe{i}", [128, 512], BF16) for i in range(5)])
            sc = sbt(ph, "C_sc", [128, NSEL], F32)
            wk = sbt(ph, "C_wk", [128, NSEL], F32)
            t8 = sbt(ph, "C_t8", [128, 16], F32)
            selm = sbt(ph, "C_selm", [128, NSEL], F32)
            okm = sbt(ph, "C_okm", [128, NSEL], F32)
            sb16 = sbt(ph, "C_sb16", [128, 256], BF16)
            P.op(DVE, lambda e: e.memset(sb16[:], 0.0), writes=[sb16])
            cf = sbt(ph, "C_cf", [128, 32], F32)
            acc_r = Rot([sbt(ph, f"C_acc{i}", [128, 4, 64], F32) for i in range(2)])
            ob_r = Rot([sbt(ph, f"C_ob{i}", [128, 4, 64], BF16) for i in range(2)])
            p_oc = pst(ph, "C_poc", [128, 512])
            p_os = pst(ph, "C_pos", [128, 512])
            p_ow = pst(ph, "C_pow", [128, 512])
            p_imp = pst(ph, "C_pimp", [128, 512])
            p_tp = pst(ph, "C_ptp", [128, 1024], BF16)
            p_s = Rot([pst(ph, f"C_ps{i}", [128, 512]) for i in range(3)])

            class Ctx:
                pass

            def pv(po, pt, vtile, vidx, first, last):
                for h in range(4):
                    P.op(PE, (lambda h: lambda e: e.matmul(po[:, h * 65:(h + 1) * 65], lhsT=pt[:, h * 128:(h + 1) * 128], rhs=vtile[:, vidx, :],
                                                           start=(first and h == 0), stop=last, skip_group_check=True))(h),
                         reads=[pt, vtile], writes=[po])

            def combine(c, po, x, firstb, lastb):
                g = c.g
                den = po[:, 0:260].rearrange("p (h c) -> p h c", h=4)[:, :, 64]
                rs_ = slice(8 * x, 8 * x + 4)
                cs = slice(8 * x + 4, 8 * x + 8)
                P.op(DVE, lambda e: e.tensor_scalar(out=cf[:, rs_], in0=den, scalar1=1e-30, scalar2=None, op0=ALU.max), reads=[po], writes=[cf])
                P.op(DVE, lambda e: e.reciprocal(out=cf[:, rs_], in_=cf[:, rs_]), reads=[cf], writes=[cf])
                gv = c.gt[:, 12 * g:12 * g + 12].rearrange("p (h x) -> p h x", x=3)[:, :, x]
                P.op(DVE, lambda e: e.tensor_tensor(out=cf[:, cs], in0=cf[:, rs_], in1=gv, op=ALU.mult), reads=[cf, c.gt], writes=[cf])
                for h in range(4):
                    src = po[:, h * 65:h * 65 + 64]
                    dst = c.ob[:, h, :] if lastb else c.acc[:, h, :]
                    col = 8 * x + 4 + h
                    if firstb:
                        P.op(DVE, lambda e: e.tensor_scalar(out=dst, in0=src, scalar1=cf[:, col:col + 1], scalar2=None, op0=ALU.mult),
                             reads=[po, cf], writes=[c.acc, c.ob])
                    else:
                        P.op(DVE, lambda e: e.scalar_tensor_tensor(out=dst, in0=src, scalar=cf[:, col:col + 1], in1=c.acc[:, h, :],
                                                                   op0=ALU.mult, op1=ALU.add), reads=[po, cf, c.acc], writes=[c.acc, c.ob])

            def comp_part(g, n):
                c = Ctx()
                c.g, c.n = g, n
                c.qa = qa_r.get()
                c.gt = gt_r.get()
                c.acc = acc_r.get()
                c.ob = ob_r.get()
                for qi in range(NQ if n >= 32 else 1):
                    P.dma(SP, (lambda qi: lambda e: e.dma_start(out=c.qa[0:64, qi], in_=QT[4 * g:4 * g + 4, :, n * 128:(n + 1) * 128].rearrange("h d s -> d h s")))(qi),
                          reads=[QT], writes=[c.qa])
                P.dma(SP, lambda e: e.dma_start(out=c.gt[:], in_=GT[n * 128:(n + 1) * 128, :]), reads=[GT], writes=[c.gt])
                c.q2 = [c.qa[:, qi].rearrange("p h t -> p (h t)") for qi in range(NQ)]
                c.ncc = min((128 * n + 127) // 2048 + 1, NCC)
                c.pcs = [None] * c.ncc
                return c

            def select_part(c):
                n, ncc = c.n, c.ncc
                for h in range(4):
                    for cc in range(ncc):
                        P.op(PE, lambda e: e.matmul(p_imp[:, h * NSEL:(h + 1) * NSEL], lhsT=c.pcs[cc][:, h * 128:(h + 1) * 128], rhs=OV[:, cc, :],
                                                    start=(cc == 0), stop=(cc == ncc - 1)), reads=[c.pcs[cc], OV], writes=[p_imp])
                combine(c, p_oc, 0, True, False)
                for h in range(4):
                    src = p_imp[:, h * NSEL:(h + 1) * NSEL]
                    if h == 0:
                        P.op(DVE, lambda e: e.tensor_scalar(out=sc[:], in0=src, scalar1=cf[:, 0:1], scalar2=None, op0=ALU.mult), reads=[p_imp, cf], writes=[sc])
                    else:
                        P.op(DVE, lambda e: e.scalar_tensor_tensor(out=sc[:], in0=src, scalar=cf[:, h:h + 1], in1=sc[:], op0=ALU.mult, op1=ALU.add),
                             reads=[p_imp, cf, sc], writes=[sc])
                w0 = NSEL - 2 * n
                P.op(DVE, lambda e: e.tensor_tensor(out=sc[:], in0=sc[:], in1=TA[:, w0:w0 + NSEL], op=ALU.mult), reads=[sc, TA], writes=[sc])
                P.op(DVE, lambda e: e.tensor_tensor(out=sc[:], in0=sc[:], in1=TB[:, w0:w0 + NSEL], op=ALU.add), reads=[sc, TB], writes=[sc])
                P.op(DVE, lambda e: e.memset(sc[:, 0:1], 1002.0), reads=[sc], writes=[sc])
                P.op(DVE, lambda e: e.max(out=t8[:, 0:8], in_=sc[:]), reads=[sc], writes=[t8])
                P.op(DVE, lambda e: e.match_replace(out=wk[:], in_to_replace=t8[:, 0:8], in_values=sc[:], imm_value=-1e9), reads=[sc, t8], writes=[wk])
                P.op(DVE, lambda e: e.max(out=t8[:, 8:16], in_=wk[:]), reads=[wk], writes=[t8])
                P.op(DVE, lambda e: e.tensor_scalar(out=selm[:], in0=sc[:], scalar1=t8[:, 15:16], scalar2=None, op0=ALU.is_ge), reads=[sc, t8], writes=[selm])
                P.op(DVE, lambda e: e.tensor_single_scalar(out=okm[:], in_=sc[:], scalar=-500.0, op=ALU.is_gt), reads=[sc], writes=[okm])
                P.op(DVE, lambda e: e.tensor_tensor(out=selm[:], in0=selm[:], in1=okm[:], op=ALU.mult), reads=[selm, okm], writes=[selm])
                lo = min(NSEL, 64)
                P.op(DVE, lambda e: e.tensor_scalar(out=sb16[:, 128 + 64:128 + 64 + lo], in0=selm[:, 0:lo], scalar1=-1.0, scalar2=-NEG, op0=ALU.add, op1=ALU.mult),
                     reads=[selm], writes=[sb16])
                P.op(PE, lambda e: e.transpose(out=p_tp[:, 0:128], in_=sb16[:, 128:256], identity=ident[:]), reads=[sb16, ident], writes=[p_tp])
                if NSEL > 64:
                    P.op(DVE, lambda e: e.tensor_scalar(out=sb16[:, 64:NSEL], in0=selm[:, 64:NSEL], scalar1=-1.0, scalar2=-NEG, op0=ALU.add, op1=ALU.mult),
                         reads=[selm], writes=[sb16])
                    if n >= 32:
                        P.op(PE, lambda e: e.transpose(out=p_tp[:, 128:256], in_=sb16[:, 0:128], identity=ident[:]), reads=[sb16, ident], writes=[p_tp])
                P.op(DVE, lambda e: e.tensor_copy(out=c.qa[64:128, 0], in_=p_tp[64:128, 0:128].unsqueeze(1).to_broadcast([64, 4, 128])),
                     reads=[p_tp], writes=[c.qa])
                if NSEL > 64 and n >= 32:
                    P.op(DVE, lambda e: e.tensor_copy(out=c.qa[64:128, 1], in_=p_tp[64:128, 128:256].unsqueeze(1).to_broadcast([64, 4, 128])),
                         reads=[p_tp], writes=[c.qa])

            def scores(item):
                kind, k, c = item
                n = c.n
                g = c.g
                ps = p_s.get()
                ksl = slice(k * 128, (k + 1) * 128)
                if kind == "c":
                    cc = k
                    off = 128 * n - 2048 * cc
                    need_mask = off < 2176
                    P.op(PE, lambda e: e.matmul(ps[:], lhsT=KCT[g][:, cc * 128:(cc + 1) * 128], rhs=c.q2[0], start=True, stop=(not need_mask)),
                         reads=[KCT[g], c.qa], writes=[ps])
                    if need_mask:
                        P.op(PE, lambda e: e.matmul(ps[:].rearrange("p (h t) -> p h t", h=4), lhsT=ident[:],
                                                    rhs=Gb[:, off:off + 128].unsqueeze(1).to_broadcast([128, 4, 128]), start=False, stop=True),
                             reads=[ident, Gb], writes=[ps])
                    pc = pc_r.get()
                    c.pcs[cc] = pc
                    P.op(ACT, lambda e: e.activation(out=pc[:], in_=ps[:], func=AF.Exp), reads=[ps], writes=[pc])
                    return pc
                if kind == "s":
                    q = c.q2[1] if k >= 32 else c.q2[0]
                    P.op(PE, lambda e: e.matmul(ps[:], lhsT=KS[:, ksl], rhs=q, start=True, stop=(k != n)), reads=[KS, c.qa], writes=[ps])
                    if k == n:
                        P.op(PE, lambda e: e.matmul(ps[:], lhsT=ident[:], rhs=CB4[:], start=False, stop=True), reads=[ident, CB4], writes=[ps])
                else:
                    edge = (k == n) or (k == n - 4)
                    P.op(PE, lambda e: e.matmul(ps[:], lhsT=KW[:, ksl], rhs=c.q2[0], start=True, stop=(not edge)), reads=[KW, c.qa], writes=[ps])
                    if k == n:
                        P.op(PE, lambda e: e.matmul(ps[:], lhsT=ident[:], rhs=CB4[:], start=False, stop=True), reads=[ident, CB4], writes=[ps])
                    elif k == n - 4:
                        P.op(PE, lambda e: e.matmul(ps[:], lhsT=ident[:], rhs=WB4[:], start=False, stop=True), reads=[ident, WB4], writes=[ps])
                pe = pe_r.get()
                P.op(ACT, lambda e: e.activation(out=pe[:], in_=ps[:], func=AF.Exp), reads=[ps], writes=[pe])
                return pe

            def finish(item, pe):
                kind, k, c = item
                n = c.n
                if kind == "c":
                    pv(p_oc, pe, VC[c.g], k, k == 0, k == c.ncc - 1)
                elif kind == "s":
                    pv(p_os, pe, VS, k, k == 0, k == n)
                    if k == n:
                        combine(c, p_os, 1, False, True)
                        P.dma(POOL, lambda e: e.dma_start(out=OB[n * 128:(n + 1) * 128, c.g * 256:(c.g + 1) * 256],
                                                          in_=c.ob[:].rearrange("p h d -> p (h d)")), reads=[c.ob], writes=[OB])
                else:
                    pv(p_ow, pe, VW, k, k == max(0, n - 4), k == n)
                    if k == n:
                        combine(c, p_ow, 2, False, False)

            LOOK = 2
            queue = []

            def push(item):
                queue.append((item, scores(item)))
                while len(queue) > LOOK:
                    it, pe = queue.pop(0)
                    finish(it, pe)

            def drain():
                while queue:
                    it, pe = queue.pop(0)
                    finish(it, pe)

            for g in range(2):
                if g == 1:
                    P.dma(SP, (lambda g: lambda e: e.dma_start(out=KS[0:64, :], in_=KT[4 + g, :, :]))(g), reads=[KT], writes=[KS])
                    P.dma(SP, (lambda g: lambda e: e.dma_start(out=KW[0:64, :], in_=KT[6 + g, :, :]))(g), reads=[KT], writes=[KW])
                    P.dma(SP, (lambda g: lambda e: e.dma_start(out=VS[:, :, 0:64], in_=VT[:, g * 64:(g + 1) * 64].rearrange("(n p) d -> p n d", p=128)))(g),
                          reads=[VT], writes=[VS])
                    P.dma(SP, (lambda g: lambda e: e.dma_start(out=VW[:, :, 0:64], in_=VT[:, 128 + g * 64:128 + (g + 1) * 64].rearrange("(n p) d -> p n d", p=128)))(g),
                          reads=[VT], writes=[VW])
                prev = None
                for n in range(NT + 1):
                    cur = comp_part(g, n) if n < NT else None
                    if cur is not None:
                        for cc in range(cur.ncc):
                            push(("c", cc, cur))
                    sel_items = [("s", k, prev) for k in range(prev.n + 1)] if prev is not None else []
                    half = len(sel_items) // 2
                    for it in sel_items[:half]:
                        push(it)
                    if cur is not None:
                        if any(it[0] == "c" and it[2] is cur for it, _ in queue):
                            drain()
                        select_part(cur)
                    for it in sel_items[half:]:
                        push(it)
                    if cur is not None:
                        for k in range(max(0, n - 4), n + 1):
                            push(("w", k, cur))
                    prev = cur
                drain()
            P.barrier()
            P.emit()

    def phase_D():
        with ExitStack() as ph:
            g01 = load_gain(ph, 0, 1)
            wo = sbt(ph, "D_wo", [128, 8, D], BF16)
            load_w_bf16(wo, e_w_out, 8, "wo")
            m_r = Rot([sbt(ph, f"D_m{i}", [128, D], BF16) for i in range(2)])
            mT_r = Rot([sbt(ph, f"D_mT{i}", [128, 8, 128], BF16) for i in range(2)])
            st_r = Rot([sbt(ph, f"D_st{i}", [128, 8], F32) for i in range(2)])
            hres_r = Rot([sbt(ph, f"D_hr{i}", [128, D], F32) for i in range(2)])
            out_r = Rot([sbt(ph, f"D_out{i}", [128, D], F32) for i in range(2)])
            tp = pst(ph, "D_tp", [128, 1024], BF16)
            pp = Rot([pst(ph, f"D_pp{i}", [128, 512]) for i in range(6)])
            def Da(n):
                m = m_r.get()
                rs = slice(n * 128, (n + 1) * 128)
                P.dma(SP, lambda e: e.dma_start(out=m[:, 0:512], in_=OA[rs, :]), reads=[OA], writes=[m])
                P.dma(SP, lambda e: e.dma_start(out=m[:, 512:1024], in_=OB[rs, :]), reads=[OB], writes=[m])
                mT = mT_r.get()
                transpose_to(tp, m, 8, lambda: mT[:], mT)
                return (n, mT)

            def Db(c):
                n, mT = c
                rs = slice(n * 128, (n + 1) * 128)
                ps2 = [pp.get(), pp.get()]
                for hf in range(2):
                    for kc in range(8):
                        P.op(PE, lambda e: e.matmul(ps2[hf][:], lhsT=mT[:, kc, :], rhs=wo[:, kc, hf * 512:(hf + 1) * 512], start=(kc == 0), stop=(kc == 7)),
                             reads=[mT, wo], writes=[ps2[hf]])
                outt = out_r.get()
                norm_residual_store(ph, ps2, g01, x_d[rs, :], Bx, H0[rs, :], H0, st_r.get(), outt, hres_r.get(), outt)

            cur = Da(0)
            for n in range(NT):
                nxt = Da(n + 1) if n + 1 < NT else None
                Db(cur)
                cur = nxt
            P.barrier()
            P.emit()

    def phase_FFN(layer, Hin, Hout_ap_fn, Hout_tl, tag):
        TS = 256
        NG = S // TS
        TPG = TS // 128
        with ExitStack() as ph:
            gpre = load_gain(ph, layer, 2)
            gpost = load_gain(ph, layer, 3)
            w1 = sbt(ph, tag + "_w1", [128, 8, 4096], BF16)
            w2 = sbt(ph, tag + "_w2", [128, 32, D], BF16)
            load_w_bf16(w1, ffn_w1[layer], 8, "w1")
            load_w_bf16(w2, ffn_w2[layer], 32, "w2")
            xt_r = Rot([sbt(ph, tag + f"_xt{i}", [128, D], F32) for i in range(2)])
            hr_r = Rot([sbt(ph, tag + f"_hr{i}", [128, D], F32) for i in range(1)])
            stA_r = Rot([sbt(ph, tag + f"_stA{i}", [128, 4], F32) for i in range(4)])
            stC_r = Rot([sbt(ph, tag + f"_stC{i}", [128, 8], F32) for i in range(2)])
            xn_r = Rot([sbt(ph, tag + f"_xn{i}", [128, D], BF16) for i in range(2)])
            xnT_r = Rot([sbt(ph, tag + f"_xnT{i}", [128, 8, TS], BF16) for i in range(2)])
            hT_r = Rot([sbt(ph, tag + f"_hT{i}", [128, 32, TS], BF16) for i in range(2)])
            rl_r = Rot([sbt(ph, tag + f"_rl{i}", [128, TS], F32) for i in range(3)])
            out_r = Rot([sbt(ph, tag + f"_out{i}", [128, D], F32) for i in range(2)])
            tp = pst(ph, tag + "_tp", [128, 1024], BF16)
            pp = Rot([pst(ph, tag + f"_pp{i}", [128, 512]) for i in range(7)])

            class Cx:
                pass

            def S1a(sg):
                c = Cx()
                c.sg = sg
                c.xnT = xnT_r.get()
                c.xns = []
                for tl in range(TPG):
                    n = sg * TPG + tl
                    rs = slice(n * 128, (n + 1) * 128)
                    xt = xt_r.get()
                    stt = stA_r.get()
                    xn = xn_r.get()
                    c.xns.append(xn)
                    P.dma(SP, lambda e: e.dma_start(out=xt[:], in_=Hin[rs, :]), reads=[Hin], writes=[xt])
                    r = rms_rstd(stt, 0, xt[:], xt, xn, D)
                    P.op(DVE, lambda e: e.scalar_tensor_tensor(out=xn[:], in0=xt[:], scalar=r, in1=gpre[:], op0=ALU.mult, op1=ALU.mult),
                         reads=[xt, stt, gpre], writes=[xn])
                return c

            def S1b(c):
                for tl in range(TPG):
                    transpose_to(tp, c.xns[tl], 8, lambda: c.xnT[:, :, tl * 128:(tl + 1) * 128], c.xnT)

            def S2(c):
                c.hT = hT_r.get()
                for fc in range(32):
                    ps = pp.get()
                    for kc in range(8):
                        P.op(PE, lambda e: e.matmul(ps[:, 0:TS], lhsT=w1[:, kc, fc * 128:(fc + 1) * 128], rhs=c.xnT[:, kc, :], start=(kc == 0), stop=(kc == 7)),
                             reads=[w1, c.xnT], writes=[ps])
                    rl = rl_r.get()
                    P.op(ACT, lambda e: e.activation(out=rl[:], in_=ps[:, 0:TS], func=AF.Relu), reads=[ps], writes=[rl])
                    P.op(DVE, lambda e: e.tensor_tensor(out=c.hT[:, fc, :], in0=rl[:], in1=rl[:], op=ALU.mult), reads=[rl], writes=[c.hT])

            def S3(c):
                for tl in range(TPG):
                    n = c.sg * TPG + tl
                    rs = slice(n * 128, (n + 1) * 128)
                    ps2 = [pp.get(), pp.get()]
                    for hf in range(2):
                        for fc in range(32):
                            P.op(PE, lambda e: e.matmul(ps2[hf][:], lhsT=c.hT[:, fc, tl * 128:(tl + 1) * 128], rhs=w2[:, fc, hf * 512:(hf + 1) * 512],
                                                        start=(fc == 0), stop=(fc == 31)), reads=[c.hT, w2], writes=[ps2[hf]])
                    outt = out_r.get()
                    norm_residual_store(ph, ps2, gpost, Hin[rs, :], Hin.b, Hout_ap_fn(rs), Hout_tl, stC_r.get(), outt, hr_r.get(), outt)

            cx = {0: S1a(0)}
            S1b(cx[0])
            for i in range(NG + 1):
                if i + 1 < NG:
                    cx[i + 1] = S1a(i + 1)
                if i < NG:
                    S2(cx[i])
                if i + 1 < NG:
                    S1b(cx[i + 1])
                if 0 <= i - 1 < NG:
                    S3(cx.pop(i - 1))
            P.barrier()
            P.emit()

    def phase_E():
        with ExitStack() as ph:
            g10 = load_gain(ph, 1, 0)
            g11 = load_gain(ph, 1, 1)
            wi = sbt(ph, "E_wi", [128, 8, 4096], BF16)
            wo = sbt(ph, "E_wo", [128, 16, D], BF16)
            load_w_bf16(wi, o_w_in, 8, "owi")
            load_w_bf16(wo, o_w_out, 16, "owo")
            U16 = sbt(ph, "E_U16", [128, 128], BF16)
            P.dma(POOL, lambda e: e.dma_start(out=U16[:], in_=cd["c_U"]), writes=[U16])
            wcT = sbt(ph, "E_wcT", [128, 8, 128], BF16)
            bs = sbt(ph, "E_bs", [128, 8], F32)
            P.dma(SP, lambda e: e.dma_start(out=bs[:], in_=o_b_s.rearrange("g t -> t g"), allow_slow_non_contiguous=True), writes=[bs])
            lng = sbt(ph, "E_lng", [128, 2048], F32)
            lnb = sbt(ph, "E_lnb", [128, 2048], F32)
            P.dma(SP, lambda e: e.dma_start(out=lng[:], in_=o_ln_g.partition_broadcast(128)), writes=[lng])
            P.dma(SP, lambda e: e.dma_start(out=lnb[:], in_=o_ln_b.partition_broadcast(128)), writes=[lnb])
            xt_r = Rot([sbt(ph, f"E_xt{i}", [128, D], F32) for i in range(2)])
            hr_r = Rot([sbt(ph, f"E_hr{i}", [128, D], F32) for i in range(1)])
            out_r = Rot([sbt(ph, f"E_out{i}", [128, D], F32) for i in range(1)])
            stA_r = Rot([sbt(ph, f"E_stA{i}", [128, 4], F32) for i in range(2)])
            stB_r = Rot([sbt(ph, f"E_stB{i}", [128, 16], F32) for i in range(3)])
            stC_r = Rot([sbt(ph, f"E_stC{i}", [128, 8], F32) for i in range(2)])
            xn_r = Rot([sbt(ph, f"E_xn{i}", [128, D], BF16) for i in range(2)])
            xnT_r = Rot([sbt(ph, f"E_xnT{i}", [128, 8, 128], BF16) for i in range(2)])
            u_r = Rot([sbt(ph, f"E_u{i}", [128, 2048], F32) for i in range(3)])
            v_r = Rot([sbt(ph, f"E_v{i}", [128, 2048], F32) for i in range(2)])
            vn_r = Rot([sbt(ph, f"E_vn{i}", [128, 2048], BF16) for i in range(2)])
            yb_r = Rot([sbt(ph, f"E_y{i}", [128, 2048], BF16) for i in range(1)])
            yT_r = Rot([sbt(ph, f"E_yT{i}", [128, 16, 128], BF16) for i in range(2)])
            tp = pst(ph, "E_tp", [128, 1024], BF16)
            pp = Rot([pst(ph, f"E_pp{i}", [128, 512]) for i in range(7)])
            ws = yb_r.tiles[0]
            P.dma(POOL, lambda e: e.dma_start(out=ws[:, 0:1024].rearrange("p (g s) -> p g s", g=8), in_=o_w_s.rearrange("g t s -> t g s")), writes=[ws])
            for g in range(8):
                P.op(PE, (lambda g: lambda e: e.transpose(out=tp[:, g * 128:(g + 1) * 128], in_=ws[:, g * 128:(g + 1) * 128], identity=ident[:]))(g),
                     reads=[ws, ident], writes=[tp])
            P.op(DVE, lambda e: e.tensor_tensor(out=wcT[:], in0=tp[:, 0:1024].rearrange("p (g t) -> p g t", g=8),
                                                in1=U16[:].unsqueeze(1).to_broadcast([128, 8, 128]), op=ALU.mult), reads=[tp, U16], writes=[wcT])

            class Cx:
                pass

            def F1a(n):
                c = Cx()
                c.n = n
                c.rs = slice(n * 128, (n + 1) * 128)
                xt = xt_r.get()
                stt = stA_r.get()
                c.xn = xn_r.get()
                c.xnT = xnT_r.get()
                P.dma(SP, lambda e: e.dma_start(out=xt[:], in_=H1[c.rs, :]), reads=[H1], writes=[xt])
                r = rms_rstd(stt, 0, xt[:], xt, c.xn, D)
                P.op(DVE, lambda e: e.scalar_tensor_tensor(out=c.xn[:], in0=xt[:], scalar=r, in1=g10[:], op0=ALU.mult, op1=ALU.mult),
                     reads=[xt, stt, g10], writes=[c.xn])
                return c

            def F1b(c):
                transpose_to(tp, c.xn, 8, lambda: c.xnT[:], c.xnT)

            def F2(c, part):
                if part == 0:
                    c.u = u_r.get()
                    c.v = v_r.get()
                    c.st = stB_r.get()
                    P.op(DVE, lambda e: e.memset(c.st[:], 0.0), writes=[c.st])
                for cg in range(4 * part, 4 * part + 4):
                    ps = pp.get()
                    for kc in range(8):
                        P.op(PE, lambda e: e.matmul(ps[:], lhsT=c.xnT[:, kc, :], rhs=wi[:, kc, cg * 512:(cg + 1) * 512], start=(kc == 0), stop=(kc == 7)),
                             reads=[wi, c.xnT], writes=[ps])
                    if cg < 4:
                        P.op(ACT, lambda e: e.activation(out=c.u[:, cg * 512:(cg + 1) * 512], in_=ps[:], func=AF.Gelu_apprx_tanh), reads=[ps], writes=[c.u])
                    else:
                        c2 = cg - 4
                        P.op(ACT, lambda e: e.activation(out=c.v[:, c2 * 512:(c2 + 1) * 512], in_=ps[:], func=AF.Gelu_apprx_tanh,
                                                         accum_out=c.st[:, 4 + c2:5 + c2]), reads=[ps], writes=[c.v, c.st])

            def L(c):
                stt, v32 = c.st, c.v
                c.vn = vn_r.get()
                vn = c.vn
                P.op(ACT, lambda e: e.activation(out=vn[:], in_=v32[:], func=AF.Square, accum_out=stt[:, 8:9]), reads=[v32], writes=[vn, stt])
                P.op(DVE, lambda e: e.tensor_tensor(out=stt[:, 9:11], in0=stt[:, 4:6], in1=stt[:, 6:8], op=ALU.add), reads=[stt], writes=[stt])
                P.op(DVE, lambda e: e.tensor_tensor(out=stt[:, 11:12], in0=stt[:, 9:10], in1=stt[:, 10:11], op=ALU.add), reads=[stt], writes=[stt])
                P.op(DVE, lambda e: e.tensor_scalar(out=stt[:, 12:13], in0=stt[:, 11:12], scalar1=1.0 / 2048, scalar2=None, op0=ALU.mult), reads=[stt], writes=[stt])
                P.op(DVE, lambda e: e.tensor_tensor(out=stt[:, 13:14], in0=stt[:, 12:13], in1=stt[:, 12:13], op=ALU.mult), reads=[stt], writes=[stt])
                P.op(DVE, lambda e: e.scalar_tensor_tensor(out=stt[:, 14:15], in0=stt[:, 8:9], scalar=1.0 / 2048, in1=stt[:, 13:14],
                                                           op0=ALU.mult, op1=ALU.subtract), reads=[stt], writes=[stt])
                P.op(ACT, lambda e: e.activation(out=stt[:, 15:16], in_=stt[:, 14:15], func=AF.Sqrt, bias=EPS), reads=[stt], writes=[stt])
                P.op(DVE, lambda e: e.reciprocal(out=stt[:, 15:16], in_=stt[:, 15:16]), reads=[stt], writes=[stt])
                P.op(DVE, lambda e: e.scalar_tensor_tensor(out=v32[:], in0=v32[:], scalar=stt[:, 12:13], in1=lng[:], op0=ALU.subtract, op1=ALU.mult),
                     reads=[v32, stt, lng], writes=[v32])
                P.op(DVE, lambda e: e.scalar_tensor_tensor(out=vn[:], in0=v32[:], scalar=stt[:, 15:16], in1=lnb[:], op0=ALU.mult, op1=ALU.add),
                     reads=[v32, stt, lnb], writes=[vn])

            def M1a(c):
                c.yb = yb_r.get()
                c.yT = yT_r.get()
                yb = c.yb
                for g2 in range(4):
                    ps = pp.get()
                    for gg in range(2):
                        g = g2 * 2 + gg
                        P.op(PE, lambda e: e.matmul(ps[:, gg * 256:(gg + 1) * 256], lhsT=wcT[:, g, :], rhs=c.vn[:, g * 256:(g + 1) * 256], start=True, stop=True),
                             reads=[wcT, c.vn], writes=[ps])
                    for gg in range(2):
                        g = g2 * 2 + gg
                        P.op(DVE, lambda e: e.scalar_tensor_tensor(out=yb[:, g * 256:(g + 1) * 256], in0=ps[:, gg * 256:(gg + 1) * 256], scalar=bs[:, g:g + 1],
                                                                   in1=c.u[:, g * 256:(g + 1) * 256], op0=ALU.add, op1=ALU.mult),
                             reads=[ps, bs, c.u], writes=[yb])

            def M1b(c):
                yb = c.yb
                for half in range(2):
                    for cc8 in range(8):
                        cc = half * 8 + cc8
                        P.op(PE, lambda e: e.transpose(out=tp[:, cc8 * 128:(cc8 + 1) * 128], in_=yb[:, cc * 128:(cc + 1) * 128], identity=ident[:]),
                             reads=[yb, ident], writes=[tp])
                    P.op(ACT, lambda e: e.copy(out=c.yT[:, half * 8:(half + 1) * 8, :], in_=tp[:, 0:1024].rearrange("p (c t) -> p c t", c=8)),
                         reads=[tp], writes=[c.yT])

            def M2a(c):
                c.ps2 = [pp.get(), pp.get()]
                for hf in range(2):
                    for cc in range(16):
                        P.op(PE, lambda e: e.matmul(c.ps2[hf][:], lhsT=c.yT[:, cc, :], rhs=wo[:, cc, hf * 512:(hf + 1) * 512], start=(cc == 0), stop=(cc == 15)),
                             reads=[c.yT, wo], writes=[c.ps2[hf]])

            def M2b(c):
                outt = out_r.get()
                norm_residual_store(ph, c.ps2, g11, H1[c.rs, :], H1.b, H2[c.rs, :], H2, stC_r.get(), outt, hr_r.get(), outt)

            cx = {}
            cx[0] = F1a(0)
            F1b(cx[0])
            F2(cx[0], 0)
            F2(cx[0], 1)
            if NT > 1:
                cx[1] = F1a(1)
                F1b(cx[1])
            for i in range(NT + 2):
                if i < NT:
                    L(cx[i])
                if i + 2 < NT:
                    cx[i + 2] = F1a(i + 2)
                if 0 <= i - 2 < NT:
                    M2a(cx[i - 2])
                    M2b(cx[i - 2])
                if 0 <= i - 1 < NT:
                    M1a(cx[i - 1])
                if i + 1 < NT:
                    F2(cx[i + 1], 0)
                if 0 <= i - 1 < NT:
                    M1b(cx[i - 1])
                if i + 1 < NT:
                    F2(cx[i + 1], 1)
                if i + 2 < NT:
                    F1b(cx[i + 2])
                cx.pop(i - 2, None)
            P.barrier()
            P.emit()

    if "A" in phases:
        phase_A()
    if "B" in phases:
        phase_BC()
    if "D" in phases:
        phase_D()
    if "F" in phases:
        phase_FFN(0, H0, lambda rs: H1[rs, :], H1, "F0")
    if "E" in phases:
        phase_E()
    if "G" in phases:
        ytl = Tl(y_d, "y")
        phase_FFN(1, H2, lambda rs: y_d[rs, :], ytl, "F1")
    P.barrier()
    P.emit()
    top.close()
    return nc, P


WEIGHT_NAMES = ["norm_g", "ffn_w1", "ffn_w2", "e_w_in", "e_w_out", "gla_w_gate", "gla_b_gate", "gla_norm", "nsa_gate_b", "nsa_cmp_pos",
                "nsa_cmp_w1", "nsa_cmp_w2", "o_w_in", "o_ln_g", "o_ln_b", "o_w_s", "o_b_s", "o_w_out"]


def prep_weights(inputs):
    m = {}
    for k in WEIGHT_NAMES:
        a = np.asarray(inputs[k], dtype=np.float32)
        if k in ("ffn_w1", "ffn_w2", "norm_g"):
            m[k] = np.ascontiguousarray(a)
        elif k == "gla_norm":
            m[k] = np.ascontiguousarray(a[0].reshape(512))
        else:
            m[k] = np.ascontiguousarray(a[0])
    return m


def kernel(**inputs):
    x = np.asarray(inputs["x"], dtype=np.float32)
    B, S, _ = x.shape
    nc, _ = build(S)
    wm = prep_weights(inputs)
    wm.update(make_consts(S))
    in_maps = [dict(wm, x=np.ascontiguousarray(x[b])) for b in range(B)]
    res = run_bass_kernel_spmd(nc, in_maps, core_ids=list(range(B)))
    return np.stack([r["y"] for r in res.results], axis=0).astype(np.float32)
```
